# Optimizing a Trainium2 kernel written in Bass

```python
import jax
import jax.numpy as jnp
from jax import lax
import numpy as np

D_MODEL = 2048
BATCH = 2
SEQ = 4096
DEPTH = 4

GRID_W = 64
CTX_LEN = 256
N_MIXERS = 3
N_MOD = 6
EPS = 1e-6
NEG_INF = -1e30
D_FF = 5632
FFN_CONV = 3
CHUNK = 128
A_WIDTH = 2 * D_MODEL
A_GROUPS = 16
NA_HEADS = 16
NA_HEAD_DIM = D_MODEL // NA_HEADS
NA_KH_MAX = 8
NA_KW = 16
RNN_WIDTH = D_MODEL
RNN_HEADS = 16
RNN_HEAD_DIM = RNN_WIDTH // RNN_HEADS
RNN_CONV = 4
RG_C = 8.0
N_A_LAYERS = (DEPTH + N_MIXERS - 1) // N_MIXERS
N_B_LAYERS = (DEPTH + N_MIXERS - 2) // N_MIXERS
N_C_LAYERS = (DEPTH + N_MIXERS - 3) // N_MIXERS

kernel_name = 'hybrid_interleaved_diffusion_block'


def rmsnorm(x, g):
    xf = x.astype(jnp.float32)
    y = xf * lax.rsqrt(jnp.mean(xf * xf, axis=-1, keepdims=True) + EPS)
    return (y * g.astype(jnp.float32)).astype(x.dtype)


def modulate(h, shift, scale):
    return h * (1 + scale) + shift


def dwconv(x, w, b, left):
    k, length = w.shape[0], x.shape[1]
    xp = jnp.pad(x, ((0, 0), (left, k - 1 - left), (0, 0)))
    y = b
    for j in range(k):
        y = y + xp[:, j:j + length] * w[j]
    return y


def conv_ffn(h, w_up, conv_w, conv_b, w_down):
    z = dwconv(h @ w_up, conv_w, conv_b, FFN_CONV // 2)
    g, v = jnp.split(z, 2, axis=-1)
    return (jax.nn.silu(g) * v) @ w_down


def chunk_mlp_mix(h, w_in, g_v, w_s, b_s, w_out):
    bsz, length, _ = h.shape
    z = jax.nn.gelu(h @ w_in)
    u, v = jnp.split(z, 2, axis=-1)
    v = rmsnorm(v, g_v).reshape(bsz, length // CHUNK, CHUNK, A_GROUPS, A_WIDTH // A_GROUPS)
    s = jnp.einsum('gpq,bnqgc->bnpgc', w_s, v) + b_s.T[None, None, :, :, None]
    return (u * s.reshape(bsz, length, A_WIDTH)) @ w_out


def na_mix(hc, hl, w_qkv, rpb, w_out, need_ctx):
    bsz, seq, _ = hl.shape
    rows = seq // GRID_W
    kh = min(NA_KH_MAX, rows)
    scale = NA_HEAD_DIM ** -0.5
    qkv = (hl @ w_qkv).reshape(bsz, rows, GRID_W, 3, NA_HEADS, NA_HEAD_DIM)
    q, k, v = qkv[:, :, :, 0], qkv[:, :, :, 1], qkv[:, :, :, 2]
    if need_ctx:
        qkv_c = (hc @ w_qkv).reshape(bsz, hc.shape[1], 3, NA_HEADS, NA_HEAD_DIM)
        q_c, k_c, v_c = qkv_c[:, :, 0], qkv_c[:, :, 1], qkv_c[:, :, 2]
    else:
        kv_c = (hc @ w_qkv[:, D_MODEL:]).reshape(bsz, hc.shape[1], 2, NA_HEADS, NA_HEAD_DIM)
        k_c, v_c = kv_c[:, :, 0], kv_c[:, :, 1]
    r = jnp.arange(rows)
    r0 = jnp.clip(r - kh // 2, 0, rows - kh)
    key_rows = r0[:, None] + jnp.arange(kh)[None, :]
    col = jnp.arange(GRID_W)
    c0 = jnp.clip(col - NA_KW // 2, 0, GRID_W - NA_KW)
    col_ok = (col[None, :] >= c0[:, None]) & (col[None, :] < c0[:, None] + NA_KW)
    k_blk = k[:, key_rows]
    v_blk = v[:, key_rows]
    s_lat = jnp.einsum('brqhd,brikhd->bhrqik', q, k_blk).astype(jnp.float32) * scale
    dr = key_rows - r[:, None]
    dc = jnp.clip(col[None, :] - col[:, None], -(NA_KW - 1), NA_KW - 1)
    bias = rpb[:, dr[:, None, :, None] + NA_KH_MAX - 1, dc[None, :, None, :] + NA_KW - 1]
    s_lat = jnp.where(col_ok[None, None, None, :, None, :], s_lat + bias[None].astype(jnp.float32), NEG_INF)
    s_ctx = jnp.einsum('brqhd,bkhd->bhrqk', q, k_c).astype(jnp.float32) * scale
    n_win = kh * GRID_W
    logits = jnp.concatenate([s_lat.reshape(bsz, NA_HEADS, rows, GRID_W, n_win), s_ctx], axis=-1)
    p = jax.nn.softmax(logits, axis=-1).astype(v.dtype)
    p_lat = p[..., :n_win].reshape(bsz, NA_HEADS, rows, GRID_W, kh, GRID_W)
    o = jnp.einsum('bhrqik,brikhd->brqhd', p_lat, v_blk) + jnp.einsum('bhrqk,bkhd->brqhd', p[..., n_win:], v_c)
    y_lat = o.reshape(bsz, seq, D_MODEL) @ w_out
    y_ctx = None
    if need_ctx:
        sc = jnp.einsum('bqhd,bkhd->bhqk', q_c, k_c).astype(jnp.float32) * scale
        pc = jax.nn.softmax(sc, axis=-1).astype(v_c.dtype)
        y_ctx = jnp.einsum('bhqk,bkhd->bqhd', pc, v_c).reshape(bsz, hc.shape[1], D_MODEL) @ w_out
    return y_ctx, y_lat


def rglru_gates(xr, w_g, b_g, lam):
    bsz, length, _ = xr.shape
    xh = xr.reshape(bsz, length, RNN_HEADS, RNN_HEAD_DIM)
    g = jnp.einsum('blhi,ghij->gblhj', xh, w_g) + b_g[:, None, None]
    g = jax.nn.sigmoid(g.astype(jnp.float32)).reshape(2, bsz, length, RNN_WIDTH)
    log_a = -RG_C * g[0] * jax.nn.softplus(-lam.astype(jnp.float32))
    a = jnp.exp(log_a)
    b = jnp.sqrt(-jnp.expm1(2.0 * log_a)) * (g[1] * xr.astype(jnp.float32))
    return a, b


def linear_scan(a, b, h0, reverse):
    idx = -1 if reverse else 0
    b = b.at[:, idx].add(a[:, idx] * h0)

    def combine(e1, e2):
        a1, b1 = e1
        a2, b2 = e2
        return a1 * a2, a2 * b1 + b2

    _, h = lax.associative_scan(combine, (a, b), reverse=reverse, axis=1)
    return h


def rglru_mix(hc, hl, w_in, conv_w, conv_b, w_gate, b_gate, lam, w_out, need_ctx):
    left = RNN_CONV // 2
    if need_ctx:
        y_c, x_c = jnp.split(hc @ w_in, 2, axis=-1)
    else:
        x_c = hc @ w_in[:, RNN_WIDTH:]
    x_c = dwconv(x_c, conv_w, conv_b, left)
    y_l, x_l = jnp.split(hl @ w_in, 2, axis=-1)
    x_l = dwconv(x_l, conv_w, conv_b, left)
    h0 = jnp.zeros((hc.shape[0], RNN_WIDTH), jnp.float32)
    h_c_dirs, h_l_dirs = [], []
    for d, rev in enumerate((False, True)):
        a_c, b_c = rglru_gates(x_c, w_gate[d], b_gate[d], lam[d])
        h_c = linear_scan(a_c, b_c, h0, rev)
        h_end = h_c[:, 0] if rev else h_c[:, -1]
        a_l, b_l = rglru_gates(x_l, w_gate[d], b_gate[d], lam[d])
        h_l_dirs.append(linear_scan(a_l, b_l, h_end, rev))
        h_c_dirs.append(h_c)
    h_l = (h_l_dirs[0] + h_l_dirs[1]).astype(hl.dtype)
    y_lat = (jax.nn.gelu(y_l) * h_l) @ w_out
    y_ctx = None
    if need_ctx:
        h_cs = (h_c_dirs[0] + h_c_dirs[1]).astype(hc.dtype)
        y_ctx = (jax.nn.gelu(y_c) * h_cs) @ w_out
    return y_ctx, y_lat


def setup_inputs(seed: int = 0) -> dict:
    key = jax.random.key(seed)
    ks = iter(jax.random.split(key, 32))

    def nrm(shape, scale):
        return jax.random.normal(next(ks), shape, jnp.float32) * scale

    D = D_MODEL
    nA, nB, nC = N_A_LAYERS, N_B_LAYERS, N_C_LAYERS
    a0 = jax.random.uniform(next(ks), (nC, 2, RNN_WIDTH), jnp.float32, 0.9, 0.999)
    s = a0 ** (1.0 / RG_C)
    c_lam = jnp.log(s) - jnp.log1p(-s)
    return {
        'x': nrm((BATCH, SEQ, D), 1.0),
        'c': nrm((BATCH, D), 1.0),
        'ctx': nrm((BATCH, CTX_LEN, D), 1.0),
        'c_ctx': nrm((D,), 1.0),
        'ada_w': nrm((DEPTH, D, N_MOD * D), D ** -0.5),
        'ada_b': nrm((DEPTH, N_MOD * D), 0.02),
        'norm_g': 1.0 + nrm((DEPTH, 2, D), 0.05),
        'ffn_w_up': nrm((DEPTH, D, 2 * D_FF), D ** -0.5),
        'ffn_conv_w': nrm((DEPTH, FFN_CONV, 2 * D_FF), FFN_CONV ** -0.5),
        'ffn_conv_b': nrm((DEPTH, 2 * D_FF), 0.02),
        'ffn_w_down': nrm((DEPTH, D_FF, D), D_FF ** -0.5),
        'a_w_in': nrm((nA, D, 2 * A_WIDTH), D ** -0.5),
        'a_g_v': 1.0 + nrm((nA, A_WIDTH), 0.05),
        'a_w_s': nrm((nA, A_GROUPS, CHUNK, CHUNK), CHUNK ** -0.5),
        'a_b_s': nrm((nA, A_GROUPS, CHUNK), 0.1),
        'a_w_out': nrm((nA, A_WIDTH, D), A_WIDTH ** -0.5),
        'b_w_qkv': nrm((nB, D, 3 * D), D ** -0.5),
        'b_rpb': nrm((nB, NA_HEADS, 2 * NA_KH_MAX - 1, 2 * NA_KW - 1), 0.2),
        'b_w_out': nrm((nB, D, D), D ** -0.5),
        'c_w_in': nrm((nC, D, 2 * RNN_WIDTH), D ** -0.5),
        'c_conv_w': nrm((nC, RNN_CONV, RNN_WIDTH), RNN_CONV ** -0.5),
        'c_conv_b': nrm((nC, RNN_WIDTH), 0.02),
        'c_w_gate': nrm((nC, 2, 2, RNN_HEADS, RNN_HEAD_DIM, RNN_HEAD_DIM), RNN_HEAD_DIM ** -0.5),
        'c_b_gate': nrm((nC, 2, 2, RNN_HEADS, RNN_HEAD_DIM), 0.1),
        'c_lam': c_lam,
        'c_w_out': nrm((nC, RNN_WIDTH, D), RNN_WIDTH ** -0.5),
        'final_g': 1.0 + nrm((D,), 0.05),
    }


def reference(x, c, ctx, c_ctx, ada_w, ada_b, norm_g, ffn_w_up, ffn_conv_w, ffn_conv_b, ffn_w_down,
              a_w_in, a_g_v, a_w_s, a_b_s, a_w_out, b_w_qkv, b_rpb, b_w_out,
              c_w_in, c_conv_w, c_conv_b, c_w_gate, c_b_gate, c_lam, c_w_out, final_g):
    bsz = x.shape[0]
    silu_c = jax.nn.silu(c)
    silu_cc = jax.nn.silu(c_ctx)
    h_lat, h_ctx = x, ctx
    for i in range(DEPTH):
        last = i == DEPTH - 1
        kind, j = i % N_MIXERS, i // N_MIXERS
        ctx_in = (not last) or kind != 0
        m_l = (silu_c @ ada_w[i] + ada_b[i]).reshape(bsz, 1, N_MOD, D_MODEL)
        n_l = modulate(rmsnorm(h_lat, norm_g[i, 0]), m_l[:, :, 0], m_l[:, :, 1])
        if ctx_in:
            m_c = (silu_cc @ ada_w[i] + ada_b[i]).reshape(N_MOD, D_MODEL)
            n_c = modulate(rmsnorm(h_ctx, norm_g[i, 0]), m_c[0], m_c[1])
        if kind == 0:
            y_l = chunk_mlp_mix(n_l, a_w_in[j], a_g_v[j], a_w_s[j], a_b_s[j], a_w_out[j])
            y_c = None if last else chunk_mlp_mix(n_c, a_w_in[j], a_g_v[j], a_w_s[j], a_b_s[j], a_w_out[j])
        elif kind == 1:
            y_c, y_l = na_mix(n_c, n_l, b_w_qkv[j], b_rpb[j], b_w_out[j], not last)
        else:
            y_c, y_l = rglru_mix(n_c, n_l, c_w_in[j], c_conv_w[j], c_conv_b[j], c_w_gate[j], c_b_gate[j],
                                 c_lam[j], c_w_out[j], not last)
        h_lat = h_lat + m_l[:, :, 2] * y_l
        n_l = modulate(rmsnorm(h_lat, norm_g[i, 1]), m_l[:, :, 3], m_l[:, :, 4])
        h_lat = h_lat + m_l[:, :, 5] * conv_ffn(n_l, ffn_w_up[i], ffn_conv_w[i], ffn_conv_b[i], ffn_w_down[i])
        if not last:
            h_ctx = h_ctx + m_c[2] * y_c
            n_c = modulate(rmsnorm(h_ctx, norm_g[i, 1]), m_c[3], m_c[4])
            h_ctx = h_ctx + m_c[5] * conv_ffn(n_c, ffn_w_up[i], ffn_conv_w[i], ffn_conv_b[i], ffn_w_down[i])
    return rmsnorm(h_lat, final_g)
```

```python
import numpy as np
import concourse.bass as bass
import concourse.mybir as mybir
from concourse.bass_utils import run_bass_kernel_spmd

F32 = mybir.dt.float32
BF16 = mybir.dt.bfloat16
AF = mybir.ActivationFunctionType
ALU = mybir.AluOpType
AX = mybir.AxisListType

D = 2048
KC = 16
SEQ = 4096
CTX = 256
T = SEQ + CTX
DEPTH = 4
DFF = 5632
NJ = DFF // 128
EPS = 1e-6
TILES = [(i * 512, 512) for i in range(8)] + [(4096, 256)]
GROUPS = [[0, 1], [2, 3], [4, 5], [6, 7], [8]]
SB_BASE = 20480
SB_END = 229376


class Buf:
    __slots__ = ("name", "w", "r")

    def __init__(self, name=""):
        self.name = name
        self.w = None
        self.r = {}


class Sem:
    def __init__(self, handle, name):
        self.h = handle
        self.name = name


class Eng:
    def __init__(self, name, sem):
        self.name = name
        self.sem = sem
        self.cnt = 0
        self.seen = {}
        self.prog = []


class Chan:
    def __init__(self, sem):
        self.sem = sem
        self.n = 0


class Layout:
    def __init__(self, Hd, tiles, segs):
        self.Hd = Hd
        self.tiles = tiles
        self.T = tiles[-1][0] + tiles[-1][1]
        self.ntiles = len(tiles)
        self.groups = [list(range(i, min(i + 2, len(tiles)))) for i in range(0, len(tiles), 2)]
        self.segs = segs
        self.nt = []
        for (t0, w, col) in tiles:
            o = 0
            while o < w:
                ww = min(256, w - o)
                self.nt.append((t0 + o, ww, col))
                o += ww
        self.zoff = []
        zb = 1
        segbase = []
        for (s0, n) in segs:
            segbase.append((s0, n, zb))
            zb += n + 2
        self.zw = (zb + 7) // 8 * 8
        for (t0, w, col) in tiles:
            for (s0, n, b) in segbase:
                if s0 <= t0 < s0 + n:
                    self.zoff.append(b + t0 - s0)
        self.HdB = {}

    def hb(self, c, t):
        if (c, t) not in self.HdB:
            self.HdB[(c, t)] = Buf(f"Hd{c}_{t}")
        return self.HdB[(c, t)]


class K:
    def __init__(self, nc):
        self.nc = nc
        self.sems = []
        self.engs = {}
        self.chans = []
        self.sb_ptr = SB_BASE
        self.uid = 0

    def new_sem(self, name):
        s = Sem(self.nc.alloc_semaphore(name) if hasattr(self.nc, "alloc_semaphore") else None, name)
        self.sems.append(s)
        return s

    def sb(self, shape, dt, name=None):
        self.uid += 1
        nbytes = int(np.prod(shape[1:])) * (4 if dt == F32 else 2)
        off = (self.sb_ptr + 31) // 32 * 32
        assert off + nbytes <= SB_END, ("sbuf overflow", name, off, nbytes)
        self.sb_ptr = off + nbytes
        return self.nc.alloc_sbuf_tensor_at(f"{name or 't'}{self.uid}", list(shape), dt, offset=off)

    def sb_mark(self):
        return self.sb_ptr

    def sb_reset(self, mark):
        self.sb_ptr = mark

    def _need(self, eng, evs):
        waits = []
        for sem, val in evs:
            if val <= 0:
                continue
            if eng.seen.get(sem, 0) < val:
                eng.seen[sem] = val
                waits.append((sem, val))
        return waits

    def _deps(self, reads, writes):
        evs = []
        for b in reads:
            if b.w is not None:
                evs.append(b.w)
        for b in writes:
            if b.w is not None:
                evs.append(b.w)
            for s, v in b.r.items():
                evs.append((s, v))
        best = {}
        for s, v in evs:
            if best.get(s, 0) < v:
                best[s] = v
        return list(best.items())

    def _mark(self, ev, reads, writes):
        s, v = ev
        for b in reads:
            if b.r.get(s, 0) < v:
                b.r[s] = v
        for b in writes:
            b.w = ev
            b.r = {}

    def op(self, eng, fns, reads=(), writes=()):
        if not isinstance(fns, (list, tuple)):
            fns = [fns]
        waits = self._need(eng, self._deps(reads, writes))
        eng.cnt += 1
        ev = (eng.sem, eng.cnt)
        self._mark(ev, reads, writes)
        eng.prog.append(("op", waits, list(fns), eng.sem))
        return ev

    def dma(self, eng, chan, out, in_, reads=(), writes=(), **kw):
        evs = self._deps(reads, writes)
        evs.append((chan.sem, 16 * chan.n))
        waits = self._need(eng, evs)
        chan.n += 1
        ev = (chan.sem, 16 * chan.n)
        self._mark(ev, reads, writes)
        eng.prog.append(("dma", waits, (out, in_, kw), chan.sem))
        return ev

    def barrier(self):
        evs = [(e.sem, e.cnt) for e in self.engs.values()] + [(c.sem, 16 * c.n) for c in self.chans]
        for e in self.engs.values():
            waits = self._need(e, [x for x in evs if x[0] is not e.sem or True])
            if waits:
                e.prog.append(("wait", waits, None, None))

    def chan(self, name):
        c = Chan(self.new_sem(name))
        self.chans.append(c)
        return c


def build_program(stop_after=None, n_layers=DEPTH, force_q=None):
    nc = bass.Bass("TRN2", target_bir_lowering=False)
    k = K(nc)

    def din(name, shape):
        return nc.dram_tensor(name, list(shape), F32, kind="ExternalInput").ap()

    xT = din("xT", [D, T])
    c_fm = din("c_fm", [128, KC, 2])
    ada_wb = din("ada_wb", [DEPTH, 24, 128, KC, 512])
    ada_b_fm = din("ada_b_fm", [128, DEPTH, 96])
    norm_g_fm = din("norm_g_fm", [128, DEPTH, 2, KC])
    final_g_fm = din("final_g_fm", [128, KC])
    ffn_up_b = din("ffn_up_b", [DEPTH, 2 * NJ, 128, KC, 128])
    ffn_cw_fm = din("ffn_cw_fm", [128, DEPTH, 3, 2 * NJ])
    ffn_cb_fm = din("ffn_cb_fm", [128, DEPTH, 2 * NJ])
    ffn_dn_b = din("ffn_dn_b", [DEPTH, KC, 128, NJ, 128])
    a_in_b = din("a_in_b", [2, 64, 128, KC, 128])
    a_gv_fm = din("a_gv_fm", [128, 2, 32])
    a_wsT = din("a_wsT", [2, 128, 16, 128])
    a_bs_bc = din("a_bs_bc", [2, 128, 32, 128])
    a_gv_bc = din("a_gv_bc", [2, 128, 4096])
    a_out_b = din("a_out_b", [2, KC, 128, 32, 128])
    b_qkv_b = din("b_qkv_b", [48, 128, KC, 128])
    b_bias = din("b_bias", [16, 128, 6, 1024])
    b_out_b = din("b_out_b", [KC, 128, KC, 128])
    c_in_b = din("c_in_b", [32, 128, KC, 128])
    c_cw_fm = din("c_cw_fm", [128, 4, KC])
    c_cb_fm = din("c_cb_fm", [128, KC])
    c_wg = din("c_wg", [128, 2, 2, KC, 128])
    c_bg_fm = din("c_bg_fm", [128, 2, 2, KC])
    c_lam_fm = din("c_lam_fm", [128, 2, KC])
    c_out_b = din("c_out_b", [KC, 128, KC, 128])
    ident_in = din("ident_in", [128, 128])
    emask_in = din("emask_in", [128, 2])
    outT = nc.dram_tensor("outT", [D, 1024], F32, kind="ExternalOutput").ap()

    Hd_full = nc.dram_tensor("Hd", [D, T + 1], F32).ap()
    Hd = Hd_full[:, 1:T + 1]
    S1 = nc.dram_tensor("S1", [DFF, T], BF16).ap()
    S2 = nc.dram_tensor("S2", [4096, T], BF16).ap()
    S3 = nc.dram_tensor("S3", [D, T], BF16).ap()
    S4 = nc.dram_tensor("S4", [D, T], BF16).ap()
    X4_full = nc.dram_tensor("X4", [D, T + 1], F32).ap()
    X4 = X4_full[:, 1:T + 1]
    Hw = nc.dram_tensor("Hw", [D, 1282], F32).ap()
    HSw = nc.dram_tensor("HSw", [D, 1282], F32).ap()
    OutW = nc.dram_tensor("OutW", [D, 1280], F32).ap()

    sem_handles = {}

    def mk_sem(name):
        cm = nc.semaphore(name)
        h = cm.__enter__()
        sem_handles[name] = (cm, h)
        return Sem(h, name)

    k.new_sem = mk_sem
    PE = Eng("pe", mk_sem("s_pe"))
    ACT = Eng("act", mk_sem("s_act"))
    DVE = Eng("dve", mk_sem("s_dve"))
    POOL = Eng("pool", mk_sem("s_pool"))
    SP = Eng("sp", mk_sem("s_sp"))
    k.engs = {"pe": PE, "act": ACT, "dve": DVE, "pool": POOL, "sp": SP}

    psum = nc.alloc_psum_tensor("psum", [128, 8, 512], F32)
    PSB = [Buf(f"ps{i}") for i in range(8)]

    ones_bf = k.sb([128, 128], BF16, "ones")
    ident_bf = k.sb([128, 128], BF16, "ident")
    ident_f = k.sb([128, 128], F32, "identf")
    eps_t = k.sb([128, 1], F32, "eps")
    one_t = k.sb([128, 1], F32, "one")
    Mall = k.sb([128, DEPTH, 96, 2], F32, "Mall")
    ng = k.sb([128, DEPTH, 2, KC], F32, "ng")
    fg = k.sb([128, KC], F32, "fg")
    emask = k.sb([128, 2], F32, "emask")
    GS = k.sb([128, 2, 2, KC], F32, "GS")
    B_const = Buf("const")
    B_M = Buf("M")
    B_GS = Buf("GS")
    ch_misc = k.chan("c_misc")

    def wait_all_fn(waits):
        pass

    k.op(POOL, lambda e: e.memset(ones_bf[:], 1.0), writes=[B_const])
    k.dma(SP, ch_misc, ident_f[:], ident_in[:, :], writes=[B_const])
    k.op(POOL, lambda e: e.tensor_copy(out=ident_bf[:], in_=ident_f[:]), reads=[B_const], writes=[B_const])
    k.op(POOL, lambda e: e.memset(eps_t[:], EPS), writes=[B_const])
    k.op(POOL, lambda e: e.memset(one_t[:], 1.0), writes=[B_const])
    k.dma(SP, ch_misc, ng[:], norm_g_fm[:, :, :, :], writes=[B_const])
    k.dma(SP, ch_misc, fg[:], final_g_fm[:, :], writes=[B_const])
    k.dma(SP, ch_misc, emask[:], emask_in[:, :], writes=[B_const])

    persist_mark = k.sb_mark()

    chan_pool = {}

    def get_chan(name):
        if name not in chan_pool:
            chan_pool[name] = k.chan(name)
        return chan_pool[name]

    class WStream:
        def __init__(self, name, shape, nslots):
            self.t = [k.sb(shape, BF16, name) for _ in range(nslots)]
            self.b = [Buf(f"{name}{i}") for i in range(nslots)]
            self.c = [get_chan(f"c_{name}{i}") for i in range(nslots)]
            self.i = 0

        def load(self, src):
            s = self.i % len(self.t)
            self.i += 1
            k.dma(POOL, self.c[s], self.t[s][:], src, writes=[self.b[s]], max_dma_last_dim=4096)
            return self.t[s], self.b[s]

    class Slots:
        def __init__(self, name, shape, dt, n, chan=True):
            self.t = [k.sb(shape, dt, name) for _ in range(n)]
            self.b = [Buf(f"{name}{i}") for i in range(n)]
            self.c = [get_chan(f"c_{name}{i}") for i in range(n)] if chan else None
            self.i = 0

        def next(self):
            s = self.i % len(self.t)
            self.i += 1
            return self.t[s], self.b[s], (self.c[s] if self.c else None)

    ps_rr = [0]

    def next_ps(lo=0, hi=8):
        i = lo + ps_rr[0] % (hi - lo)
        ps_rr[0] += 1
        return i

    def mm_group(out_ap, pairs, reads, writes):
        n = len(pairs)
        fns = []
        for i, (l, r) in enumerate(pairs):
            fns.append(lambda e, l=l, r=r, i=i: e.matmul(out_ap, lhsT=l, rhs=r, start=(i == 0), stop=(i == n - 1)))
        return k.op(PE, fns, reads=reads, writes=writes)

    ch_cp = [k.chan(f"c_cp{i}") for i in range(4)]
    Lm = Layout(Hd, [(i * 512, 512, 0) for i in range(8)] + [(4096, 256, 1)], [(0, SEQ), (SEQ, CTX)])
    LW2 = Layout(Hw[:, 0:1282], [(0, 512, 0), (512, 512, 0), (1024, 258, 0)], [(0, 1282)])
    LW3 = Layout(Hw[:, 1:1281], [(0, 512, 0), (512, 512, 0), (1024, 256, 0)], [(0, 1280)])

    for t, (t0, w, col) in enumerate(Lm.tiles):
        k.dma(SP, ch_cp[t % 4], Hd[:, t0:t0 + w], xT[:, t0:t0 + w])
    zpad = k.sb([128, KC], F32, "zpad")
    Bz = Buf("zpad")
    k.op(POOL, lambda e: e.memset(zpad[:], 0.0), writes=[Bz])
    for dstf in (Hd_full, X4_full):
        k.dma(SP, ch_misc, dstf[:, 0:1].rearrange("(c p) o -> p (c o)", p=128), zpad[:], reads=[Bz],
              allow_slow_non_contiguous=True)
    k.barrier()

    def stage_ada():
        mark = k.sb_mark()
        cf = k.sb([128, KC, 2], F32, "cf")
        sc = k.sb([128, KC, 2], BF16, "sc")
        sg = k.sb([128, KC, 2], F32, "sg")
        adab = k.sb([128, DEPTH, 96], F32, "adab")
        Bc = Buf("cf")
        k.dma(SP, ch_misc, cf[:], c_fm[:, :, :], writes=[Bc])
        k.dma(SP, ch_misc, adab[:], ada_b_fm[:, :, :], writes=[Bc])
        k.op(ACT, lambda e: e.activation(out=sg[:], in_=cf[:], func=AF.Sigmoid), reads=[Bc], writes=[Bc])
        k.op(DVE, lambda e: e.tensor_tensor(out=sc[:], in0=cf[:], in1=sg[:], op=ALU.mult), reads=[Bc], writes=[Bc])
        ws = WStream("adaw", [128, KC, 512], 3)
        for li in range(n_layers):
            for j0 in range(0, 96, 16):
                pb = next_ps()
                for j4 in range(j0, j0 + 16, 4):
                    wt, wb = ws.load(ada_wb[li, j4 // 4])
                    for jj in range(4):
                        j = j4 + jj
                        mm_group(psum[:, pb, (j - j0) * 2:(j - j0) * 2 + 2],
                                 [(wt[:, kk, jj * 128:(jj + 1) * 128], sc[:, kk, :]) for kk in range(KC)],
                                 reads=[wb, Bc], writes=[PSB[pb]])
                k.op(DVE, lambda e, pb=pb, li=li, j0=j0: e.tensor_tensor(
                    out=Mall[:, li, j0:j0 + 16, :],
                    in0=psum[:, pb, 0:32].rearrange("p (j two) -> p j two", two=2),
                    in1=adab[:, li, j0:j0 + 16].unsqueeze(2).to_broadcast([128, 16, 2]),
                    op=ALU.add), reads=[PSB[pb], Bc], writes=[B_M])
        k.barrier()
        k.sb_reset(mark)

    def prep_gs(li, which):
        sh, scl = (0, 1) if which == 0 else (3, 4)
        for col in range(2):
            k.op(DVE, lambda e, col=col: e.scalar_tensor_tensor(
                out=GS[:, 0, col, :], in0=Mall[:, li, scl * 16:(scl + 1) * 16, col], scalar=1.0,
                in1=ng[:, li, which, :], op0=ALU.add, op1=ALU.mult), reads=[B_M, B_const], writes=[B_GS])
            k.op(DVE, lambda e, col=col: e.tensor_copy(out=GS[:, 1, col, :], in_=Mall[:, li, sh * 16:(sh + 1) * 16, col]),
                 reads=[B_M], writes=[B_GS])

    def stage_norm(L, Narena, NB, li, which, final_dst=None, edge_mask=False):
        mark = k.sb_mark()
        final = final_dst is not None
        if not final:
            prep_gs(li, which)
        Ht = Slots("Ht", [128, KC, 256], F32, 3)
        sq = Slots("sq", [128, 4, 256], BF16, 2, chan=False)
        std = Slots("std", [128, 256], F32, 2, chan=False)
        ost = Slots("nout", [128, KC, 256], F32, 1) if final else None
        for ti, (t0, w, col) in enumerate(L.nt):
            ht, hbuf, hch = Ht.next()
            k.dma(SP, hch, ht[:, :, 0:w], L.Hd[:, t0:t0 + w].rearrange("(c p) t -> p c t", p=128), writes=[hbuf])
            pb = next_ps()
            for c4 in range(4):
                st, sbuf_, _ = sq.next()
                k.op(ACT, lambda e, st=st, ht=ht, c4=c4, w=w: e.activation(out=st[:, :, 0:w], in_=ht[:, c4 * 4:(c4 + 1) * 4, 0:w], func=AF.Square),
                     reads=[hbuf], writes=[sbuf_])
                fns = []
                for cc in range(4):
                    first = (c4 == 0 and cc == 0)
                    last = (c4 == 3 and cc == 3)
                    fns.append(lambda e, st=st, cc=cc, first=first, last=last, pb=pb, w=w: e.matmul(
                        psum[:, pb, 0:w], lhsT=ones_bf[:], rhs=st[:, cc, 0:w], start=first, stop=last))
                k.op(PE, fns, reads=[sbuf_, B_const], writes=[PSB[pb]])
            sd, sdb, _ = std.next()
            k.op(ACT, lambda e, sd=sd, pb=pb, w=w: e.activation(out=sd[:, 0:w], in_=psum[:, pb, 0:w], func=AF.Sqrt,
                                                               scale=1.0 / D, bias=eps_t[:]),
                 reads=[PSB[pb], B_const], writes=[sdb])
            k.op(DVE, lambda e, sd=sd, w=w: e.reciprocal(out=sd[:, 0:w], in_=sd[:, 0:w]), reads=[sdb], writes=[sdb])
            if final:
                ot, ob, och = ost.next()
            k.op(DVE, lambda e, ht=ht, sd=sd, w=w: e.tensor_tensor(
                out=ht[:, :, 0:w], in0=ht[:, :, 0:w], in1=sd[:, 0:w].unsqueeze(1).to_broadcast([128, KC, w]), op=ALU.mult),
                 reads=[hbuf, sdb], writes=[hbuf])
            for c in range(KC):
                if final:
                    k.op(ACT, lambda e, ht=ht, c=c, ot=ot, w=w: e.activation(out=ot[:, c, 0:w], in_=ht[:, c, 0:w], func=AF.Copy, scale=fg[:, c:c + 1]),
                         reads=[hbuf, B_const], writes=[ob])
                else:
                    k.op(ACT, lambda e, ht=ht, c=c, col=col, t0=t0, w=w: e.activation(
                        out=Narena[:, c, t0:t0 + w], in_=ht[:, c, 0:w], func=AF.Identity,
                        scale=GS[:, 0, col, c:c + 1], bias=GS[:, 1, col, c:c + 1]),
                         reads=[hbuf, B_GS], writes=[NB[t0 // 512]])
            if final:
                k.dma(SP, och, final_dst[:, t0:t0 + w].rearrange("(c p) t -> p c t", p=128), ot[:, :, 0:w], reads=[ob])
        if edge_mask:
            for cidx, mi in ((0, 0), (L.T - 1, 1)):
                k.op(DVE, lambda e, cidx=cidx, mi=mi: e.tensor_scalar(
                    out=Narena[:, :, cidx], in0=Narena[:, :, cidx], scalar1=emask[:, mi:mi + 1], scalar2=None, op0=ALU.mult),
                     reads=[NB[cidx // 512], B_const], writes=[NB[cidx // 512]])
        k.barrier()
        k.sb_reset(mark)

    def proj1(L, Narena, NB, ws, wsrcs, evac, ps_lo=0, ps_hi=6):
        nblk = len(wsrcs)
        pre = max(1, len(ws.t) - 1)
        loaded = {}
        for bi in range(min(pre, nblk)):
            loaded[bi] = ws.load(wsrcs[bi])
        for bi in range(nblk):
            if bi + pre < nblk:
                loaded[bi + pre] = ws.load(wsrcs[bi + pre])
            wt, wb = loaded.pop(bi)
            for t in range(L.ntiles):
                t0, w, col = L.tiles[t]
                pb = next_ps(ps_lo, ps_hi)
                mm_group(psum[:, pb, 0:w], [(wt[:, kk, :], Narena[:, kk, t0:t0 + w]) for kk in range(KC)],
                         reads=[wb, NB[t]], writes=[PSB[pb]])
                evac(bi, t, pb)

    def hupdate(L, Hc, stc, sti, c, t, pb, li, gate_which):
        t0, w, col = L.tiles[t]
        ht, hbuf, hch = Hc.next()
        k.dma(SP, hch, ht[:, 0:w], L.Hd[c * 128:(c + 1) * 128, t0:t0 + w], reads=[L.hb(c, t)], writes=[hbuf])
        k.op(DVE, lambda e: e.scalar_tensor_tensor(
            out=ht[:, 0:w], in0=psum[:, pb, 0:w], scalar=Mall[:, li, gate_which * 16 + c, col:col + 1],
            in1=ht[:, 0:w], op0=ALU.mult, op1=ALU.add), reads=[PSB[pb], hbuf, B_M], writes=[hbuf])
        k.dma(SP, stc[sti % 4], L.Hd[c * 128:(c + 1) * 128, t0:t0 + w], ht[:, 0:w], reads=[hbuf], writes=[L.hb(c, t)])

    def proj2(L, src, KB, wsrc_c, li, gate_which, resident=False):
        mark = k.sb_mark()
        Ag = [k.sb([128, KB, 512], BF16, "Ag") for _ in range(2)]
        AgB = [Buf("Ag0"), Buf("Ag1")]
        AgC = [get_chan("c_Ag0"), get_chan("c_Ag1")]
        Hc = Slots("Hc", [128, 512], F32, 4)
        stc = [get_chan(f"c_hst{i}") for i in range(4)]
        if resident:
            wres = k.sb([128, KC, KB, 128], BF16, "wres")
            wresB = Buf("wres")
            chw = get_chan("c_wres")
            for c in range(KC):
                k.dma(POOL, chw, wres[:, c], wsrc_c(c), writes=[wresB], max_dma_last_dim=4096)
        else:
            ws = WStream("w2_", [128, KB, 128], 6)
        sti = 0
        for g in L.groups:
            for ti, t in enumerate(g):
                t0, w, col = L.tiles[t]
                k.dma(SP, AgC[ti], Ag[ti][:, :, 0:w], src[:, t0:t0 + w].rearrange("(j p) t -> p j t", p=128),
                      writes=[AgB[ti]])
            for c in range(KC):
                if resident:
                    wt, wb = wres[:, c], wresB
                else:
                    wt, wb = ws.load(wsrc_c(c))
                for ti, t in enumerate(g):
                    t0, w, col = L.tiles[t]
                    pb = next_ps()
                    mm_group(psum[:, pb, 0:w], [(wt[:, j, :], Ag[ti][:, j, 0:w]) for j in range(KB)],
                             reads=[wb, AgB[ti]], writes=[PSB[pb]])
                    hupdate(L, Hc, stc, sti, c, t, pb, li, gate_which)
                    sti += 1
        k.barrier()
        k.sb_reset(mark)

    def stage_ffn(L, Narena, NB, li):
        mark = k.sb_mark()
        ws = WStream("wup", [128, KC, 128], 3)
        ZW = L.zw
        zg = [k.sb([128, ZW], BF16, "zg") for _ in range(2)]
        zv = [k.sb([128, ZW], BF16, "zv") for _ in range(2)]
        zgB = [Buf("zg0"), Buf("zg1")]
        zvB = [Buf("zv0"), Buf("zv1")]
        cw = k.sb([128, 3, 2 * NJ], F32, "cw")
        cb = k.sb([128, 2 * NJ], F32, "cb")
        Bcw = Buf("cw")
        k.dma(SP, ch_misc, cw[:], ffn_cw_fm[:, li, :, :], writes=[Bcw])
        k.dma(SP, ch_misc, cb[:], ffn_cb_fm[:, li, :], writes=[Bcw])
        for s in range(2):
            k.op(POOL, lambda e, s=s: e.memset(zg[s][:], 0.0), writes=[zgB[s]])
            k.op(POOL, lambda e, s=s: e.memset(zv[s][:], 0.0), writes=[zvB[s]])
        yg = Slots("yg", [128, 512], F32, 2, chan=False)
        yv = Slots("yv", [128, 512], F32, 2, chan=False)
        sgs = Slots("sgs", [128, 512], F32, 2, chan=False)
        ao = Slots("ao", [128, 512], BF16, 3)

        for j in range(NJ):
            s = j % 2
            for half, (zt, zb) in enumerate(((zg[s], zgB[s]), (zv[s], zvB[s]))):
                blk = j + half * NJ
                wt, wb = ws.load(ffn_up_b[li, blk])
                for t in range(L.ntiles):
                    t0, w, col = L.tiles[t]
                    pb = next_ps()
                    mm_group(psum[:, pb, 0:w], [(wt[:, kk, :], Narena[:, kk, t0:t0 + w]) for kk in range(KC)],
                             reads=[wb, NB[t]], writes=[PSB[pb]])
                    zo = L.zoff[t]
                    k.op(ACT, lambda e, zt=zt, pb=pb, zo=zo, w=w: e.activation(out=zt[:, zo:zo + w], in_=psum[:, pb, 0:w], func=AF.Copy),
                         reads=[PSB[pb]], writes=[zb])
            for t in range(L.ntiles):
                t0, w, col = L.tiles[t]
                zo = L.zoff[t]
                res = []
                for half, (zt, zb, ys) in enumerate(((zg[s], zgB[s], yg), (zv[s], zvB[s], yv))):
                    blk = j + half * NJ
                    yt, yb, _ = ys.next()
                    k.op(ACT, lambda e, yt=yt, zt=zt, zo=zo, w=w, blk=blk: e.activation(
                        out=yt[:, 0:w], in_=zt[:, zo:zo + w], func=AF.Identity,
                        scale=cw[:, 1, blk:blk + 1], bias=cb[:, blk:blk + 1]), reads=[zb, Bcw], writes=[yb])
                    k.op(DVE, lambda e, yt=yt, zt=zt, zo=zo, w=w, blk=blk: e.scalar_tensor_tensor(
                        out=yt[:, 0:w], in0=zt[:, zo - 1:zo - 1 + w], scalar=cw[:, 0, blk:blk + 1], in1=yt[:, 0:w],
                        op0=ALU.mult, op1=ALU.add), reads=[zb, Bcw, yb], writes=[yb])
                    k.op(DVE, lambda e, yt=yt, zt=zt, zo=zo, w=w, blk=blk: e.scalar_tensor_tensor(
                        out=yt[:, 0:w], in0=zt[:, zo + 1:zo + 1 + w], scalar=cw[:, 2, blk:blk + 1], in1=yt[:, 0:w],
                        op0=ALU.mult, op1=ALU.add), reads=[zb, Bcw, yb], writes=[yb])
                    res.append((yt, yb))
                (ygt, ygb), (yvt, yvb) = res
                st, sb_, _ = sgs.next()
                k.op(ACT, lambda e, st=st, ygt=ygt, w=w: e.activation(out=st[:, 0:w], in_=ygt[:, 0:w], func=AF.Silu),
                     reads=[ygb], writes=[sb_])
                at, ab, ach = ao.next()
                k.op(DVE, lambda e, at=at, st=st, yvt=yvt, w=w: e.tensor_tensor(out=at[:, 0:w], in0=st[:, 0:w], in1=yvt[:, 0:w], op=ALU.mult),
                     reads=[sb_, yvb], writes=[ab])
                k.dma(SP, ach, S1[j * 128:(j + 1) * 128, t0:t0 + w], at[:, 0:w], reads=[ab])
        k.barrier()
        k.sb_reset(mark)

    def stage_mixA(L, Narena, NB, li, ja, arena_mark):
        nchunk = L.T // 128
        mark = k.sb_mark()
        ws = WStream("wain", [128, KC, 128], 4)
        acc = k.sb([128, L.T], F32, "acc")
        accB = [Buf(f"acc{t}") for t in range(L.ntiles)]
        uo = Slots("uo", [128, 512], BF16, 4)
        sqv = Slots("sqv", [128, 512], F32, 6, chan=False)
        for t in range(L.ntiles):
            t0, w, col = L.tiles[t]
            k.op(POOL, lambda e, t0=t0, w=w: e.memset(acc[:, t0:t0 + w], 0.0), writes=[accB[t]])

        def evac(bi, t, pb):
            t0, w, col = L.tiles[t]
            ot, ob, och = uo.next()
            k.op(ACT, lambda e: e.activation(out=ot[:, 0:w], in_=psum[:, pb, 0:w], func=AF.Gelu_apprx_tanh),
                 reads=[PSB[pb]], writes=[ob])
            if bi < 32:
                k.dma(SP, och, S1[bi * 128:(bi + 1) * 128, t0:t0 + w], ot[:, 0:w], reads=[ob])
            else:
                vb = bi - 32
                k.dma(SP, och, S2[vb * 128:(vb + 1) * 128, t0:t0 + w], ot[:, 0:w], reads=[ob])
                st, sb_, _ = sqv.next()
                k.op(ACT, lambda e: e.activation(out=st[:, 0:w], in_=ot[:, 0:w], func=AF.Square), reads=[ob], writes=[sb_])
                k.op(POOL, lambda e: e.tensor_tensor(out=acc[:, t0:t0 + w], in0=acc[:, t0:t0 + w], in1=st[:, 0:w], op=ALU.add),
                     reads=[sb_, accB[t]], writes=[accB[t]])

        proj1(L, Narena, NB, ws, [a_in_b[ja, bi] for bi in range(64)], evac)
        rs = k.sb([128, 40], F32, "rs")
        rsB = Buf("rs")
        pb = next_ps()
        for n in range(nchunk):
            k.op(PE, lambda e, n=n, pb=pb: e.matmul(psum[:, pb, n:n + 1], lhsT=acc[:, n * 128:(n + 1) * 128], rhs=one_t[:],
                                                   start=True, stop=True),
                 reads=[accB[n // 4], B_const], writes=[PSB[pb]])
        k.op(ACT, lambda e: e.activation(out=rs[:, 0:nchunk], in_=psum[:, pb, 0:nchunk], func=AF.Sqrt, scale=1.0 / 4096, bias=eps_t[:]),
             reads=[PSB[pb], B_const], writes=[rsB])
        k.op(DVE, lambda e: e.reciprocal(out=rs[:, 0:nchunk], in_=rs[:, 0:nchunk]), reads=[rsB], writes=[rsB])
        k.barrier()
        k.sb_reset(arena_mark)
        rs2 = k.sb([128, 40], F32, "rs2")
        assert k.sb_mark() <= mark
        k.op(DVE, lambda e: e.tensor_copy(out=rs2[:, 0:nchunk], in_=rs[:, 0:nchunk]), reads=[rsB], writes=[rsB])
        k.barrier()
        wsT = k.sb([128, 16, 128], BF16, "wsT")
        bsb = k.sb([128, 32, 128], F32, "bsb")
        gvrow = k.sb([128, 4096], BF16, "gvrow")
        Bw = Buf("wsT")
        chA = get_chan("c_wsT")
        k.dma(POOL, chA, wsT[:], a_wsT[ja], writes=[Bw])
        k.dma(SP, ch_misc, bsb[:], a_bs_bc[ja], writes=[Bw])
        k.dma(POOL, chA, gvrow[:], a_gv_bc[ja], writes=[Bw], max_dma_last_dim=4096)
        Ut = Slots("Ut", [128, 32, 256], BF16, 2)
        Vt = Slots("Vt", [128, 32, 256], BF16, 2)
        US = [k.sb([128, 32, 512], BF16, "US") for _ in range(2)]
        USB = [Buf("US0"), Buf("US1")]
        vtm = Slots("vtm", [128, 512], BF16, 3, chan=False)
        stmp = Slots("stmp", [128, 512], F32, 2, chan=False)
        ws2 = WStream("waout", [128, 32, 128], 3)
        Hc = Slots("HcA", [128, 512], F32, 4)
        stc = [get_chan(f"c_hst{i}") for i in range(4)]
        sti = 0
        def sp_front(vt, vb, cnl, n, f4):
            pt = next_ps()
            ptile = psum[:, pt, :].bitcast(BF16)
            fns = []
            for ff in range(4):
                fb = f4 * 4 + ff
                fns.append(lambda e, ff=ff, fb=fb: e.transpose(
                    ptile[:, ff * 128:(ff + 1) * 128], vt[:, fb, cnl * 128:(cnl + 1) * 128], ident_bf[:]))
            k.op(PE, fns, reads=[vb, B_const], writes=[PSB[pt]])
            lt, lb, _ = vtm.next()
            k.op(DVE, lambda e: e.scalar_tensor_tensor(
                out=lt[:], in0=ptile[:, 0:512], scalar=rs2[:, n:n + 1], in1=gvrow[:, f4 * 512:(f4 + 1) * 512],
                op0=ALU.mult, op1=ALU.mult), reads=[PSB[pt], rsB, Bw], writes=[lb])
            return lt, lb

        def sp_back(lt, lb, ut, ub, cnl, ti, cn, f4):
            ps2 = next_ps()
            fns = []
            for ff in range(4):
                fb = f4 * 4 + ff
                fns.append(lambda e, ff=ff, fb=fb: e.matmul(
                    psum[:, ps2, ff * 128:(ff + 1) * 128], lhsT=lt[:, ff * 128:(ff + 1) * 128], rhs=wsT[:, fb // 2, :],
                    start=True, stop=True))
            k.op(PE, fns, reads=[lb, Bw], writes=[PSB[ps2]])
            tt, tb, _ = stmp.next()
            k.op(DVE, lambda e: e.tensor_tensor(
                out=tt[:], in0=psum[:, ps2, 0:512],
                in1=bsb[:, f4 * 4:(f4 + 1) * 4, :].rearrange("p a b -> p (a b)"), op=ALU.add),
                 reads=[PSB[ps2], Bw], writes=[tb])
            k.op(POOL, lambda e: e.tensor_tensor(
                out=US[ti][:, f4 * 4:(f4 + 1) * 4, cn * 128:(cn + 1) * 128],
                in0=tt[:].rearrange("p (a b) -> p a b", b=128),
                in1=ut[:, f4 * 4:(f4 + 1) * 4, cnl * 128:(cnl + 1) * 128], op=ALU.mult),
                 reads=[tb, ub], writes=[USB[ti]])

        for g in L.groups:
            pend = None
            for ti, t in enumerate(g):
                t0, w, col = L.tiles[t]
                for h0 in range(0, w, 256):
                    hw = min(256, w - h0)
                    ut, ub, uch = Ut.next()
                    vt, vb, vch = Vt.next()
                    k.dma(SP, vch, vt[:, :, 0:hw], S2[0:4096, t0 + h0:t0 + h0 + hw].rearrange("(j p) t -> p j t", p=128), writes=[vb])
                    k.dma(SP, uch, ut[:, :, 0:hw], S1[0:4096, t0 + h0:t0 + h0 + hw].rearrange("(j p) t -> p j t", p=128), writes=[ub])
                    for cnl in range(hw // 128):
                        cn = h0 // 128 + cnl
                        n = t0 // 128 + cn
                        for f4 in range(8):
                            lt, lb = sp_front(vt, vb, cnl, n, f4)
                            if pend is not None:
                                sp_back(*pend)
                            pend = (lt, lb, ut, ub, cnl, ti, cn, f4)
            if pend is not None:
                sp_back(*pend)
            for c in range(KC):
                wt, wb = ws2.load(a_out_b[ja, c])
                for ti, t in enumerate(g):
                    t0, w, col = L.tiles[t]
                    pb = next_ps()
                    mm_group(psum[:, pb, 0:w], [(wt[:, j, :], US[ti][:, j, 0:w]) for j in range(32)],
                             reads=[wb, USB[ti]], writes=[PSB[pb]])
                    hupdate(L, Hc, stc, sti, c, t, pb, li, 2)
                    sti += 1
        k.barrier()
        k.sb_reset(mark)

    def stage_mixC_x(Narena, NB, arena_mark):
        mark = k.sb_mark()
        ws = WStream("wcin", [128, KC, 128], 4)
        xo = Slots("xo", [128, 512], F32, 3)

        def evac(bi, t, pb):
            t0, w, col = Lm.tiles[t]
            ot, ob, och = xo.next()
            k.op(DVE, lambda e: e.tensor_copy(out=ot[:, 0:w], in_=psum[:, pb, 0:w]), reads=[PSB[pb]], writes=[ob])
            k.dma(SP, och, X4[bi * 128:(bi + 1) * 128, t0:t0 + w], ot[:, 0:w], reads=[ob])

        proj1(Lm, Narena, NB, ws, [c_in_b[16 + bi] for bi in range(16)], evac)
        k.barrier()
        k.sb_reset(arena_mark)
        ZW = 4360
        LOFF, COFF = 2, 4101
        xz = Slots("xz", [128, ZW], F32, 1)
        xz2c = [get_chan("c_xzb0"), get_chan("c_xzb1")]
        xcs = [k.sb([128, T], F32, "xc") for _ in range(2)]; xcBs = [Buf("xc0"), Buf("xc1")]
        xcbs = [k.sb([128, T], BF16, "xcb") for _ in range(2)]; xcbBs = [Buf("xcb0"), Buf("xcb1")]
        Rbs = [k.sb([128, T], F32, "R") for _ in range(2)]; RBts = [[Buf(f"R{d}_{t}") for t in range(9)] for d in range(2)]
        Ibs = [k.sb([128, T], F32, "I") for _ in range(2)]; IBts = [[Buf(f"I{d}_{t}") for t in range(9)] for d in range(2)]
        Tb = k.sb([128, T], F32, "Tm"); TBt = [Buf(f"Tm{t}") for t in range(9)]
        HF = k.sb([128, T], F32, "HF"); HFB = Buf("HF")
        HR = k.sb([128, T], F32, "HR"); HRB = Buf("HR")
        hst = [get_chan("c_hs0"), get_chan("c_hs1")]
        X4B = [Buf(f"X4_{h}") for h in range(KC)]
        wg = WStream("wg", [128, 2, 2, 128], 2)
        ccw = k.sb([128, 4, KC], F32, "ccw")
        ccb = k.sb([128, KC], F32, "ccb")
        bgt = k.sb([128, 2, 2, KC], F32, "bgt")
        lam = k.sb([128, 2, KC], F32, "lam")
        cneg = k.sb([128, 2, KC], F32, "cneg")
        Bp = Buf("cparams")
        k.dma(SP, ch_misc, ccw[:], c_cw_fm[:, :, :], writes=[Bp])
        k.dma(SP, ch_misc, ccb[:], c_cb_fm[:, :], writes=[Bp])
        k.dma(SP, ch_misc, bgt[:], c_bg_fm[:, :, :, :], writes=[Bp])
        k.dma(SP, ch_misc, lam[:], c_lam_fm[:, :, :], writes=[Bp])
        k.op(ACT, lambda e: e.activation(out=cneg[:], in_=lam[:], func=AF.Exp, scale=-1.0), reads=[Bp], writes=[Bp])
        k.op(ACT, lambda e: e.activation(out=cneg[:], in_=cneg[:], func=AF.Ln, bias=one_t[:], scale=1.0), reads=[Bp, B_const], writes=[Bp])
        k.op(DVE, lambda e: e.tensor_scalar(out=cneg[:], in0=cneg[:], scalar1=-8.0, scalar2=None, op0=ALU.mult), reads=[Bp], writes=[Bp])
        for s in range(1):
            t_, b_, _ = xz.next()
            k.op(POOL, lambda e, t_=t_: e.memset(t_[:], 0.0), writes=[b_])
        nctx = True
        segs = [(0, SEQ, LOFF)] + ([(SEQ, CTX, COFF)] if nctx else [])
        ntok = SEQ + (CTX if nctx else 0)
        def stage_A(h):
            xc, xcB = xcs[h % 2], xcBs[h % 2]
            xcb, xcbB = xcbs[h % 2], xcbBs[h % 2]
            xt, xb, xch = xz.next()
            k.dma(SP, xch, xt[:, LOFF:LOFF + SEQ], X4[h * 128:(h + 1) * 128, 0:SEQ], reads=[X4B[h]], writes=[xb])
            k.dma(SP, xz2c[h % 2], xt[:, COFF:COFF + CTX], X4[h * 128:(h + 1) * 128, SEQ:T], reads=[X4B[h]], writes=[xb])
            wgt, wgb = wg.load(c_wg[:, :, :, h, :])
            for (c0, n, off) in segs:
                k.op(DVE, lambda e, c0=c0, n=n, off=off: e.tensor_scalar(
                    out=xc[:, c0:c0 + n], in0=xt[:, off:off + n], scalar1=ccw[:, 2, h:h + 1], scalar2=ccb[:, h:h + 1],
                    op0=ALU.mult, op1=ALU.add), reads=[xb, Bp], writes=[xcB])
                for j in (0, 1, 3):
                    k.op(DVE, lambda e, c0=c0, n=n, off=off, j=j: e.scalar_tensor_tensor(
                        out=xc[:, c0:c0 + n], in0=xt[:, off + j - 2:off + j - 2 + n], scalar=ccw[:, j, h:h + 1],
                        in1=xc[:, c0:c0 + n], op0=ALU.mult, op1=ALU.add), reads=[xb, Bp, xcB], writes=[xcB])
            k.op(POOL, lambda e: e.tensor_copy(out=xcb[:, 0:ntok], in_=xc[:, 0:ntok]), reads=[xcB], writes=[xcbB])
            return dict(h=h, xc=xc, xcB=xcB, xcb=xcb, xcbB=xcbB, wgt=wgt, wgb=wgb)

        def stage_B(C, d):
            h, xc, xcB, xcb, xcbB, wgt, wgb = C["h"], C["xc"], C["xcB"], C["xcb"], C["xcbB"], C["wgt"], C["wgb"]
            Rb, Ib, RBt, IBt = Rbs[d], Ibs[d], RBts[d], IBts[d]
            for t in range(Lm.ntiles):
                t0, w, col_ = Lm.tiles[t]
                for gidx, (gt, gBs) in enumerate(((Rb, RBt), (Ib, IBt))):
                    pb = next_ps()
                    k.op(PE, lambda e, pb=pb, w=w, t0=t0, gidx=gidx: e.matmul(
                        psum[:, pb, 0:w], lhsT=wgt[:, d, gidx, :], rhs=xcb[:, t0:t0 + w], start=True, stop=True),
                         reads=[wgb, xcbB], writes=[PSB[pb]])
                    k.op(ACT, lambda e, pb=pb, w=w, t0=t0, gidx=gidx, gt=gt: e.activation(
                        out=gt[:, t0:t0 + w], in_=psum[:, pb, 0:w], func=AF.Sigmoid,
                        bias=bgt[:, d, gidx, h:h + 1], scale=1.0), reads=[PSB[pb], Bp], writes=[gBs[t]])
            for t in range(Lm.ntiles):
                t0, w, col_ = Lm.tiles[t]
                k.op(ACT, lambda e, t0=t0, w=w: e.activation(out=Rb[:, t0:t0 + w], in_=Rb[:, t0:t0 + w], func=AF.Exp,
                                                             scale=cneg[:, d, h:h + 1]), reads=[RBt[t], Bp], writes=[RBt[t]])
                k.op(DVE, lambda e, t0=t0, w=w: e.tensor_tensor(out=Tb[:, t0:t0 + w], in0=Rb[:, t0:t0 + w], in1=Rb[:, t0:t0 + w], op=ALU.mult),
                     reads=[RBt[t]], writes=[TBt[t]])
                k.op(ACT, lambda e, t0=t0, w=w: e.activation(out=Tb[:, t0:t0 + w], in_=Tb[:, t0:t0 + w], func=AF.Sqrt, scale=-1.0, bias=one_t[:]),
                     reads=[TBt[t], B_const], writes=[TBt[t]])
                k.op(POOL, lambda e, t0=t0, w=w: e.tensor_tensor(out=Ib[:, t0:t0 + w], in0=Ib[:, t0:t0 + w], in1=xc[:, t0:t0 + w], op=ALU.mult),
                     reads=[IBt[t], xcB], writes=[IBt[t]])
                k.op(POOL, lambda e, t0=t0, w=w: e.tensor_tensor(out=Ib[:, t0:t0 + w], in0=Ib[:, t0:t0 + w], in1=Tb[:, t0:t0 + w], op=ALU.mult),
                     reads=[IBt[t], TBt[t]], writes=[IBt[t]])

        def stage_C(C, d):
            Rb, Ib, RBt, IBt = Rbs[d], Ibs[d], RBts[d], IBts[d]
            Hout, HoutB = (HF, HFB) if d == 0 else (HR, HRB)
            if d == 0:
                k.op(DVE, lambda e: e.tensor_tensor_scan(
                    out=Hout[:, SEQ:T], data0=Rb[:, SEQ:T], data1=Ib[:, SEQ:T], initial=0.0, op0=ALU.mult, op1=ALU.add),
                     reads=[RBt[8], IBt[8]], writes=[HoutB])
                init = Hout[:, T - 1:T]
                k.op(DVE, lambda e: e.tensor_tensor_scan(
                    out=Hout[:, 0:SEQ], data0=Rb[:, 0:SEQ], data1=Ib[:, 0:SEQ], initial=init, op0=ALU.mult, op1=ALU.add),
                     reads=RBt[0:8] + IBt[0:8] + [HoutB], writes=[HoutB])
            else:
                k.op(DVE, lambda e: e.tensor_tensor_scan(
                    out=Hout[:, SEQ:T][:, ::-1], data0=Rb[:, SEQ:T][:, ::-1], data1=Ib[:, SEQ:T][:, ::-1], initial=0.0,
                    op0=ALU.mult, op1=ALU.add), reads=[RBt[8], IBt[8]], writes=[HoutB])
                init = Hout[:, SEQ:SEQ + 1]
                k.op(DVE, lambda e: e.tensor_tensor_scan(
                    out=Hout[:, 0:SEQ][:, ::-1], data0=Rb[:, 0:SEQ][:, ::-1], data1=Ib[:, 0:SEQ][:, ::-1], initial=init,
                    op0=ALU.mult, op1=ALU.add), reads=RBt[0:8] + IBt[0:8] + [HoutB], writes=[HoutB])

        def stage_D(C):
            h = C["h"]
            k.op(DVE, lambda e: e.tensor_tensor(out=HR[:, 0:ntok], in0=HF[:, 0:ntok], in1=HR[:, 0:ntok], op=ALU.add),
                 reads=[HFB, HRB], writes=[HRB])
            k.dma(SP, hst[h % 2], X4[h * 128:(h + 1) * 128, 0:ntok], HR[:, 0:ntok], reads=[HRB], writes=[X4B[h]])

        ctxs = {0: stage_A(0)}
        for h in range(KC):
            stage_B(ctxs[h], 0)
            if h + 1 < KC:
                ctxs[h + 1] = stage_A(h + 1)
            stage_B(ctxs[h], 1)
            stage_C(ctxs[h], 0)
            stage_C(ctxs[h], 1)
            stage_D(ctxs[h])
        k.barrier()
        k.sb_reset(mark)

    def stage_mixC_y(L, Narena, NB, li):
        mark = k.sb_mark()
        ws = WStream("wcin", [128, KC, 128], 4)
        yo = Slots("yo", [128, 512], F32, 3, chan=False)
        hs = Slots("hsw", [128, 512], F32, 3)
        oo = Slots("oo", [128, 512], BF16, 3)

        def evac(bi, t, pb):
            t0, w, col = L.tiles[t]
            yt, yb, _ = yo.next()
            k.op(ACT, lambda e: e.activation(out=yt[:, 0:w], in_=psum[:, pb, 0:w], func=AF.Gelu_apprx_tanh),
                 reads=[PSB[pb]], writes=[yb])
            ht, hbuf, hch = hs.next()
            k.dma(SP, hch, ht[:, 0:w], HSw[bi * 128:(bi + 1) * 128, t0:t0 + w], writes=[hbuf])
            ot, ob, och = oo.next()
            k.op(DVE, lambda e: e.tensor_tensor(out=ot[:, 0:w], in0=yt[:, 0:w], in1=ht[:, 0:w], op=ALU.mult),
                 reads=[yb, hbuf], writes=[ob])
            k.dma(SP, och, S4[bi * 128:(bi + 1) * 128, t0:t0 + w], ot[:, 0:w], reads=[ob])

        proj1(L, Narena, NB, ws, [c_in_b[bi] for bi in range(16)], evac)
        k.barrier()
        k.sb_reset(mark)

    def stage_mixB(Narena, NB, li, arena_mark):
        ntiles = 9
        mark = k.sb_mark()
        ws = WStream("wqkv", [128, KC, 128], 4)
        qo = Slots("qo", [128, 512], BF16, 4)
        qscale = float(128 ** -0.5)

        def evac(bi, t, pb):
            t0, w, col_ = Lm.tiles[t]
            ot, ob, och = qo.next()
            which, hh = bi // 16, bi % 16
            dst = (S1, S2, S3)[which]
            if which == 0:
                k.op(ACT, lambda e: e.activation(out=ot[:, 0:w], in_=psum[:, pb, 0:w], func=AF.Copy, scale=qscale),
                     reads=[PSB[pb]], writes=[ob])
            else:
                k.op(DVE, lambda e: e.tensor_copy(out=ot[:, 0:w], in_=psum[:, pb, 0:w]), reads=[PSB[pb]], writes=[ob])
            k.dma(SP, och, dst[hh * 128:(hh + 1) * 128, t0:t0 + w], ot[:, 0:w], reads=[ob])

        proj1(Lm, Narena, NB, ws, [b_qkv_b[bi] for bi in range(48)], evac)
        k.barrier()
        k.sb_reset(arena_mark)
        Qs = Slots("Qh", [128, T], BF16, 2)
        Ks = Slots("Kh", [128, T], BF16, 2)
        Vs = Slots("Vh", [128, T], BF16, 2)
        Bt = Slots("Bt", [128, 6, 1024], BF16, 2)
        Vtm = Slots("Vtm", [128, 34, 128], BF16, 2, chan=False)
        Oh = Slots("Oh", [128, T], BF16, 2)
        Ps = Slots("P", [128, 1024], BF16, 3, chan=False)
        Dgs = Slots("Dg", [128, 128], BF16, 3, chan=False)
        PTs = Slots("PT", [128, 7, 128], BF16, 3, chan=False)
        sm = Slots("sm", [128, 4], F32, 4, chan=False)
        sc_rr = [0]
        o_rr = [0]
        nunits = 34 if ntiles == 9 else 32
        def do_head(h):
            qt, qb, qch = Qs.next()
            kt, kb, kch = Ks.next()
            vt, vb, vch = Vs.next()
            bt, bb, bch = Bt.next()
            k.dma(SP, qch, qt[:], S1[h * 128:(h + 1) * 128, :], writes=[qb])
            k.dma(SP, kch, kt[:], S2[h * 128:(h + 1) * 128, :], writes=[kb])
            k.dma(SP, vch, vt[:], S3[h * 128:(h + 1) * 128, :], writes=[vb])
            k.dma(POOL, bch, bt[:], b_bias[h], writes=[bb], max_dma_last_dim=4096)
            vtm, vtmb, _ = Vtm.next()
            for c4 in range(0, 34, 4):
                nb_ = min(4, 34 - c4)
                pt = 6 + o_rr[0] % 2
                o_rr[0] += 1
                ptile = psum[:, pt, :].bitcast(BF16)
                fns = []
                for ff in range(nb_):
                    cn = c4 + ff
                    fns.append(lambda e, ff=ff, cn=cn, ptile=ptile, vt=vt: e.transpose(
                        ptile[:, ff * 128:(ff + 1) * 128], vt[:, cn * 128:(cn + 1) * 128], ident_bf[:]))
                k.op(PE, fns, reads=[vb, B_const], writes=[PSB[pt]])
                k.op(ACT, lambda e, c4=c4, nb_=nb_, ptile=ptile, vtm=vtm: e.activation(
                    out=vtm[:, c4:c4 + nb_, :], in_=ptile[:, 0:nb_ * 128].rearrange("p (a b) -> p a b", b=128), func=AF.Copy),
                     reads=[PSB[pt]], writes=[vtmb])
            ot, ob, och = Oh.next()
            pend_pt = None
            pend_pv = None

            def do_scores(u):
                if u < 32:
                    a = u
                    ti = {0: 1, 1: 2, 30: 3, 31: 4}.get(a, 0)
                    lo = (2 * a - 4) if 2 <= a <= 29 else (0 if a < 2 else 56)
                    k0 = lo * 64
                    q0 = a * 128
                    nlat = 576
                else:
                    ti = 5
                    q0 = SEQ + (u - 32) * 128
                    k0 = 0
                    nlat = 0
                pa = (sc_rr[0] % 2) * 2
                pb2 = pa + 1
                sc_rr[0] += 1
                qv = qt[:, q0:q0 + 128]
                if nlat:
                    k.op(PE, [lambda e: e.matmul(psum[:, pa, 0:512], lhsT=ident_bf[:], rhs=bt[:, ti, 0:512], start=True, stop=False),
                              lambda e: e.matmul(psum[:, pa, 0:512], lhsT=qv, rhs=kt[:, k0:k0 + 512], start=False, stop=True)],
                         reads=[qb, kb, bb, B_const], writes=[PSB[pa]])
                    k.op(PE, [lambda e: e.matmul(psum[:, pb2, 0:512], lhsT=ident_bf[:], rhs=bt[:, ti, 512:1024], start=True, stop=False),
                              lambda e: e.matmul(psum[:, pb2, 0:64], lhsT=qv, rhs=kt[:, k0 + 512:k0 + 576], start=False, stop=False),
                              lambda e: e.matmul(psum[:, pb2, 64:320], lhsT=qv, rhs=kt[:, SEQ:T], start=False, stop=True)],
                         reads=[qb, kb, bb, B_const], writes=[PSB[pb2]])
                else:
                    k.op(PE, lambda e: e.matmul(psum[:, pa, 0:512], lhsT=ident_bf[:], rhs=bt[:, ti, 0:512], start=True, stop=True),
                         reads=[bb, B_const], writes=[PSB[pa]])
                    k.op(PE, [lambda e: e.matmul(psum[:, pb2, 0:512], lhsT=ident_bf[:], rhs=bt[:, ti, 512:1024], start=True, stop=False),
                              lambda e: e.matmul(psum[:, pb2, 0:256], lhsT=qv, rhs=kt[:, SEQ:T], start=False, stop=True)],
                         reads=[qb, kb, bb, B_const], writes=[PSB[pb2]])
                lv = psum[:, pa:pa + 2, :].rearrange("p a b -> p (a b)")
                st, sb_, _ = sm.next()
                k.op(DVE, lambda e: e.reduce_max(out=st[:, 0:1], in_=lv, axis=AX.X), reads=[PSB[pa], PSB[pb2]], writes=[sb_])
                k.op(DVE, lambda e: e.tensor_scalar(out=st[:, 1:2], in0=st[:, 0:1], scalar1=-1.0, scalar2=None, op0=ALU.mult),
                     reads=[sb_], writes=[sb_])
                pt_, pb_, _ = Ps.next()
                k.op(ACT, lambda e: e.activation(out=pt_[:, 0:1024], in_=lv, func=AF.Exp, bias=st[:, 1:2], scale=1.0, accum_out=st[:, 2:3]),
                     reads=[PSB[pa], PSB[pb2], sb_], writes=[pb_, sb_])
                k.op(DVE, lambda e: e.reciprocal(out=st[:, 3:4], in_=st[:, 2:3]), reads=[sb_], writes=[sb_])
                dg, dgb, _ = Dgs.next()
                k.op(DVE, lambda e: e.tensor_scalar(out=dg[:], in0=ident_bf[:], scalar1=st[:, 3:4], scalar2=None, op0=ALU.mult),
                     reads=[sb_, B_const], writes=[dgb])
                blocks = []
                if nlat:
                    for i in range(4):
                        blocks.append((i * 128, 128, k0 // 128 + i))
                    blocks.append((512, 64, k0 // 128 + 4))
                    blocks.append((576, 128, 32))
                    blocks.append((704, 128, 33))
                else:
                    blocks.append((512, 128, 32))
                    blocks.append((640, 128, 33))
                return dict(P=pt_, Pb=pb_, dg=dg, dgb=dgb, blocks=blocks, q0=q0)

            def do_pt(U):
                ptt, ptb, _ = PTs.next()
                blocks = U["blocks"]
                for bank, lo_, hi_ in ((4, 0, 4), (5, 4, 7)):
                    bl = blocks[lo_:hi_]
                    if not bl:
                        continue
                    fns = []
                    for bi_, (pc, nk_, ch_) in enumerate(bl):
                        fns.append(lambda e, bi_=bi_, pc=pc, nk_=nk_, bank=bank: e.matmul(
                            psum[0:nk_, bank, bi_ * 128:(bi_ + 1) * 128], lhsT=U["P"][:, pc:pc + nk_], rhs=U["dg"][:],
                            start=True, stop=True))
                    k.op(PE, fns, reads=[U["Pb"], U["dgb"]], writes=[PSB[bank]])
                    nb_ = len(bl)
                    k.op(ACT, lambda e, bank=bank, lo_=lo_, nb_=nb_: e.activation(
                        out=ptt[:, lo_:lo_ + nb_, :], in_=psum[:, bank, 0:nb_ * 128].rearrange("p (a b) -> p a b", b=128), func=AF.Copy),
                         reads=[PSB[bank]], writes=[ptb])
                U["PT"] = ptt
                U["PTb"] = ptb

            def do_pv(U):
                po = 6 + o_rr[0] % 2
                o_rr[0] += 1
                blocks = U["blocks"]
                n = len(blocks)
                fns = []
                for bi_, (pc, nk_, ch_) in enumerate(blocks):
                    fns.append(lambda e, bi_=bi_, nk_=nk_, ch_=ch_, po=po: e.matmul(
                        psum[:, po, 0:128], lhsT=vtm[0:nk_, ch_, :], rhs=U["PT"][0:nk_, bi_, :],
                        start=(bi_ == 0), stop=(bi_ == n - 1)))
                k.op(PE, fns, reads=[vtmb, U["PTb"]], writes=[PSB[po]])
                q0 = U["q0"]
                k.op(DVE, lambda e: e.tensor_copy(out=ot[:, q0:q0 + 128], in_=psum[:, po, 0:128]), reads=[PSB[po]], writes=[ob])

            for u in range(nunits + 2):
                Unew = do_scores(u) if u < nunits else None
                if pend_pt is not None:
                    do_pt(pend_pt)
                if pend_pv is not None:
                    do_pv(pend_pv)
                pend_pv = pend_pt
                pend_pt = Unew
            ntok = T
            k.dma(SP, och, S4[h * 128:(h + 1) * 128, 0:ntok], ot[:, 0:ntok], reads=[ob])
        for h in range(KC):
            do_head(h)
        k.barrier()
        k.sb_reset(arena_mark)
        proj2(Lm, S4, KC, lambda c: b_out_b[c], li, 2, resident=True)
        k.sb_reset(mark)

    def new_arena(L):
        k.sb_reset(arena_mark)
        Na = k.sb([128, KC, L.T], BF16, "N")
        return Na, [Buf(f"N{t}") for t in range(L.ntiles)]

    def ffn_block(L, li, edge_mask=False):
        Na, NB = new_arena(L)
        stage_norm(L, Na, NB, li, 1, edge_mask=edge_mask)
        stage_ffn(L, Na, NB, li)
        k.sb_reset(arena_mark)
        proj2(L, S1, NJ, lambda c, li=li: ffn_dn_b[li, c], li, 5)

    def dyn_vals(h):
        if force_q is not None:
            q = force_q
            s3 = min(max(1024 * q - 128, 0), 2816)
            return s3, max(s3 - 1, 0), 1024 * q - s3
        pid = h.partition_id()
        q = pid % 4
        nz = (q + 3) // 4
        ge2 = q // 2
        is3 = q // 3
        s3 = 896 * nz + 1024 * ge2 + 896 * is3
        s3m1 = 895 * nz + 1024 * ge2 + 896 * is3
        own = 128 * nz + 128 * is3
        return s3, s3m1, own

    def dsl(start, size):
        if isinstance(start, int):
            return slice(start, start + size)
        return bass.ds(start, size)

    stage_ada()
    arena_mark = k.sb_mark()
    done = False
    for li in range(min(n_layers, 3)):
        Na, NB = new_arena(Lm)
        stage_norm(Lm, Na, NB, li, 0)
        if li == 0:
            stage_mixA(Lm, Na, NB, li, 0, arena_mark)
        elif li == 1:
            stage_mixB(Na, NB, li, arena_mark)
        else:
            stage_mixC_x(Na, NB, arena_mark)
            for ci, (dst, srcT) in enumerate(((Hw, Hd_full), (HSw, X4_full))):
                k.dma(SP, ch_cp[ci], dst[:, 0:1282], lambda h, srcT=srcT: srcT[:, dsl(dyn_vals(h)[0], 1282)])
            k.barrier()
            Na, NB = new_arena(LW2)
            stage_norm(LW2, Na, NB, li, 0)
            stage_mixC_y(LW2, Na, NB, li)
            k.sb_reset(arena_mark)
            proj2(LW2, S4, KC, lambda c: c_out_b[c], li, 2, resident=True)
        if stop_after == (li, "mix"):
            done = True
            break
        ffn_block(Lm if li < 2 else LW2, li, edge_mask=(li == 2))
        if stop_after == (li, "ffn"):
            done = True
            break
    if not done and n_layers == DEPTH:
        li = 3
        Na, NB = new_arena(LW3)
        stage_norm(LW3, Na, NB, li, 0)
        stage_mixA(LW3, Na, NB, li, 1, arena_mark)
        if stop_after != (3, "mix"):
            ffn_block(LW3, li)
            if stop_after != (3, "ffn"):
                k.sb_reset(arena_mark)
                stage_norm(LW3, None, None, 0, 0, final_dst=OutW)
                k.dma(SP, ch_cp[2], outT[:, :], lambda h: OutW[:, dsl(dyn_vals(h)[2], 1024)])
    if stop_after is not None:
        li_, st_ = stop_after
        if li_ < 2 or (li_ == 2 and False):
            dbg = nc.dram_tensor("dbgH", [D, T], F32, kind="ExternalOutput").ap()
            k.dma(SP, ch_cp[3], dbg[:, :], Hd[:, :])
        else:
            dbg = nc.dram_tensor("dbgH", [D, 1282], F32, kind="ExternalOutput").ap()
            k.dma(SP, ch_cp[3], dbg[:, :], Hw[:, :])
    k.barrier()

    def replay(eng, h):
        for kind_, waits, payload, sem in eng.prog:
            for s, v in waits:
                h.wait_ge(s.h, v)
            if kind_ == "wait":
                continue
            if kind_ == "op":
                ins = None
                for fn in payload:
                    ins = fn(h)
                ins.then_inc(sem.h, 1)
            elif kind_ == "dma":
                out, in_, kw = payload
                if callable(out):
                    out = out(h)
                if callable(in_):
                    in_ = in_(h)
                try:
                    ins = h.dma_start(out=out, in_=in_, **kw)
                except Exception:
                    print("DMA FAILED", out, in_, kw)
                    raise
                ins.then_inc(sem.h, 16)

    with nc.Block() as block:
        @block.tensor
        def _(e):
            replay(PE, e)

        @block.scalar
        def _(e):
            replay(ACT, e)

        @block.vector
        def _(e):
            replay(DVE, e)

        @block.gpsimd
        def _(e):
            replay(POOL, e)

        @block.sync
        def _(e):
            replay(SP, e)

    for name, (cm, h) in sem_handles.items():
        pass
    return nc


def _blk(w, kc=KC):
    Kd, Nd = w.shape
    nb = Nd // 128
    return np.ascontiguousarray(w.reshape(kc, 128, nb, 128).transpose(2, 1, 0, 3))


def _blk2(w, kb):
    return np.ascontiguousarray(w.reshape(kb, 128, KC, 128).transpose(2, 1, 0, 3))


def _fm(v):
    lead = v.shape[:-1]
    C = v.shape[-1] // 128
    a = v.reshape(*lead, C, 128)
    return np.ascontiguousarray(np.moveaxis(a, -1, 0))


def _na_bias(rpb):
    NEG = np.float32(-1e30)
    H = rpb.shape[0]
    out = np.full((H, 128, 6, 1024), NEG, np.float32)
    col = np.arange(64)
    c0 = np.clip(col - 8, 0, 48)
    col_ok = (col[None, :] >= c0[:, None]) & (col[None, :] < c0[:, None] + 16)
    dc = np.clip(col[None, :] - col[:, None], -15, 15) + 15
    specs = [(8, 12, 9), (0, 0, 8), (1, 0, 8), (30, 56, 8), (31, 56, 8)]
    for ti, (a, lo, nrows) in enumerate(specs):
        tab = np.full((H, 128, 1024), NEG, np.float32)
        tab[:, :, 576:832] = 0.0
        for qr in range(2):
            r = 2 * a + qr
            r0 = min(max(r - 4, 0), 56)
            for i in range(8):
                kr = r0 + i
                u = kr - lo
                assert 0 <= u < nrows
                dr = kr - r
                vals = rpb[:, dr + 7, :][:, dc]
                vals = np.where(col_ok[None], vals, NEG)
                tab[:, qr * 64:(qr + 1) * 64, u * 64:(u + 1) * 64] = vals
        out[:, :, ti, :] = tab
    out[:, :, 5, 512:768] = 0.0
    return out


def _prep_inputs(inp):
    f = lambda a: np.asarray(a, np.float32)
    shared = {}
    aw = f(inp["ada_w"])
    shared["ada_wb"] = np.ascontiguousarray(aw.reshape(DEPTH, KC, 128, 24, 512).transpose(0, 3, 2, 1, 4))
    shared["ada_b_fm"] = _fm(f(inp["ada_b"]))
    shared["norm_g_fm"] = _fm(f(inp["norm_g"]))
    shared["final_g_fm"] = _fm(f(inp["final_g"]))
    shared["ffn_up_b"] = np.stack([_blk(f(inp["ffn_w_up"])[i]) for i in range(DEPTH)])
    shared["ffn_cw_fm"] = _fm(f(inp["ffn_conv_w"]))
    shared["ffn_cb_fm"] = _fm(f(inp["ffn_conv_b"]))
    shared["ffn_dn_b"] = np.stack([_blk2(f(inp["ffn_w_down"])[i], NJ) for i in range(DEPTH)])
    shared["a_in_b"] = np.stack([_blk(f(inp["a_w_in"])[i]) for i in range(2)])
    shared["a_gv_fm"] = _fm(f(inp["a_g_v"]))
    shared["a_wsT"] = np.ascontiguousarray(f(inp["a_w_s"]).transpose(0, 3, 1, 2))
    shared["a_bs_bc"] = np.ascontiguousarray(np.broadcast_to(np.repeat(f(inp["a_b_s"]), 2, axis=1)[:, None], (2, 128, 32, 128)))
    shared["a_gv_bc"] = np.ascontiguousarray(np.broadcast_to(f(inp["a_g_v"])[:, None], (2, 128, 4096)))
    shared["a_out_b"] = np.stack([_blk2(f(inp["a_w_out"])[i], 32) for i in range(2)])
    shared["b_qkv_b"] = _blk(f(inp["b_w_qkv"])[0])
    shared["b_bias"] = _na_bias(f(inp["b_rpb"])[0])
    shared["b_out_b"] = _blk2(f(inp["b_w_out"])[0], KC)
    shared["c_in_b"] = _blk(f(inp["c_w_in"])[0])
    shared["c_cw_fm"] = _fm(f(inp["c_conv_w"])[0])
    shared["c_cb_fm"] = _fm(f(inp["c_conv_b"])[0])
    shared["c_wg"] = np.ascontiguousarray(f(inp["c_w_gate"])[0].transpose(3, 0, 1, 2, 4))
    shared["c_bg_fm"] = np.ascontiguousarray(f(inp["c_b_gate"])[0].transpose(3, 0, 1, 2))
    shared["c_lam_fm"] = _fm(f(inp["c_lam"])[0])
    shared["c_out_b"] = _blk2(f(inp["c_w_out"])[0], KC)
    shared["ident_in"] = np.eye(128, dtype=np.float32)
    maps = []
    x = f(inp["x"]); ctx = f(inp["ctx"]); c = f(inp["c"]); cc = f(inp["c_ctx"])
    per_b = []
    for b in range(2):
        per_b.append((np.ascontiguousarray(np.concatenate([x[b], ctx[b]], axis=0).T),
                      np.ascontiguousarray(np.stack([_fm(c[b]), _fm(cc)], axis=-1))))
    for r in range(8):
        b, q = r // 4, r % 4
        m = dict(shared)
        m["xT"], m["c_fm"] = per_b[b]
        em = np.ones((128, 2), np.float32)
        if q == 0:
            em[:, 0] = 0.0
        if q == 3:
            em[:, 1] = 0.0
        m["emask_in"] = em
        maps.append(m)
    return maps


_CACHE = {}


def kernel(**inputs):
    maps = _prep_inputs(inputs)
    if "nc" not in _CACHE:
        _CACHE["nc"] = build_program()
    nc = _CACHE["nc"]
    res = run_bass_kernel_spmd(nc, maps, core_ids=list(range(8)))
    out = np.empty((2, SEQ, D), np.float32)
    for r in range(8):
        b, q = r // 4, r % 4
        out[b, q * 1024:(q + 1) * 1024, :] = res.results[r]["outT"].T
    return out
```

```python
import numpy as np
import concourse.bass as bass
import concourse.mybir as mybir
from concourse.bass_utils import run_bass_kernel_spmd

F32 = mybir.dt.float32
BF16 = mybir.dt.bfloat16
AF = mybir.ActivationFunctionType
ALU = mybir.AluOpType
AX = mybir.AxisListType

D = 2048
KC = 16
SEQ = 4096
CTX = 256
T = SEQ + CTX
DEPTH = 4
DFF = 5632
NJ = DFF // 128
EPS = 1e-6
TILES = [(i * 512, 512) for i in range(8)] + [(4096, 256)]
GROUPS = [[0, 1], [2, 3], [4, 5], [6, 7], [8]]
SB_BASE = 20480
SB_END = 229376


class Buf:
    __slots__ = ("name", "w", "r")

    def __init__(self, name=""):
        self.name = name
        self.w = None
        self.r = {}


class Sem:
    def __init__(self, handle, name):
        self.h = handle
        self.name = name


class Eng:
    def __init__(self, name, sem):
        self.name = name
        self.sem = sem
        self.cnt = 0
        self.seen = {}
        self.prog = []


class Chan:
    def __init__(self, sem):
        self.sem = sem
        self.n = 0


class Layout:
    def __init__(self, Hd, tiles, segs):
        self.Hd = Hd
        self.tiles = tiles
        self.T = tiles[-1][0] + tiles[-1][1]
        self.ntiles = len(tiles)
        self.groups = [list(range(i, min(i + 2, len(tiles)))) for i in range(0, len(tiles), 2)]
        self.segs = segs
        self.nt = []
        for (t0, w, col) in tiles:
            o = 0
            while o < w:
                ww = min(256, w - o)
                self.nt.append((t0 + o, ww, col))
                o += ww
        self.zoff = []
        zb = 1
        segbase = []
        for (s0, n) in segs:
            segbase.append((s0, n, zb))
            zb += n + 2
        self.zw = (zb + 7) // 8 * 8
        for (t0, w, col) in tiles:
            for (s0, n, b) in segbase:
                if s0 <= t0 < s0 + n:
                    self.zoff.append(b + t0 - s0)
        self.HdB = {}

    def hb(self, c, t):
        if (c, t) not in self.HdB:
            self.HdB[(c, t)] = Buf(f"Hd{c}_{t}")
        return self.HdB[(c, t)]


class K:
    def __init__(self, nc):
        self.nc = nc
        self.sems = []
        self.engs = {}
        self.chans = []
        self.sb_ptr = SB_BASE
        self.uid = 0

    def new_sem(self, name):
        s = Sem(self.nc.alloc_semaphore(name) if hasattr(self.nc, "alloc_semaphore") else None, name)
        self.sems.append(s)
        return s

    def sb(self, shape, dt, name=None):
        self.uid += 1
        nbytes = int(np.prod(shape[1:])) * (4 if dt == F32 else 2)
        off = (self.sb_ptr + 31) // 32 * 32
        assert off + nbytes <= SB_END, ("sbuf overflow", name, off, nbytes)
        self.sb_ptr = off + nbytes
        return self.nc.alloc_sbuf_tensor_at(f"{name or 't'}{self.uid}", list(shape), dt, offset=off)

    def sb_mark(self):
        return self.sb_ptr

    def sb_reset(self, mark):
        self.sb_ptr = mark

    def _need(self, eng, evs):
        waits = []
        for sem, val in evs:
            if val <= 0:
                continue
            if eng.seen.get(sem, 0) < val:
                eng.seen[sem] = val
                waits.append((sem, val))
        return waits

    def _deps(self, reads, writes):
        evs = []
        for b in reads:
            if b.w is not None:
                evs.append(b.w)
        for b in writes:
            if b.w is not None:
                evs.append(b.w)
            for s, v in b.r.items():
                evs.append((s, v))
        best = {}
        for s, v in evs:
            if best.get(s, 0) < v:
                best[s] = v
        return list(best.items())

    def _mark(self, ev, reads, writes):
        s, v = ev
        for b in reads:
            if b.r.get(s, 0) < v:
                b.r[s] = v
        for b in writes:
            b.w = ev
            b.r = {}

    def op(self, eng, fns, reads=(), writes=()):
        if not isinstance(fns, (list, tuple)):
            fns = [fns]
        waits = self._need(eng, self._deps(reads, writes))
        eng.cnt += 1
        ev = (eng.sem, eng.cnt)
        self._mark(ev, reads, writes)
        eng.prog.append(("op", waits, list(fns), eng.sem))
        return ev

    def dma(self, eng, chan, out, in_, reads=(), writes=(), **kw):
        evs = self._deps(reads, writes)
        evs.append((chan.sem, 16 * chan.n))
        waits = self._need(eng, evs)
        chan.n += 1
        ev = (chan.sem, 16 * chan.n)
        self._mark(ev, reads, writes)
        eng.prog.append(("dma", waits, (out, in_, kw), chan.sem))
        return ev

    def barrier(self):
        evs = [(e.sem, e.cnt) for e in self.engs.values()] + [(c.sem, 16 * c.n) for c in self.chans]
        for e in self.engs.values():
            waits = self._need(e, [x for x in evs if x[0] is not e.sem or True])
            if waits:
                e.prog.append(("wait", waits, None, None))

    def chan(self, name):
        c = Chan(self.new_sem(name))
        self.chans.append(c)
        return c


def build_program(stop_after=None, n_layers=DEPTH, force_q=None):
    nc = bass.Bass("TRN2", target_bir_lowering=False)
    k = K(nc)

    def din(name, shape):
        return nc.dram_tensor(name, list(shape), F32, kind="ExternalInput").ap()

    xT = din("xT", [D, T])
    c_fm = din("c_fm", [128, KC, 2])
    ada_wb = din("ada_wb", [DEPTH, 24, 128, KC, 512])
    ada_b_fm = din("ada_b_fm", [128, DEPTH, 96])
    norm_g_fm = din("norm_g_fm", [128, DEPTH, 2, KC])
    final_g_fm = din("final_g_fm", [128, KC])
    ffn_up_b = din("ffn_up_b", [DEPTH, 2 * NJ, 128, KC, 128])
    ffn_cw_fm = din("ffn_cw_fm", [128, DEPTH, 3, 2 * NJ])
    ffn_cb_fm = din("ffn_cb_fm", [128, DEPTH, 2 * NJ])
    ffn_dn_b = din("ffn_dn_b", [DEPTH, KC, 128, NJ, 128])
    a_in_b = din("a_in_b", [2, 64, 128, KC, 128])
    a_gv_fm = din("a_gv_fm", [128, 2, 32])
    a_wsT = din("a_wsT", [2, 128, 16, 128])
    a_bs_bc = din("a_bs_bc", [2, 128, 32, 128])
    a_gv_bc = din("a_gv_bc", [2, 128, 4096])
    a_out_b = din("a_out_b", [2, KC, 128, 32, 128])
    b_qkv_b = din("b_qkv_b", [48, 128, KC, 128])
    b_bias = din("b_bias", [16, 128, 6, 1024])
    b_out_b = din("b_out_b", [KC, 128, KC, 128])
    c_in_b = din("c_in_b", [32, 128, KC, 128])
    c_cw_fm = din("c_cw_fm", [128, 4, KC])
    c_cb_fm = din("c_cb_fm", [128, KC])
    c_wg = din("c_wg", [128, 2, 2, KC, 128])
    c_bg_fm = din("c_bg_fm", [128, 2, 2, KC])
    c_lam_fm = din("c_lam_fm", [128, 2, KC])
    c_out_b = din("c_out_b", [KC, 128, KC, 128])
    ident_in = din("ident_in", [128, 128])
    emask_in = din("emask_in", [128, 2])
    outT = nc.dram_tensor("outT", [D, 1024], F32, kind="ExternalOutput").ap()

    Hd_full = nc.dram_tensor("Hd", [D, T + 1], F32).ap()
    Hd = Hd_full[:, 1:T + 1]
    S1 = nc.dram_tensor("S1", [DFF, T], BF16).ap()
    S2 = nc.dram_tensor("S2", [4096, T], BF16).ap()
    S3 = nc.dram_tensor("S3", [D, T], BF16).ap()
    S4 = nc.dram_tensor("S4", [D, T], BF16).ap()
    X4_full = nc.dram_tensor("X4", [D, T + 1], F32).ap()
    X4 = X4_full[:, 1:T + 1]
    Hw = nc.dram_tensor("Hw", [D, 1282], F32).ap()
    HSw = nc.dram_tensor("HSw", [D, 1282], F32).ap()
    OutW = nc.dram_tensor("OutW", [D, 1280], F32).ap()

    sem_handles = {}

    def mk_sem(name):
        cm = nc.semaphore(name)
        h = cm.__enter__()
        sem_handles[name] = (cm, h)
        return Sem(h, name)

    k.new_sem = mk_sem
    PE = Eng("pe", mk_sem("s_pe"))
    ACT = Eng("act", mk_sem("s_act"))
    DVE = Eng("dve", mk_sem("s_dve"))
    POOL = Eng("pool", mk_sem("s_pool"))
    SP = Eng("sp", mk_sem("s_sp"))
    k.engs = {"pe": PE, "act": ACT, "dve": DVE, "pool": POOL, "sp": SP}

    psum = nc.alloc_psum_tensor("psum", [128, 8, 512], F32)
    PSB = [Buf(f"ps{i}") for i in range(8)]

    ones_bf = k.sb([128, 128], BF16, "ones")
    ident_bf = k.sb([128, 128], BF16, "ident")
    ident_f = k.sb([128, 128], F32, "identf")
    eps_t = k.sb([128, 1], F32, "eps")
    one_t = k.sb([128, 1], F32, "one")
    Mall = k.sb([128, DEPTH, 96, 2], F32, "Mall")
    ng = k.sb([128, DEPTH, 2, KC], F32, "ng")
    fg = k.sb([128, KC], F32, "fg")
    emask = k.sb([128, 2], F32, "emask")
    GS = k.sb([128, 2, 2, KC], F32, "GS")
    B_const = Buf("const")
    B_M = Buf("M")
    B_GS = Buf("GS")
    ch_misc = k.chan("c_misc")

    def wait_all_fn(waits):
        pass

    k.op(POOL, lambda e: e.memset(ones_bf[:], 1.0), writes=[B_const])
    k.dma(SP, ch_misc, ident_f[:], ident_in[:, :], writes=[B_const])
    k.op(POOL, lambda e: e.tensor_copy(out=ident_bf[:], in_=ident_f[:]), reads=[B_const], writes=[B_const])
    k.op(POOL, lambda e: e.memset(eps_t[:], EPS), writes=[B_const])
    k.op(POOL, lambda e: e.memset(one_t[:], 1.0), writes=[B_const])
    k.dma(SP, ch_misc, ng[:], norm_g_fm[:, :, :, :], writes=[B_const])
    k.dma(SP, ch_misc, fg[:], final_g_fm[:, :], writes=[B_const])
    k.dma(SP, ch_misc, emask[:], emask_in[:, :], writes=[B_const])

    persist_mark = k.sb_mark()

    chan_pool = {}

    def get_chan(name):
        if name not in chan_pool:
            chan_pool[name] = k.chan(name)
        return chan_pool[name]

    class WStream:
        def __init__(self, name, shape, nslots):
            self.t = [k.sb(shape, BF16, name) for _ in range(nslots)]
            self.b = [Buf(f"{name}{i}") for i in range(nslots)]
            self.c = [get_chan(f"c_{name}{i}") for i in range(nslots)]
            self.i = 0

        def load(self, src):
            s = self.i % len(self.t)
            self.i += 1
            k.dma(POOL, self.c[s], self.t[s][:], src, writes=[self.b[s]], max_dma_last_dim=4096)
            return self.t[s], self.b[s]

    class Slots:
        def __init__(self, name, shape, dt, n, chan=True):
            self.t = [k.sb(shape, dt, name) for _ in range(n)]
            self.b = [Buf(f"{name}{i}") for i in range(n)]
            self.c = [get_chan(f"c_{name}{i}") for i in range(n)] if chan else None
            self.i = 0

        def next(self):
            s = self.i % len(self.t)
            self.i += 1
            return self.t[s], self.b[s], (self.c[s] if self.c else None)

    ps_rr = [0]

    def next_ps(lo=0, hi=8):
        i = lo + ps_rr[0] % (hi - lo)
        ps_rr[0] += 1
        return i

    def mm_group(out_ap, pairs, reads, writes):
        n = len(pairs)
        fns = []
        for i, (l, r) in enumerate(pairs):
            fns.append(lambda e, l=l, r=r, i=i: e.matmul(out_ap, lhsT=l, rhs=r, start=(i == 0), stop=(i == n - 1)))
        return k.op(PE, fns, reads=reads, writes=writes)

    ch_cp = [k.chan(f"c_cp{i}") for i in range(4)]
    Lm = Layout(Hd, [(i * 512, 512, 0) for i in range(8)] + [(4096, 256, 1)], [(0, SEQ), (SEQ, CTX)])
    LW2 = Layout(Hw[:, 0:1282], [(0, 512, 0), (512, 512, 0), (1024, 258, 0)], [(0, 1282)])
    LW3 = Layout(Hw[:, 1:1281], [(0, 512, 0), (512, 512, 0), (1024, 256, 0)], [(0, 1280)])

    for t, (t0, w, col) in enumerate(Lm.tiles):
        k.dma(SP, ch_cp[t % 4], Hd[:, t0:t0 + w], xT[:, t0:t0 + w])
    zpad = k.sb([128, KC], F32, "zpad")
    Bz = Buf("zpad")
    k.op(POOL, lambda e: e.memset(zpad[:], 0.0), writes=[Bz])
    for dstf in (Hd_full, X4_full):
        k.dma(SP, ch_misc, dstf[:, 0:1].rearrange("(c p) o -> p (c o)", p=128), zpad[:], reads=[Bz],
              allow_slow_non_contiguous=True)
    k.barrier()

    def stage_ada():
        mark = k.sb_mark()
        cf = k.sb([128, KC, 2], F32, "cf")
        sc = k.sb([128, KC, 2], BF16, "sc")
        sg = k.sb([128, KC, 2], F32, "sg")
        adab = k.sb([128, DEPTH, 96], F32, "adab")
        Bc = Buf("cf")
        k.dma(SP, ch_misc, cf[:], c_fm[:, :, :], writes=[Bc])
        k.dma(SP, ch_misc, adab[:], ada_b_fm[:, :, :], writes=[Bc])
        k.op(ACT, lambda e: e.activation(out=sg[:], in_=cf[:], func=AF.Sigmoid), reads=[Bc], writes=[Bc])
        k.op(DVE, lambda e: e.tensor_tensor(out=sc[:], in0=cf[:], in1=sg[:], op=ALU.mult), reads=[Bc], writes=[Bc])
        ws = WStream("adaw", [128, KC, 512], 3)
        for li in range(n_layers):
            for j0 in range(0, 96, 16):
                pb = next_ps()
                for j4 in range(j0, j0 + 16, 4):
                    wt, wb = ws.load(ada_wb[li, j4 // 4])
                    for jj in range(4):
                        j = j4 + jj
                        mm_group(psum[:, pb, (j - j0) * 2:(j - j0) * 2 + 2],
                                 [(wt[:, kk, jj * 128:(jj + 1) * 128], sc[:, kk, :]) for kk in range(KC)],
                                 reads=[wb, Bc], writes=[PSB[pb]])
                k.op(DVE, lambda e, pb=pb, li=li, j0=j0: e.tensor_tensor(
                    out=Mall[:, li, j0:j0 + 16, :],
                    in0=psum[:, pb, 0:32].rearrange("p (j two) -> p j two", two=2),
                    in1=adab[:, li, j0:j0 + 16].unsqueeze(2).to_broadcast([128, 16, 2]),
                    op=ALU.add), reads=[PSB[pb], Bc], writes=[B_M])
        k.barrier()
        k.sb_reset(mark)

    def prep_gs(li, which):
        sh, scl = (0, 1) if which == 0 else (3, 4)
        for col in range(2):
            k.op(DVE, lambda e, col=col: e.scalar_tensor_tensor(
                out=GS[:, 0, col, :], in0=Mall[:, li, scl * 16:(scl + 1) * 16, col], scalar=1.0,
                in1=ng[:, li, which, :], op0=ALU.add, op1=ALU.mult), reads=[B_M, B_const], writes=[B_GS])
            k.op(DVE, lambda e, col=col: e.tensor_copy(out=GS[:, 1, col, :], in_=Mall[:, li, sh * 16:(sh + 1) * 16, col]),
                 reads=[B_M], writes=[B_GS])

    def stage_norm(L, Narena, NB, li, which, final_dst=None, edge_mask=False):
        mark = k.sb_mark()
        final = final_dst is not None
        if not final:
            prep_gs(li, which)
        Ht = Slots("Ht", [128, KC, 256], F32, 3)
        sq = Slots("sq", [128, 4, 256], BF16, 2, chan=False)
        std = Slots("std", [128, 256], F32, 2, chan=False)
        ost = Slots("nout", [128, KC, 256], F32, 1) if final else None
        for ti, (t0, w, col) in enumerate(L.nt):
            ht, hbuf, hch = Ht.next()
            k.dma(SP, hch, ht[:, :, 0:w], L.Hd[:, t0:t0 + w].rearrange("(c p) t -> p c t", p=128), writes=[hbuf])
            pb = next_ps()
            for c4 in range(4):
                st, sbuf_, _ = sq.next()
                k.op(ACT, lambda e, st=st, ht=ht, c4=c4, w=w: e.activation(out=st[:, :, 0:w], in_=ht[:, c4 * 4:(c4 + 1) * 4, 0:w], func=AF.Square),
                     reads=[hbuf], writes=[sbuf_])
                fns = []
                for cc in range(4):
                    first = (c4 == 0 and cc == 0)
                    last = (c4 == 3 and cc == 3)
                    fns.append(lambda e, st=st, cc=cc, first=first, last=last, pb=pb, w=w: e.matmul(
                        psum[:, pb, 0:w], lhsT=ones_bf[:], rhs=st[:, cc, 0:w], start=first, stop=last))
                k.op(PE, fns, reads=[sbuf_, B_const], writes=[PSB[pb]])
            sd, sdb, _ = std.next()
            k.op(ACT, lambda e, sd=sd, pb=pb, w=w: e.activation(out=sd[:, 0:w], in_=psum[:, pb, 0:w], func=AF.Sqrt,
                                                               scale=1.0 / D, bias=eps_t[:]),
                 reads=[PSB[pb], B_const], writes=[sdb])
            k.op(DVE, lambda e, sd=sd, w=w: e.reciprocal(out=sd[:, 0:w], in_=sd[:, 0:w]), reads=[sdb], writes=[sdb])
            if final:
                ot, ob, och = ost.next()
            k.op(DVE, lambda e, ht=ht, sd=sd, w=w: e.tensor_tensor(
                out=ht[:, :, 0:w], in0=ht[:, :, 0:w], in1=sd[:, 0:w].unsqueeze(1).to_broadcast([128, KC, w]), op=ALU.mult),
                 reads=[hbuf, sdb], writes=[hbuf])
            for c in range(KC):
                if final:
                    k.op(ACT, lambda e, ht=ht, c=c, ot=ot, w=w: e.activation(out=ot[:, c, 0:w], in_=ht[:, c, 0:w], func=AF.Copy, scale=fg[:, c:c + 1]),
                         reads=[hbuf, B_const], writes=[ob])
                else:
                    k.op(ACT, lambda e, ht=ht, c=c, col=col, t0=t0, w=w: e.activation(
                        out=Narena[:, c, t0:t0 + w], in_=ht[:, c, 0:w], func=AF.Identity,
                        scale=GS[:, 0, col, c:c + 1], bias=GS[:, 1, col, c:c + 1]),
                         reads=[hbuf, B_GS], writes=[NB[t0 // 512]])
            if final:
                k.dma(SP, och, final_dst[:, t0:t0 + w].rearrange("(c p) t -> p c t", p=128), ot[:, :, 0:w], reads=[ob])
        if edge_mask:
            for cidx, mi in ((0, 0), (L.T - 1, 1)):
                k.op(DVE, lambda e, cidx=cidx, mi=mi: e.tensor_scalar(
                    out=Narena[:, :, cidx], in0=Narena[:, :, cidx], scalar1=emask[:, mi:mi + 1], scalar2=None, op0=ALU.mult),
                     reads=[NB[cidx // 512], B_const], writes=[NB[cidx // 512]])
        k.barrier()
        k.sb_reset(mark)

    def proj1(L, Narena, NB, ws, wsrcs, evac, ps_lo=0, ps_hi=6):
        nblk = len(wsrcs)
        pre = max(1, len(ws.t) - 1)
        loaded = {}
        for bi in range(min(pre, nblk)):
            loaded[bi] = ws.load(wsrcs[bi])
        for bi in range(nblk):
            if bi + pre < nblk:
                loaded[bi + pre] = ws.load(wsrcs[bi + pre])
            wt, wb = loaded.pop(bi)
            for t in range(L.ntiles):
                t0, w, col = L.tiles[t]
                pb = next_ps(ps_lo, ps_hi)
                mm_group(psum[:, pb, 0:w], [(wt[:, kk, :], Narena[:, kk, t0:t0 + w]) for kk in range(KC)],
                         reads=[wb, NB[t]], writes=[PSB[pb]])
                evac(bi, t, pb)

    def hupdate(L, Hc, stc, sti, c, t, pb, li, gate_which):
        t0, w, col = L.tiles[t]
        ht, hbuf, hch = Hc.next()
        k.dma(SP, hch, ht[:, 0:w], L.Hd[c * 128:(c + 1) * 128, t0:t0 + w], reads=[L.hb(c, t)], writes=[hbuf])
        k.op(DVE, lambda e: e.scalar_tensor_tensor(
            out=ht[:, 0:w], in0=psum[:, pb, 0:w], scalar=Mall[:, li, gate_which * 16 + c, col:col + 1],
            in1=ht[:, 0:w], op0=ALU.mult, op1=ALU.add), reads=[PSB[pb], hbuf, B_M], writes=[hbuf])
        k.dma(SP, stc[sti % 4], L.Hd[c * 128:(c + 1) * 128, t0:t0 + w], ht[:, 0:w], reads=[hbuf], writes=[L.hb(c, t)])

    def proj2(L, src, KB, wsrc_c, li, gate_which, resident=False):
        mark = k.sb_mark()
        Ag = [k.sb([128, KB, 512], BF16, "Ag") for _ in range(2)]
        AgB = [Buf("Ag0"), Buf("Ag1")]
        AgC = [get_chan("c_Ag0"), get_chan("c_Ag1")]
        Hc = Slots("Hc", [128, 512], F32, 4)
        stc = [get_chan(f"c_hst{i}") for i in range(4)]
        if resident:
            wres = k.sb([128, KC, KB, 128], BF16, "wres")
            wresB = Buf("wres")
            chw = get_chan("c_wres")
            for c in range(KC):
                k.dma(POOL, chw, wres[:, c], wsrc_c(c), writes=[wresB], max_dma_last_dim=4096)
        else:
            ws = WStream("w2_", [128, KB, 128], 6)
        sti = 0
        for g in L.groups:
            for ti, t in enumerate(g):
                t0, w, col = L.tiles[t]
                k.dma(SP, AgC[ti], Ag[ti][:, :, 0:w], src[:, t0:t0 + w].rearrange("(j p) t -> p j t", p=128),
                      writes=[AgB[ti]])
            for c in range(KC):
                if resident:
                    wt, wb = wres[:, c], wresB
                else:
                    wt, wb = ws.load(wsrc_c(c))
                for ti, t in enumerate(g):
                    t0, w, col = L.tiles[t]
                    pb = next_ps()
                    mm_group(psum[:, pb, 0:w], [(wt[:, j, :], Ag[ti][:, j, 0:w]) for j in range(KB)],
                             reads=[wb, AgB[ti]], writes=[PSB[pb]])
                    hupdate(L, Hc, stc, sti, c, t, pb, li, gate_which)
                    sti += 1
        k.barrier()
        k.sb_reset(mark)

    def stage_ffn(L, Narena, NB, li):
        mark = k.sb_mark()
        ws = WStream("wup", [128, KC, 128], 3)
        ZW = L.zw
        zg = [k.sb([128, ZW], BF16, "zg") for _ in range(2)]
        zv = [k.sb([128, ZW], BF16, "zv") for _ in range(2)]
        zgB = [Buf("zg0"), Buf("zg1")]
        zvB = [Buf("zv0"), Buf("zv1")]
        cw = k.sb([128, 3, 2 * NJ], F32, "cw")
        cb = k.sb([128, 2 * NJ], F32, "cb")
        Bcw = Buf("cw")
        k.dma(SP, ch_misc, cw[:], ffn_cw_fm[:, li, :, :], writes=[Bcw])
        k.dma(SP, ch_misc, cb[:], ffn_cb_fm[:, li, :], writes=[Bcw])
        for s in range(2):
            k.op(POOL, lambda e, s=s: e.memset(zg[s][:], 0.0), writes=[zgB[s]])
            k.op(POOL, lambda e, s=s: e.memset(zv[s][:], 0.0), writes=[zvB[s]])
        yg = Slots("yg", [128, 512], F32, 2, chan=False)
        yv = Slots("yv", [128, 512], F32, 2, chan=False)
        sgs = Slots("sgs", [128, 512], F32, 2, chan=False)
        ao = Slots("ao", [128, 512], BF16, 3)

        for j in range(NJ):
            s = j % 2
            for half, (zt, zb) in enumerate(((zg[s], zgB[s]), (zv[s], zvB[s]))):
                blk = j + half * NJ
                wt, wb = ws.load(ffn_up_b[li, blk])
                for t in range(L.ntiles):
                    t0, w, col = L.tiles[t]
                    pb = next_ps()
                    mm_group(psum[:, pb, 0:w], [(wt[:, kk, :], Narena[:, kk, t0:t0 + w]) for kk in range(KC)],
                             reads=[wb, NB[t]], writes=[PSB[pb]])
                    zo = L.zoff[t]
                    k.op(ACT, lambda e, zt=zt, pb=pb, zo=zo, w=w: e.activation(out=zt[:, zo:zo + w], in_=psum[:, pb, 0:w], func=AF.Copy),
                         reads=[PSB[pb]], writes=[zb])
            for t in range(L.ntiles):
                t0, w, col = L.tiles[t]
                zo = L.zoff[t]
                res = []
                for half, (zt, zb, ys) in enumerate(((zg[s], zgB[s], yg), (zv[s], zvB[s], yv))):
                    blk = j + half * NJ
                    yt, yb, _ = ys.next()
                    k.op(ACT, lambda e, yt=yt, zt=zt, zo=zo, w=w, blk=blk: e.activation(
                        out=yt[:, 0:w], in_=zt[:, zo:zo + w], func=AF.Identity,
                        scale=cw[:, 1, blk:blk + 1], bias=cb[:, blk:blk + 1]), reads=[zb, Bcw], writes=[yb])
                    k.op(DVE, lambda e, yt=yt, zt=zt, zo=zo, w=w, blk=blk: e.scalar_tensor_tensor(
                        out=yt[:, 0:w], in0=zt[:, zo - 1:zo - 1 + w], scalar=cw[:, 0, blk:blk + 1], in1=yt[:, 0:w],
                        op0=ALU.mult, op1=ALU.add), reads=[zb, Bcw, yb], writes=[yb])
                    k.op(DVE, lambda e, yt=yt, zt=zt, zo=zo, w=w, blk=blk: e.scalar_tensor_tensor(
                        out=yt[:, 0:w], in0=zt[:, zo + 1:zo + 1 + w], scalar=cw[:, 2, blk:blk + 1], in1=yt[:, 0:w],
                        op0=ALU.mult, op1=ALU.add), reads=[zb, Bcw, yb], writes=[yb])
                    res.append((yt, yb))
                (ygt, ygb), (yvt, yvb) = res
                st, sb_, _ = sgs.next()
                k.op(ACT, lambda e, st=st, ygt=ygt, w=w: e.activation(out=st[:, 0:w], in_=ygt[:, 0:w], func=AF.Silu),
                     reads=[ygb], writes=[sb_])
                at, ab, ach = ao.next()
                k.op(DVE, lambda e, at=at, st=st, yvt=yvt, w=w: e.tensor_tensor(out=at[:, 0:w], in0=st[:, 0:w], in1=yvt[:, 0:w], op=ALU.mult),
                     reads=[sb_, yvb], writes=[ab])
                k.dma(SP, ach, S1[j * 128:(j + 1) * 128, t0:t0 + w], at[:, 0:w], reads=[ab])
        k.barrier()
        k.sb_reset(mark)

    def stage_mixA(L, Narena, NB, li, ja, arena_mark):
        nchunk = L.T // 128
        mark = k.sb_mark()
        ws = WStream("wain", [128, KC, 128], 4)
        acc = k.sb([128, L.T], F32, "acc")
        accB = [Buf(f"acc{t}") for t in range(L.ntiles)]
        uo = Slots("uo", [128, 512], BF16, 4)
        sqv = Slots("sqv", [128, 512], F32, 6, chan=False)
        for t in range(L.ntiles):
            t0, w, col = L.tiles[t]
            k.op(POOL, lambda e, t0=t0, w=w: e.memset(acc[:, t0:t0 + w], 0.0), writes=[accB[t]])

        def evac(bi, t, pb):
            t0, w, col = L.tiles[t]
            ot, ob, och = uo.next()
            k.op(ACT, lambda e: e.activation(out=ot[:, 0:w], in_=psum[:, pb, 0:w], func=AF.Gelu_apprx_tanh),
                 reads=[PSB[pb]], writes=[ob])
            if bi < 32:
                k.dma(SP, och, S1[bi * 128:(bi + 1) * 128, t0:t0 + w], ot[:, 0:w], reads=[ob])
            else:
                vb = bi - 32
                k.dma(SP, och, S2[vb * 128:(vb + 1) * 128, t0:t0 + w], ot[:, 0:w], reads=[ob])
                st, sb_, _ = sqv.next()
                k.op(ACT, lambda e: e.activation(out=st[:, 0:w], in_=ot[:, 0:w], func=AF.Square), reads=[ob], writes=[sb_])
                k.op(POOL, lambda e: e.tensor_tensor(out=acc[:, t0:t0 + w], in0=acc[:, t0:t0 + w], in1=st[:, 0:w], op=ALU.add),
                     reads=[sb_, accB[t]], writes=[accB[t]])

        proj1(L, Narena, NB, ws, [a_in_b[ja, bi] for bi in range(64)], evac)
        rs = k.sb([128, 40], F32, "rs")
        rsB = Buf("rs")
        pb = next_ps()
        for n in range(nchunk):
            k.op(PE, lambda e, n=n, pb=pb: e.matmul(psum[:, pb, n:n + 1], lhsT=acc[:, n * 128:(n + 1) * 128], rhs=one_t[:],
                                                   start=True, stop=True),
                 reads=[accB[n // 4], B_const], writes=[PSB[pb]])
        k.op(ACT, lambda e: e.activation(out=rs[:, 0:nchunk], in_=psum[:, pb, 0:nchunk], func=AF.Sqrt, scale=1.0 / 4096, bias=eps_t[:]),
             reads=[PSB[pb], B_const], writes=[rsB])
        k.op(DVE, lambda e: e.reciprocal(out=rs[:, 0:nchunk], in_=rs[:, 0:nchunk]), reads=[rsB], writes=[rsB])
        k.barrier()
        k.sb_reset(arena_mark)
        rs2 = k.sb([128, 40], F32, "rs2")
        assert k.sb_mark() <= mark
        k.op(DVE, lambda e: e.tensor_copy(out=rs2[:, 0:nchunk], in_=rs[:, 0:nchunk]), reads=[rsB], writes=[rsB])
        k.barrier()
        wsT = k.sb([128, 16, 128], BF16, "wsT")
        bsb = k.sb([128, 32, 128], F32, "bsb")
        gvrow = k.sb([128, 4096], BF16, "gvrow")
        Bw = Buf("wsT")
        chA = get_chan("c_wsT")
        k.dma(POOL, chA, wsT[:], a_wsT[ja], writes=[Bw])
        k.dma(SP, ch_misc, bsb[:], a_bs_bc[ja], writes=[Bw])
        k.dma(POOL, chA, gvrow[:], a_gv_bc[ja], writes=[Bw], max_dma_last_dim=4096)
        Ut = Slots("Ut", [128, 32, 256], BF16, 2)
        Vt = Slots("Vt", [128, 32, 256], BF16, 2)
        US = [k.sb([128, 32, 512], BF16, "US") for _ in range(2)]
        USB = [Buf("US0"), Buf("US1")]
        vtm = Slots("vtm", [128, 512], BF16, 3, chan=False)
        stmp = Slots("stmp", [128, 512], F32, 2, chan=False)
        ws2 = WStream("waout", [128, 32, 128], 3)
        Hc = Slots("HcA", [128, 512], F32, 4)
        stc = [get_chan(f"c_hst{i}") for i in range(4)]
        sti = 0
        def sp_front(vt, vb, cnl, n, f4):
            pt = next_ps()
            ptile = psum[:, pt, :].bitcast(BF16)
            fns = []
            for ff in range(4):
                fb = f4 * 4 + ff
                fns.append(lambda e, ff=ff, fb=fb: e.transpose(
                    ptile[:, ff * 128:(ff + 1) * 128], vt[:, fb, cnl * 128:(cnl + 1) * 128], ident_bf[:]))
            k.op(PE, fns, reads=[vb, B_const], writes=[PSB[pt]])
            lt, lb, _ = vtm.next()
            k.op(DVE, lambda e: e.scalar_tensor_tensor(
                out=lt[:], in0=ptile[:, 0:512], scalar=rs2[:, n:n + 1], in1=gvrow[:, f4 * 512:(f4 + 1) * 512],
                op0=ALU.mult, op1=ALU.mult), reads=[PSB[pt], rsB, Bw], writes=[lb])
            return lt, lb

        def sp_back(lt, lb, ut, ub, cnl, ti, cn, f4):
            ps2 = next_ps()
            fns = []
            for ff in range(4):
                fb = f4 * 4 + ff
                fns.append(lambda e, ff=ff, fb=fb: e.matmul(
                    psum[:, ps2, ff * 128:(ff + 1) * 128], lhsT=lt[:, ff * 128:(ff + 1) * 128], rhs=wsT[:, fb // 2, :],
                    start=True, stop=True))
            k.op(PE, fns, reads=[lb, Bw], writes=[PSB[ps2]])
            tt, tb, _ = stmp.next()
            k.op(DVE, lambda e: e.tensor_tensor(
                out=tt[:], in0=psum[:, ps2, 0:512],
                in1=bsb[:, f4 * 4:(f4 + 1) * 4, :].rearrange("p a b -> p (a b)"), op=ALU.add),
                 reads=[PSB[ps2], Bw], writes=[tb])
            k.op(POOL, lambda e: e.tensor_tensor(
                out=US[ti][:, f4 * 4:(f4 + 1) * 4, cn * 128:(cn + 1) * 128],
                in0=tt[:].rearrange("p (a b) -> p a b", b=128),
                in1=ut[:, f4 * 4:(f4 + 1) * 4, cnl * 128:(cnl + 1) * 128], op=ALU.mult),
                 reads=[tb, ub], writes=[USB[ti]])

        for g in L.groups:
            pend = None
            for ti, t in enumerate(g):
                t0, w, col = L.tiles[t]
                for h0 in range(0, w, 256):
                    hw = min(256, w - h0)
                    ut, ub, uch = Ut.next()
                    vt, vb, vch = Vt.next()
                    k.dma(SP, vch, vt[:, :, 0:hw], S2[0:4096, t0 + h0:t0 + h0 + hw].rearrange("(j p) t -> p j t", p=128), writes=[vb])
                    k.dma(SP, uch, ut[:, :, 0:hw], S1[0:4096, t0 + h0:t0 + h0 + hw].rearrange("(j p) t -> p j t", p=128), writes=[ub])
                    for cnl in range(hw // 128):
                        cn = h0 // 128 + cnl
                        n = t0 // 128 + cn
                        for f4 in range(8):
                            lt, lb = sp_front(vt, vb, cnl, n, f4)
                            if pend is not None:
                                sp_back(*pend)
                            pend = (lt, lb, ut, ub, cnl, ti, cn, f4)
            if pend is not None:
                sp_back(*pend)
            for c in range(KC):
                wt, wb = ws2.load(a_out_b[ja, c])
                for ti, t in enumerate(g):
                    t0, w, col = L.tiles[t]
                    pb = next_ps()
                    mm_group(psum[:, pb, 0:w], [(wt[:, j, :], US[ti][:, j, 0:w]) for j in range(32)],
                             reads=[wb, USB[ti]], writes=[PSB[pb]])
                    hupdate(L, Hc, stc, sti, c, t, pb, li, 2)
                    sti += 1
        k.barrier()
        k.sb_reset(mark)

    def stage_mixC_x(Narena, NB, arena_mark):
        mark = k.sb_mark()
        ws = WStream("wcin", [128, KC, 128], 4)
        xo = Slots("xo", [128, 512], F32, 3)

        def evac(bi, t, pb):
            t0, w, col = Lm.tiles[t]
            ot, ob, och = xo.next()
            k.op(DVE, lambda e: e.tensor_copy(out=ot[:, 0:w], in_=psum[:, pb, 0:w]), reads=[PSB[pb]], writes=[ob])
            k.dma(SP, och, X4[bi * 128:(bi + 1) * 128, t0:t0 + w], ot[:, 0:w], reads=[ob])

        proj1(Lm, Narena, NB, ws, [c_in_b[16 + bi] for bi in range(16)], evac)
        k.barrier()
        k.sb_reset(arena_mark)
        ZW = 4360
        LOFF, COFF = 2, 4101
        xz = Slots("xz", [128, ZW], F32, 1)
        xz2c = [get_chan("c_xzb0"), get_chan("c_xzb1")]
        xcs = [k.sb([128, T], F32, "xc") for _ in range(2)]; xcBs = [Buf("xc0"), Buf("xc1")]
        xcbs = [k.sb([128, T], BF16, "xcb") for _ in range(2)]; xcbBs = [Buf("xcb0"), Buf("xcb1")]
        Rbs = [k.sb([128, T], F32, "R") for _ in range(2)]; RBts = [[Buf(f"R{d}_{t}") for t in range(9)] for d in range(2)]
        Ibs = [k.sb([128, T], F32, "I") for _ in range(2)]; IBts = [[Buf(f"I{d}_{t}") for t in range(9)] for d in range(2)]
        Tb = k.sb([128, T], F32, "Tm"); TBt = [Buf(f"Tm{t}") for t in range(9)]
        HF = k.sb([128, T], F32, "HF"); HFB = Buf("HF")
        HR = k.sb([128, T], F32, "HR"); HRB = Buf("HR")
        hst = [get_chan("c_hs0"), get_chan("c_hs1")]
        X4B = [Buf(f"X4_{h}") for h in range(KC)]
        wg = WStream("wg", [128, 2, 2, 128], 2)
        ccw = k.sb([128, 4, KC], F32, "ccw")
        ccb = k.sb([128, KC], F32, "ccb")
        bgt = k.sb([128, 2, 2, KC], F32, "bgt")
        lam = k.sb([128, 2, KC], F32, "lam")
        cneg = k.sb([128, 2, KC], F32, "cneg")
        Bp = Buf("cparams")
        k.dma(SP, ch_misc, ccw[:], c_cw_fm[:, :, :], writes=[Bp])
        k.dma(SP, ch_misc, ccb[:], c_cb_fm[:, :], writes=[Bp])
        k.dma(SP, ch_misc, bgt[:], c_bg_fm[:, :, :, :], writes=[Bp])
        k.dma(SP, ch_misc, lam[:], c_lam_fm[:, :, :], writes=[Bp])
        k.op(ACT, lambda e: e.activation(out=cneg[:], in_=lam[:], func=AF.Exp, scale=-1.0), reads=[Bp], writes=[Bp])
        k.op(ACT, lambda e: e.activation(out=cneg[:], in_=cneg[:], func=AF.Ln, bias=one_t[:], scale=1.0), reads=[Bp, B_const], writes=[Bp])
        k.op(DVE, lambda e: e.tensor_scalar(out=cneg[:], in0=cneg[:], scalar1=-8.0, scalar2=None, op0=ALU.mult), reads=[Bp], writes=[Bp])
        for s in range(1):
            t_, b_, _ = xz.next()
            k.op(POOL, lambda e, t_=t_: e.memset(t_[:], 0.0), writes=[b_])
        nctx = True
        segs = [(0, SEQ, LOFF)] + ([(SEQ, CTX, COFF)] if nctx else [])
        ntok = SEQ + (CTX if nctx else 0)
        def stage_A(h):
            xc, xcB = xcs[h % 2], xcBs[h % 2]
            xcb, xcbB = xcbs[h % 2], xcbBs[h % 2]
            xt, xb, xch = xz.next()
            k.dma(SP, xch, xt[:, LOFF:LOFF + SEQ], X4[h * 128:(h + 1) * 128, 0:SEQ], reads=[X4B[h]], writes=[xb])
            k.dma(SP, xz2c[h % 2], xt[:, COFF:COFF + CTX], X4[h * 128:(h + 1) * 128, SEQ:T], reads=[X4B[h]], writes=[xb])
            wgt, wgb = wg.load(c_wg[:, :, :, h, :])
            for (c0, n, off) in segs:
                k.op(DVE, lambda e, c0=c0, n=n, off=off: e.tensor_scalar(
                    out=xc[:, c0:c0 + n], in0=xt[:, off:off + n], scalar1=ccw[:, 2, h:h + 1], scalar2=ccb[:, h:h + 1],
                    op0=ALU.mult, op1=ALU.add), reads=[xb, Bp], writes=[xcB])
                for j in (0, 1, 3):
                    k.op(DVE, lambda e, c0=c0, n=n, off=off, j=j: e.scalar_tensor_tensor(
                        out=xc[:, c0:c0 + n], in0=xt[:, off + j - 2:off + j - 2 + n], scalar=ccw[:, j, h:h + 1],
                        in1=xc[:, c0:c0 + n], op0=ALU.mult, op1=ALU.add), reads=[xb, Bp, xcB], writes=[xcB])
            k.op(POOL, lambda e: e.tensor_copy(out=xcb[:, 0:ntok], in_=xc[:, 0:ntok]), reads=[xcB], writes=[xcbB])
            return dict(h=h, xc=xc, xcB=xcB, xcb=xcb, xcbB=xcbB, wgt=wgt, wgb=wgb)

        def stage_B(C, d):
            h, xc, xcB, xcb, xcbB, wgt, wgb = C["h"], C["xc"], C["xcB"], C["xcb"], C["xcbB"], C["wgt"], C["wgb"]
            Rb, Ib, RBt, IBt = Rbs[d], Ibs[d], RBts[d], IBts[d]
            for t in range(Lm.ntiles):
                t0, w, col_ = Lm.tiles[t]
                for gidx, (gt, gBs) in enumerate(((Rb, RBt), (Ib, IBt))):
                    pb = next_ps()
                    k.op(PE, lambda e, pb=pb, w=w, t0=t0, gidx=gidx: e.matmul(
                        psum[:, pb, 0:w], lhsT=wgt[:, d, gidx, :], rhs=xcb[:, t0:t0 + w], start=True, stop=True),
                         reads=[wgb, xcbB], writes=[PSB[pb]])
                    k.op(ACT, lambda e, pb=pb, w=w, t0=t0, gidx=gidx, gt=gt: e.activation(
                        out=gt[:, t0:t0 + w], in_=psum[:, pb, 0:w], func=AF.Sigmoid,
                        bias=bgt[:, d, gidx, h:h + 1], scale=1.0), reads=[PSB[pb], Bp], writes=[gBs[t]])
            for t in range(Lm.ntiles):
                t0, w, col_ = Lm.tiles[t]
                k.op(ACT, lambda e, t0=t0, w=w: e.activation(out=Rb[:, t0:t0 + w], in_=Rb[:, t0:t0 + w], func=AF.Exp,
                                                             scale=cneg[:, d, h:h + 1]), reads=[RBt[t], Bp], writes=[RBt[t]])
                k.op(DVE, lambda e, t0=t0, w=w: e.tensor_tensor(out=Tb[:, t0:t0 + w], in0=Rb[:, t0:t0 + w], in1=Rb[:, t0:t0 + w], op=ALU.mult),
                     reads=[RBt[t]], writes=[TBt[t]])
                k.op(ACT, lambda e, t0=t0, w=w: e.activation(out=Tb[:, t0:t0 + w], in_=Tb[:, t0:t0 + w], func=AF.Sqrt, scale=-0.99999994, bias=one_t[:]),
                     reads=[TBt[t], B_const], writes=[TBt[t]])
                k.op(POOL, lambda e, t0=t0, w=w: e.tensor_tensor(out=Ib[:, t0:t0 + w], in0=Ib[:, t0:t0 + w], in1=xc[:, t0:t0 + w], op=ALU.mult),
                     reads=[IBt[t], xcB], writes=[IBt[t]])
                k.op(POOL, lambda e, t0=t0, w=w: e.tensor_tensor(out=Ib[:, t0:t0 + w], in0=Ib[:, t0:t0 + w], in1=Tb[:, t0:t0 + w], op=ALU.mult),
                     reads=[IBt[t], TBt[t]], writes=[IBt[t]])

        def stage_C(C, d):
            Rb, Ib, RBt, IBt = Rbs[d], Ibs[d], RBts[d], IBts[d]
            Hout, HoutB = (HF, HFB) if d == 0 else (HR, HRB)
            if d == 0:
                k.op(DVE, lambda e: e.tensor_tensor_scan(
                    out=Hout[:, SEQ:T], data0=Rb[:, SEQ:T], data1=Ib[:, SEQ:T], initial=0.0, op0=ALU.mult, op1=ALU.add),
                     reads=[RBt[8], IBt[8]], writes=[HoutB])
                init = Hout[:, T - 1:T]
                k.op(DVE, lambda e: e.tensor_tensor_scan(
                    out=Hout[:, 0:SEQ], data0=Rb[:, 0:SEQ], data1=Ib[:, 0:SEQ], initial=init, op0=ALU.mult, op1=ALU.add),
                     reads=RBt[0:8] + IBt[0:8] + [HoutB], writes=[HoutB])
            else:
                k.op(DVE, lambda e: e.tensor_tensor_scan(
                    out=Hout[:, SEQ:T][:, ::-1], data0=Rb[:, SEQ:T][:, ::-1], data1=Ib[:, SEQ:T][:, ::-1], initial=0.0,
                    op0=ALU.mult, op1=ALU.add), reads=[RBt[8], IBt[8]], writes=[HoutB])
                init = Hout[:, SEQ:SEQ + 1]
                k.op(DVE, lambda e: e.tensor_tensor_scan(
                    out=Hout[:, 0:SEQ][:, ::-1], data0=Rb[:, 0:SEQ][:, ::-1], data1=Ib[:, 0:SEQ][:, ::-1], initial=init,
                    op0=ALU.mult, op1=ALU.add), reads=RBt[0:8] + IBt[0:8] + [HoutB], writes=[HoutB])

        def stage_D(C):
            h = C["h"]
            k.op(DVE, lambda e: e.tensor_tensor(out=HR[:, 0:ntok], in0=HF[:, 0:ntok], in1=HR[:, 0:ntok], op=ALU.add),
                 reads=[HFB, HRB], writes=[HRB])
            k.dma(SP, hst[h % 2], X4[h * 128:(h + 1) * 128, 0:ntok], HR[:, 0:ntok], reads=[HRB], writes=[X4B[h]])

        ctxs = {0: stage_A(0)}
        for h in range(KC):
            stage_B(ctxs[h], 0)
            if h + 1 < KC:
                ctxs[h + 1] = stage_A(h + 1)
            stage_B(ctxs[h], 1)
            stage_C(ctxs[h], 0)
            stage_C(ctxs[h], 1)
            stage_D(ctxs[h])
        k.barrier()
        k.sb_reset(mark)

    def stage_mixC_y(L, Narena, NB, li):
        mark = k.sb_mark()
        ws = WStream("wcin", [128, KC, 128], 4)
        yo = Slots("yo", [128, 512], F32, 3, chan=False)
        hs = Slots("hsw", [128, 512], F32, 3)
        oo = Slots("oo", [128, 512], BF16, 3)

        def evac(bi, t, pb):
            t0, w, col = L.tiles[t]
            yt, yb, _ = yo.next()
            k.op(ACT, lambda e: e.activation(out=yt[:, 0:w], in_=psum[:, pb, 0:w], func=AF.Gelu_apprx_tanh),
                 reads=[PSB[pb]], writes=[yb])
            ht, hbuf, hch = hs.next()
            k.dma(SP, hch, ht[:, 0:w], HSw[bi * 128:(bi + 1) * 128, t0:t0 + w], writes=[hbuf])
            ot, ob, och = oo.next()
            k.op(DVE, lambda e: e.tensor_tensor(out=ot[:, 0:w], in0=yt[:, 0:w], in1=ht[:, 0:w], op=ALU.mult),
                 reads=[yb, hbuf], writes=[ob])
            k.dma(SP, och, S4[bi * 128:(bi + 1) * 128, t0:t0 + w], ot[:, 0:w], reads=[ob])

        proj1(L, Narena, NB, ws, [c_in_b[bi] for bi in range(16)], evac)
        k.barrier()
        k.sb_reset(mark)

    def stage_mixB(Narena, NB, li, arena_mark):
        ntiles = 9
        mark = k.sb_mark()
        ws = WStream("wqkv", [128, KC, 128], 4)
        qo = Slots("qo", [128, 512], BF16, 4)
        qscale = float(128 ** -0.5)

        def evac(bi, t, pb):
            t0, w, col_ = Lm.tiles[t]
            ot, ob, och = qo.next()
            which, hh = bi // 16, bi % 16
            dst = (S1, S2, S3)[which]
            if which == 0:
                k.op(ACT, lambda e: e.activation(out=ot[:, 0:w], in_=psum[:, pb, 0:w], func=AF.Copy, scale=qscale),
                     reads=[PSB[pb]], writes=[ob])
            else:
                k.op(DVE, lambda e: e.tensor_copy(out=ot[:, 0:w], in_=psum[:, pb, 0:w]), reads=[PSB[pb]], writes=[ob])
            k.dma(SP, och, dst[hh * 128:(hh + 1) * 128, t0:t0 + w], ot[:, 0:w], reads=[ob])

        proj1(Lm, Narena, NB, ws, [b_qkv_b[bi] for bi in range(48)], evac)
        k.barrier()
        k.sb_reset(arena_mark)
        Qs = Slots("Qh", [128, T], BF16, 2)
        Ks = Slots("Kh", [128, T], BF16, 2)
        Vs = Slots("Vh", [128, T], BF16, 2)
        Bt = Slots("Bt", [128, 6, 1024], BF16, 2)
        Vtm = Slots("Vtm", [128, 34, 128], BF16, 2, chan=False)
        Oh = Slots("Oh", [128, T], BF16, 2)
        Ps = Slots("P", [128, 1024], BF16, 3, chan=False)
        Dgs = Slots("Dg", [128, 128], BF16, 3, chan=False)
        PTs = Slots("PT", [128, 7, 128], BF16, 3, chan=False)
        sm = Slots("sm", [128, 4], F32, 4, chan=False)
        sc_rr = [0]
        o_rr = [0]
        nunits = 34 if ntiles == 9 else 32
        def do_head(h):
            qt, qb, qch = Qs.next()
            kt, kb, kch = Ks.next()
            vt, vb, vch = Vs.next()
            bt, bb, bch = Bt.next()
            k.dma(SP, qch, qt[:], S1[h * 128:(h + 1) * 128, :], writes=[qb])
            k.dma(SP, kch, kt[:], S2[h * 128:(h + 1) * 128, :], writes=[kb])
            k.dma(SP, vch, vt[:], S3[h * 128:(h + 1) * 128, :], writes=[vb])
            k.dma(POOL, bch, bt[:], b_bias[h], writes=[bb], max_dma_last_dim=4096)
            vtm, vtmb, _ = Vtm.next()
            for c4 in range(0, 34, 4):
                nb_ = min(4, 34 - c4)
                pt = 6 + o_rr[0] % 2
                o_rr[0] += 1
                ptile = psum[:, pt, :].bitcast(BF16)
                fns = []
                for ff in range(nb_):
                    cn = c4 + ff
                    fns.append(lambda e, ff=ff, cn=cn, ptile=ptile, vt=vt: e.transpose(
                        ptile[:, ff * 128:(ff + 1) * 128], vt[:, cn * 128:(cn + 1) * 128], ident_bf[:]))
                k.op(PE, fns, reads=[vb, B_const], writes=[PSB[pt]])
                k.op(ACT, lambda e, c4=c4, nb_=nb_, ptile=ptile, vtm=vtm: e.activation(
                    out=vtm[:, c4:c4 + nb_, :], in_=ptile[:, 0:nb_ * 128].rearrange("p (a b) -> p a b", b=128), func=AF.Copy),
                     reads=[PSB[pt]], writes=[vtmb])
            ot, ob, och = Oh.next()
            pend_pt = None
            pend_pv = None

            def do_scores(u):
                if u < 32:
                    a = u
                    ti = {0: 1, 1: 2, 30: 3, 31: 4}.get(a, 0)
                    lo = (2 * a - 4) if 2 <= a <= 29 else (0 if a < 2 else 56)
                    k0 = lo * 64
                    q0 = a * 128
                    nlat = 576
                else:
                    ti = 5
                    q0 = SEQ + (u - 32) * 128
                    k0 = 0
                    nlat = 0
                pa = (sc_rr[0] % 2) * 2
                pb2 = pa + 1
                sc_rr[0] += 1
                qv = qt[:, q0:q0 + 128]
                if nlat:
                    k.op(PE, [lambda e: e.matmul(psum[:, pa, 0:512], lhsT=ident_bf[:], rhs=bt[:, ti, 0:512], start=True, stop=False),
                              lambda e: e.matmul(psum[:, pa, 0:512], lhsT=qv, rhs=kt[:, k0:k0 + 512], start=False, stop=True)],
                         reads=[qb, kb, bb, B_const], writes=[PSB[pa]])
                    k.op(PE, [lambda e: e.matmul(psum[:, pb2, 0:512], lhsT=ident_bf[:], rhs=bt[:, ti, 512:1024], start=True, stop=False),
                              lambda e: e.matmul(psum[:, pb2, 0:64], lhsT=qv, rhs=kt[:, k0 + 512:k0 + 576], start=False, stop=False),
                              lambda e: e.matmul(psum[:, pb2, 64:320], lhsT=qv, rhs=kt[:, SEQ:T], start=False, stop=True)],
                         reads=[qb, kb, bb, B_const], writes=[PSB[pb2]])
                else:
                    k.op(PE, lambda e: e.matmul(psum[:, pa, 0:512], lhsT=ident_bf[:], rhs=bt[:, ti, 0:512], start=True, stop=True),
                         reads=[bb, B_const], writes=[PSB[pa]])
                    k.op(PE, [lambda e: e.matmul(psum[:, pb2, 0:512], lhsT=ident_bf[:], rhs=bt[:, ti, 512:1024], start=True, stop=False),
                              lambda e: e.matmul(psum[:, pb2, 0:256], lhsT=qv, rhs=kt[:, SEQ:T], start=False, stop=True)],
                         reads=[qb, kb, bb, B_const], writes=[PSB[pb2]])
                lv = psum[:, pa:pa + 2, :].rearrange("p a b -> p (a b)")
                st, sb_, _ = sm.next()
                k.op(DVE, lambda e: e.reduce_max(out=st[:, 0:1], in_=lv, axis=AX.X), reads=[PSB[pa], PSB[pb2]], writes=[sb_])
                k.op(DVE, lambda e: e.tensor_scalar(out=st[:, 1:2], in0=st[:, 0:1], scalar1=-1.0, scalar2=None, op0=ALU.mult),
                     reads=[sb_], writes=[sb_])
                pt_, pb_, _ = Ps.next()
                k.op(ACT, lambda e: e.activation(out=pt_[:, 0:1024], in_=lv, func=AF.Exp, bias=st[:, 1:2], scale=1.0, accum_out=st[:, 2:3]),
                     reads=[PSB[pa], PSB[pb2], sb_], writes=[pb_, sb_])
                k.op(DVE, lambda e: e.reciprocal(out=st[:, 3:4], in_=st[:, 2:3]), reads=[sb_], writes=[sb_])
                dg, dgb, _ = Dgs.next()
                k.op(DVE, lambda e: e.tensor_scalar(out=dg[:], in0=ident_bf[:], scalar1=st[:, 3:4], scalar2=None, op0=ALU.mult),
                     reads=[sb_, B_const], writes=[dgb])
                blocks = []
                if nlat:
                    for i in range(4):
                        blocks.append((i * 128, 128, k0 // 128 + i))
                    blocks.append((512, 64, k0 // 128 + 4))
                    blocks.append((576, 128, 32))
                    blocks.append((704, 128, 33))
                else:
                    blocks.append((512, 128, 32))
                    blocks.append((640, 128, 33))
                return dict(P=pt_, Pb=pb_, dg=dg, dgb=dgb, blocks=blocks, q0=q0)

            def do_pt(U):
                ptt, ptb, _ = PTs.next()
                blocks = U["blocks"]
                for bank, lo_, hi_ in ((4, 0, 4), (5, 4, 7)):
                    bl = blocks[lo_:hi_]
                    if not bl:
                        continue
                    fns = []
                    for bi_, (pc, nk_, ch_) in enumerate(bl):
                        fns.append(lambda e, bi_=bi_, pc=pc, nk_=nk_, bank=bank: e.matmul(
                            psum[0:nk_, bank, bi_ * 128:(bi_ + 1) * 128], lhsT=U["P"][:, pc:pc + nk_], rhs=U["dg"][:],
                            start=True, stop=True))
                    k.op(PE, fns, reads=[U["Pb"], U["dgb"]], writes=[PSB[bank]])
                    nb_ = len(bl)
                    k.op(ACT, lambda e, bank=bank, lo_=lo_, nb_=nb_: e.activation(
                        out=ptt[:, lo_:lo_ + nb_, :], in_=psum[:, bank, 0:nb_ * 128].rearrange("p (a b) -> p a b", b=128), func=AF.Copy),
                         reads=[PSB[bank]], writes=[ptb])
                U["PT"] = ptt
                U["PTb"] = ptb

            def do_pv(U):
                po = 6 + o_rr[0] % 2
                o_rr[0] += 1
                blocks = U["blocks"]
                n = len(blocks)
                fns = []
                for bi_, (pc, nk_, ch_) in enumerate(blocks):
                    fns.append(lambda e, bi_=bi_, nk_=nk_, ch_=ch_, po=po: e.matmul(
                        psum[:, po, 0:128], lhsT=vtm[0:nk_, ch_, :], rhs=U["PT"][0:nk_, bi_, :],
                        start=(bi_ == 0), stop=(bi_ == n - 1)))
                k.op(PE, fns, reads=[vtmb, U["PTb"]], writes=[PSB[po]])
                q0 = U["q0"]
                k.op(DVE, lambda e: e.tensor_copy(out=ot[:, q0:q0 + 128], in_=psum[:, po, 0:128]), reads=[PSB[po]], writes=[ob])

            for u in range(nunits + 2):
                Unew = do_scores(u) if u < nunits else None
                if pend_pt is not None:
                    do_pt(pend_pt)
                if pend_pv is not None:
                    do_pv(pend_pv)
                pend_pv = pend_pt
                pend_pt = Unew
            ntok = T
            k.dma(SP, och, S4[h * 128:(h + 1) * 128, 0:ntok], ot[:, 0:ntok], reads=[ob])
        for h in range(KC):
            do_head(h)
        k.barrier()
        k.sb_reset(arena_mark)
        proj2(Lm, S4, KC, lambda c: b_out_b[c], li, 2, resident=True)
        k.sb_reset(mark)

    def new_arena(L):
        k.sb_reset(arena_mark)
        Na = k.sb([128, KC, L.T], BF16, "N")
        return Na, [Buf(f"N{t}") for t in range(L.ntiles)]

    def ffn_block(L, li, edge_mask=False):
        Na, NB = new_arena(L)
        stage_norm(L, Na, NB, li, 1, edge_mask=edge_mask)
        stage_ffn(L, Na, NB, li)
        k.sb_reset(arena_mark)
        proj2(L, S1, NJ, lambda c, li=li: ffn_dn_b[li, c], li, 5)

    def dyn_vals(h):
        if force_q is not None:
            q = force_q
            s3 = min(max(1024 * q - 128, 0), 2816)
            return s3, max(s3 - 1, 0), 1024 * q - s3
        pid = h.partition_id()
        q = pid % 4
        nz = (q + 3) // 4
        ge2 = q // 2
        is3 = q // 3
        s3 = 896 * nz + 1024 * ge2 + 896 * is3
        s3m1 = 895 * nz + 1024 * ge2 + 896 * is3
        own = 128 * nz + 128 * is3
        return s3, s3m1, own

    def dsl(start, size):
        if isinstance(start, int):
            return slice(start, start + size)
        return bass.ds(start, size)

    stage_ada()
    arena_mark = k.sb_mark()
    done = False
    for li in range(min(n_layers, 3)):
        Na, NB = new_arena(Lm)
        stage_norm(Lm, Na, NB, li, 0)
        if li == 0:
            stage_mixA(Lm, Na, NB, li, 0, arena_mark)
        elif li == 1:
            stage_mixB(Na, NB, li, arena_mark)
        else:
            stage_mixC_x(Na, NB, arena_mark)
            for ci, (dst, srcT) in enumerate(((Hw, Hd_full), (HSw, X4_full))):
                k.dma(SP, ch_cp[ci], dst[:, 0:1282], lambda h, srcT=srcT: srcT[:, dsl(dyn_vals(h)[0], 1282)])
            k.barrier()
            Na, NB = new_arena(LW2)
            stage_norm(LW2, Na, NB, li, 0)
            stage_mixC_y(LW2, Na, NB, li)
            k.sb_reset(arena_mark)
            proj2(LW2, S4, KC, lambda c: c_out_b[c], li, 2, resident=True)
        if stop_after == (li, "mix"):
            done = True
            break
        ffn_block(Lm if li < 2 else LW2, li, edge_mask=(li == 2))
        if stop_after == (li, "ffn"):
            done = True
            break
    if not done and n_layers == DEPTH:
        li = 3
        Na, NB = new_arena(LW3)
        stage_norm(LW3, Na, NB, li, 0)
        stage_mixA(LW3, Na, NB, li, 1, arena_mark)
        if stop_after != (3, "mix"):
            ffn_block(LW3, li)
            if stop_after != (3, "ffn"):
                k.sb_reset(arena_mark)
                stage_norm(LW3, None, None, 0, 0, final_dst=OutW)
                k.dma(SP, ch_cp[2], outT[:, :], lambda h: OutW[:, dsl(dyn_vals(h)[2], 1024)])
    if stop_after is not None:
        li_, st_ = stop_after
        if li_ < 2 or (li_ == 2 and False):
            dbg = nc.dram_tensor("dbgH", [D, T], F32, kind="ExternalOutput").ap()
            k.dma(SP, ch_cp[3], dbg[:, :], Hd[:, :])
        else:
            dbg = nc.dram_tensor("dbgH", [D, 1282], F32, kind="ExternalOutput").ap()
            k.dma(SP, ch_cp[3], dbg[:, :], Hw[:, :])
    k.barrier()

    def replay(eng, h):
        for kind_, waits, payload, sem in eng.prog:
            for s, v in waits:
                h.wait_ge(s.h, v)
            if kind_ == "wait":
                continue
            if kind_ == "op":
                ins = None
                for fn in payload:
                    ins = fn(h)
                ins.then_inc(sem.h, 1)
            elif kind_ == "dma":
                out, in_, kw = payload
                if callable(out):
                    out = out(h)
                if callable(in_):
                    in_ = in_(h)
                try:
                    ins = h.dma_start(out=out, in_=in_, **kw)
                except Exception:
                    print("DMA FAILED", out, in_, kw)
                    raise
                ins.then_inc(sem.h, 16)

    with nc.Block() as block:
        @block.tensor
        def _(e):
            replay(PE, e)

        @block.scalar
        def _(e):
            replay(ACT, e)

        @block.vector
        def _(e):
            replay(DVE, e)

        @block.gpsimd
        def _(e):
            replay(POOL, e)

        @block.sync
        def _(e):
            replay(SP, e)

    for name, (cm, h) in sem_handles.items():
        pass
    return nc


def _blk(w, kc=KC):
    Kd, Nd = w.shape
    nb = Nd // 128
    return np.ascontiguousarray(w.reshape(kc, 128, nb, 128).transpose(2, 1, 0, 3))


def _blk2(w, kb):
    return np.ascontiguousarray(w.reshape(kb, 128, KC, 128).transpose(2, 1, 0, 3))


def _fm(v):
    lead = v.shape[:-1]
    C = v.shape[-1] // 128
    a = v.reshape(*lead, C, 128)
    return np.ascontiguousarray(np.moveaxis(a, -1, 0))


def _na_bias(rpb):
    NEG = np.float32(-1e30)
    H = rpb.shape[0]
    out = np.full((H, 128, 6, 1024), NEG, np.float32)
    col = np.arange(64)
    c0 = np.clip(col - 8, 0, 48)
    col_ok = (col[None, :] >= c0[:, None]) & (col[None, :] < c0[:, None] + 16)
    dc = np.clip(col[None, :] - col[:, None], -15, 15) + 15
    specs = [(8, 12, 9), (0, 0, 8), (1, 0, 8), (30, 56, 8), (31, 56, 8)]
    for ti, (a, lo, nrows) in enumerate(specs):
        tab = np.full((H, 128, 1024), NEG, np.float32)
        tab[:, :, 576:832] = 0.0
        for qr in range(2):
            r = 2 * a + qr
            r0 = min(max(r - 4, 0), 56)
            for i in range(8):
                kr = r0 + i
                u = kr - lo
                assert 0 <= u < nrows
                dr = kr - r
                vals = rpb[:, dr + 7, :][:, dc]
                vals = np.where(col_ok[None], vals, NEG)
                tab[:, qr * 64:(qr + 1) * 64, u * 64:(u + 1) * 64] = vals
        out[:, :, ti, :] = tab
    out[:, :, 5, 512:768] = 0.0
    return out


def _prep_inputs(inp):
    f = lambda a: np.asarray(a, np.float32)
    shared = {}
    aw = f(inp["ada_w"])
    shared["ada_wb"] = np.ascontiguousarray(aw.reshape(DEPTH, KC, 128, 24, 512).transpose(0, 3, 2, 1, 4))
    shared["ada_b_fm"] = _fm(f(inp["ada_b"]))
    shared["norm_g_fm"] = _fm(f(inp["norm_g"]))
    shared["final_g_fm"] = _fm(f(inp["final_g"]))
    shared["ffn_up_b"] = np.stack([_blk(f(inp["ffn_w_up"])[i]) for i in range(DEPTH)])
    shared["ffn_cw_fm"] = _fm(f(inp["ffn_conv_w"]))
    shared["ffn_cb_fm"] = _fm(f(inp["ffn_conv_b"]))
    shared["ffn_dn_b"] = np.stack([_blk2(f(inp["ffn_w_down"])[i], NJ) for i in range(DEPTH)])
    shared["a_in_b"] = np.stack([_blk(f(inp["a_w_in"])[i]) for i in range(2)])
    shared["a_gv_fm"] = _fm(f(inp["a_g_v"]))
    shared["a_wsT"] = np.ascontiguousarray(f(inp["a_w_s"]).transpose(0, 3, 1, 2))
    shared["a_bs_bc"] = np.ascontiguousarray(np.broadcast_to(np.repeat(f(inp["a_b_s"]), 2, axis=1)[:, None], (2, 128, 32, 128)))
    shared["a_gv_bc"] = np.ascontiguousarray(np.broadcast_to(f(inp["a_g_v"])[:, None], (2, 128, 4096)))
    shared["a_out_b"] = np.stack([_blk2(f(inp["a_w_out"])[i], 32) for i in range(2)])
    shared["b_qkv_b"] = _blk(f(inp["b_w_qkv"])[0])
    shared["b_bias"] = _na_bias(f(inp["b_rpb"])[0])
    shared["b_out_b"] = _blk2(f(inp["b_w_out"])[0], KC)
    shared["c_in_b"] = _blk(f(inp["c_w_in"])[0])
    shared["c_cw_fm"] = _fm(f(inp["c_conv_w"])[0])
    shared["c_cb_fm"] = _fm(f(inp["c_conv_b"])[0])
    shared["c_wg"] = np.ascontiguousarray(f(inp["c_w_gate"])[0].transpose(3, 0, 1, 2, 4))
    shared["c_bg_fm"] = np.ascontiguousarray(f(inp["c_b_gate"])[0].transpose(3, 0, 1, 2))
    shared["c_lam_fm"] = _fm(f(inp["c_lam"])[0])
    shared["c_out_b"] = _blk2(f(inp["c_w_out"])[0], KC)
    shared["ident_in"] = np.eye(128, dtype=np.float32)
    maps = []
    x = f(inp["x"]); ctx = f(inp["ctx"]); c = f(inp["c"]); cc = f(inp["c_ctx"])
    per_b = []
    for b in range(2):
        per_b.append((np.ascontiguousarray(np.concatenate([x[b], ctx[b]], axis=0).T),
                      np.ascontiguousarray(np.stack([_fm(c[b]), _fm(cc)], axis=-1))))
    for r in range(8):
        b, q = r // 4, r % 4
        m = dict(shared)
        m["xT"], m["c_fm"] = per_b[b]
        em = np.ones((128, 2), np.float32)
        if q == 0:
            em[:, 0] = 0.0
        if q == 3:
            em[:, 1] = 0.0
        m["emask_in"] = em
        maps.append(m)
    return maps


_CACHE = {}


def kernel(**inputs):
    maps = _prep_inputs(inputs)
    if "nc" not in _CACHE:
        _CACHE["nc"] = build_program()
    nc = _CACHE["nc"]
    res = run_bass_kernel_spmd(nc, maps, core_ids=list(range(8)))
    out = np.empty((2, SEQ, D), np.float32)
    for r in range(8):
        b, q = r // 4, r % 4
        out[b, q * 1024:(q + 1) * 1024, :] = res.results[r]["outT"].T
    return out
```

```python
import numpy as np
import concourse.bass as bass
import concourse.mybir as mybir
from concourse.bass_utils import run_bass_kernel_spmd

F32 = mybir.dt.float32
BF16 = mybir.dt.bfloat16
AF = mybir.ActivationFunctionType
ALU = mybir.AluOpType
AX = mybir.AxisListType

D = 2048
KC = 16
SEQ = 4096
CTX = 256
T = SEQ + CTX
DEPTH = 4
DFF = 5632
NJ = DFF // 128
EPS = 1e-6
TILES = [(i * 512, 512) for i in range(8)] + [(4096, 256)]
GROUPS = [[0, 1], [2, 3], [4, 5], [6, 7], [8]]
SB_BASE = 20480
SB_END = 229376


class Buf:
    __slots__ = ("name", "w", "r")

    def __init__(self, name=""):
        self.name = name
        self.w = None
        self.r = {}


class Sem:
    def __init__(self, handle, name):
        self.h = handle
        self.name = name


class Eng:
    def __init__(self, name, sem):
        self.name = name
        self.sem = sem
        self.cnt = 0
        self.seen = {}
        self.prog = []


class Chan:
    def __init__(self, sem):
        self.sem = sem
        self.n = 0


class Layout:
    def __init__(self, Hd, tiles, segs):
        self.Hd = Hd
        self.tiles = tiles
        self.T = tiles[-1][0] + tiles[-1][1]
        self.ntiles = len(tiles)
        self.groups = [list(range(i, min(i + 2, len(tiles)))) for i in range(0, len(tiles), 2)]
        self.segs = segs
        self.nt = []
        for (t0, w, col) in tiles:
            o = 0
            while o < w:
                ww = min(256, w - o)
                self.nt.append((t0 + o, ww, col))
                o += ww
        self.zoff = []
        zb = 1
        segbase = []
        for (s0, n) in segs:
            segbase.append((s0, n, zb))
            zb += n + 2
        self.zw = (zb + 7) // 8 * 8
        for (t0, w, col) in tiles:
            for (s0, n, b) in segbase:
                if s0 <= t0 < s0 + n:
                    self.zoff.append(b + t0 - s0)
        self.HdB = {}

    def hb(self, c, t):
        if (c, t) not in self.HdB:
            self.HdB[(c, t)] = Buf(f"Hd{c}_{t}")
        return self.HdB[(c, t)]


class K:
    def __init__(self, nc):
        self.nc = nc
        self.sems = []
        self.engs = {}
        self.chans = []
        self.sb_ptr = SB_BASE
        self.uid = 0

    def new_sem(self, name):
        s = Sem(self.nc.alloc_semaphore(name) if hasattr(self.nc, "alloc_semaphore") else None, name)
        self.sems.append(s)
        return s

    def sb(self, shape, dt, name=None):
        self.uid += 1
        nbytes = int(np.prod(shape[1:])) * (4 if dt == F32 else 2)
        off = (self.sb_ptr + 31) // 32 * 32
        assert off + nbytes <= SB_END, ("sbuf overflow", name, off, nbytes)
        self.sb_ptr = off + nbytes
        return self.nc.alloc_sbuf_tensor_at(f"{name or 't'}{self.uid}", list(shape), dt, offset=off)

    def sb_mark(self):
        return self.sb_ptr

    def sb_reset(self, mark):
        self.sb_ptr = mark

    def _need(self, eng, evs):
        waits = []
        for sem, val in evs:
            if val <= 0:
                continue
            if eng.seen.get(sem, 0) < val:
                eng.seen[sem] = val
                waits.append((sem, val))
        return waits

    def _deps(self, reads, writes):
        evs = []
        for b in reads:
            if b.w is not None:
                evs.append(b.w)
        for b in writes:
            if b.w is not None:
                evs.append(b.w)
            for s, v in b.r.items():
                evs.append((s, v))
        best = {}
        for s, v in evs:
            if best.get(s, 0) < v:
                best[s] = v
        return list(best.items())

    def _mark(self, ev, reads, writes):
        s, v = ev
        for b in reads:
            if b.r.get(s, 0) < v:
                b.r[s] = v
        for b in writes:
            b.w = ev
            b.r = {}

    def op(self, eng, fns, reads=(), writes=()):
        if not isinstance(fns, (list, tuple)):
            fns = [fns]
        waits = self._need(eng, self._deps(reads, writes))
        eng.cnt += 1
        ev = (eng.sem, eng.cnt)
        self._mark(ev, reads, writes)
        eng.prog.append(("op", waits, list(fns), eng.sem))
        return ev

    def dma(self, eng, chan, out, in_, reads=(), writes=(), **kw):
        evs = self._deps(reads, writes)
        evs.append((chan.sem, 16 * chan.n))
        waits = self._need(eng, evs)
        chan.n += 1
        ev = (chan.sem, 16 * chan.n)
        self._mark(ev, reads, writes)
        eng.prog.append(("dma", waits, (out, in_, kw), chan.sem))
        return ev

    def barrier(self):
        evs = [(e.sem, e.cnt) for e in self.engs.values()] + [(c.sem, 16 * c.n) for c in self.chans]
        for e in self.engs.values():
            waits = self._need(e, [x for x in evs if x[0] is not e.sem or True])
            if waits:
                e.prog.append(("wait", waits, None, None))

    def chan(self, name):
        c = Chan(self.new_sem(name))
        self.chans.append(c)
        return c


def build_program(stop_after=None, n_layers=DEPTH, force_q=None):
    nc = bass.Bass("TRN2", target_bir_lowering=False)
    k = K(nc)

    def din(name, shape):
        return nc.dram_tensor(name, list(shape), F32, kind="ExternalInput").ap()

    xT = din("xT", [D, T])
    c_fm = din("c_fm", [128, KC, 2])
    ada_wb = din("ada_wb", [DEPTH, 24, 128, KC, 512])
    ada_b_fm = din("ada_b_fm", [128, DEPTH, 96])
    norm_g_fm = din("norm_g_fm", [128, DEPTH, 2, KC])
    final_g_fm = din("final_g_fm", [128, KC])
    ffn_up_b = din("ffn_up_b", [DEPTH, 2 * NJ, 128, KC, 128])
    ffn_cw_fm = din("ffn_cw_fm", [128, DEPTH, 3, 2 * NJ])
    ffn_cb_fm = din("ffn_cb_fm", [128, DEPTH, 2 * NJ])
    ffn_dn_b = din("ffn_dn_b", [DEPTH, KC, 128, NJ, 128])
    a_in_b = din("a_in_b", [2, 64, 128, KC, 128])
    a_gv_fm = din("a_gv_fm", [128, 2, 32])
    a_wsT = din("a_wsT", [2, 128, 16, 128])
    a_bs_bc = din("a_bs_bc", [2, 128, 32, 128])
    a_gv_bc = din("a_gv_bc", [2, 128, 4096])
    a_out_b = din("a_out_b", [2, KC, 128, 32, 128])
    b_qkv_b = din("b_qkv_b", [48, 128, KC, 128])
    b_bias = din("b_bias", [16, 128, 6, 1024])
    b_out_b = din("b_out_b", [KC, 128, KC, 128])
    c_in_b = din("c_in_b", [32, 128, KC, 128])
    c_cw_fm = din("c_cw_fm", [128, 4, KC])
    c_cb_fm = din("c_cb_fm", [128, KC])
    c_wg = din("c_wg", [128, 2, 2, KC, 128])
    c_bg_fm = din("c_bg_fm", [128, 2, 2, KC])
    c_lam_fm = din("c_lam_fm", [128, 2, KC])
    c_out_b = din("c_out_b", [KC, 128, KC, 128])
    ident_in = din("ident_in", [128, 128])
    emask_in = din("emask_in", [128, 2])
    outT = nc.dram_tensor("outT", [D, 1024], F32, kind="ExternalOutput").ap()

    Hd_full = nc.dram_tensor("Hd", [D, T + 1], F32).ap()
    Hd = Hd_full[:, 1:T + 1]
    S1 = nc.dram_tensor("S1", [DFF, T], BF16).ap()
    S2 = nc.dram_tensor("S2", [4096, T], BF16).ap()
    S3 = nc.dram_tensor("S3", [D, T], BF16).ap()
    S4 = nc.dram_tensor("S4", [D, T], BF16).ap()
    X4_full = nc.dram_tensor("X4", [D, T + 1], F32).ap()
    X4 = X4_full[:, 1:T + 1]
    Hw = nc.dram_tensor("Hw", [D, 1282], F32).ap()
    HSw = nc.dram_tensor("HSw", [D, 1282], F32).ap()
    OutW = nc.dram_tensor("OutW", [D, 1280], F32).ap()

    sem_handles = {}

    def mk_sem(name):
        cm = nc.semaphore(name)
        h = cm.__enter__()
        sem_handles[name] = (cm, h)
        return Sem(h, name)

    k.new_sem = mk_sem
    PE = Eng("pe", mk_sem("s_pe"))
    ACT = Eng("act", mk_sem("s_act"))
    DVE = Eng("dve", mk_sem("s_dve"))
    POOL = Eng("pool", mk_sem("s_pool"))
    SP = Eng("sp", mk_sem("s_sp"))
    k.engs = {"pe": PE, "act": ACT, "dve": DVE, "pool": POOL, "sp": SP}

    psum = nc.alloc_psum_tensor("psum", [128, 8, 512], F32)
    PSB = [Buf(f"ps{i}") for i in range(8)]

    ones_bf = k.sb([128, 128], BF16, "ones")
    ident_bf = k.sb([128, 128], BF16, "ident")
    ident_f = k.sb([128, 128], F32, "identf")
    eps_t = k.sb([128, 1], F32, "eps")
    one_t = k.sb([128, 1], F32, "one")
    Mall = k.sb([128, DEPTH, 96, 2], F32, "Mall")
    ng = k.sb([128, DEPTH, 2, KC], F32, "ng")
    fg = k.sb([128, KC], F32, "fg")
    emask = k.sb([128, 2], F32, "emask")
    GS = k.sb([128, 2, 2, KC], F32, "GS")
    B_const = Buf("const")
    B_M = Buf("M")
    B_GS = Buf("GS")
    ch_misc = k.chan("c_misc")

    def wait_all_fn(waits):
        pass

    k.op(POOL, lambda e: e.memset(ones_bf[:], 1.0), writes=[B_const])
    k.dma(SP, ch_misc, ident_f[:], ident_in[:, :], writes=[B_const])
    k.op(POOL, lambda e: e.tensor_copy(out=ident_bf[:], in_=ident_f[:]), reads=[B_const], writes=[B_const])
    k.op(POOL, lambda e: e.memset(eps_t[:], EPS), writes=[B_const])
    k.op(POOL, lambda e: e.memset(one_t[:], 1.0), writes=[B_const])
    k.dma(SP, ch_misc, ng[:], norm_g_fm[:, :, :, :], writes=[B_const])
    k.dma(SP, ch_misc, fg[:], final_g_fm[:, :], writes=[B_const])
    k.dma(SP, ch_misc, emask[:], emask_in[:, :], writes=[B_const])

    persist_mark = k.sb_mark()

    chan_pool = {}

    def get_chan(name):
        if name not in chan_pool:
            chan_pool[name] = k.chan(name)
        return chan_pool[name]

    class WStream:
        def __init__(self, name, shape, nslots):
            self.t = [k.sb(shape, BF16, name) for _ in range(nslots)]
            self.b = [Buf(f"{name}{i}") for i in range(nslots)]
            self.c = [get_chan(f"c_{name}{i}") for i in range(nslots)]
            self.i = 0

        def load(self, src):
            s = self.i % len(self.t)
            self.i += 1
            k.dma(POOL, self.c[s], self.t[s][:], src, writes=[self.b[s]], max_dma_last_dim=4096)
            return self.t[s], self.b[s]

    class Slots:
        def __init__(self, name, shape, dt, n, chan=True):
            self.t = [k.sb(shape, dt, name) for _ in range(n)]
            self.b = [Buf(f"{name}{i}") for i in range(n)]
            self.c = [get_chan(f"c_{name}{i}") for i in range(n)] if chan else None
            self.i = 0

        def next(self):
            s = self.i % len(self.t)
            self.i += 1
            return self.t[s], self.b[s], (self.c[s] if self.c else None)

    ps_rr = [0]

    def next_ps(lo=0, hi=8):
        i = lo + ps_rr[0] % (hi - lo)
        ps_rr[0] += 1
        return i

    def mm_group(out_ap, pairs, reads, writes):
        n = len(pairs)
        fns = []
        for i, (l, r) in enumerate(pairs):
            fns.append(lambda e, l=l, r=r, i=i: e.matmul(out_ap, lhsT=l, rhs=r, start=(i == 0), stop=(i == n - 1)))
        return k.op(PE, fns, reads=reads, writes=writes)

    ch_cp = [k.chan(f"c_cp{i}") for i in range(4)]
    Lm = Layout(Hd, [(i * 512, 512, 0) for i in range(8)] + [(4096, 256, 1)], [(0, SEQ), (SEQ, CTX)])
    LW2 = Layout(Hw[:, 0:1282], [(0, 512, 0), (512, 512, 0), (1024, 258, 0)], [(0, 1282)])
    LW3 = Layout(Hw[:, 1:1281], [(0, 512, 0), (512, 512, 0), (1024, 256, 0)], [(0, 1280)])

    for t, (t0, w, col) in enumerate(Lm.tiles):
        k.dma(SP, ch_cp[t % 4], Hd[:, t0:t0 + w], xT[:, t0:t0 + w])
    zpad = k.sb([128, KC], F32, "zpad")
    Bz = Buf("zpad")
    k.op(POOL, lambda e: e.memset(zpad[:], 0.0), writes=[Bz])
    for dstf in (Hd_full, X4_full):
        k.dma(SP, ch_misc, dstf[:, 0:1].rearrange("(c p) o -> p (c o)", p=128), zpad[:], reads=[Bz],
              allow_slow_non_contiguous=True)
    k.barrier()

    def stage_ada():
        mark = k.sb_mark()
        cf = k.sb([128, KC, 2], F32, "cf")
        sc = k.sb([128, KC, 2], BF16, "sc")
        sg = k.sb([128, KC, 2], F32, "sg")
        adab = k.sb([128, DEPTH, 96], F32, "adab")
        Bc = Buf("cf")
        k.dma(SP, ch_misc, cf[:], c_fm[:, :, :], writes=[Bc])
        k.dma(SP, ch_misc, adab[:], ada_b_fm[:, :, :], writes=[Bc])
        k.op(ACT, lambda e: e.activation(out=sg[:], in_=cf[:], func=AF.Sigmoid), reads=[Bc], writes=[Bc])
        k.op(DVE, lambda e: e.tensor_tensor(out=sc[:], in0=cf[:], in1=sg[:], op=ALU.mult), reads=[Bc], writes=[Bc])
        ws = WStream("adaw", [128, KC, 512], 3)
        for li in range(n_layers):
            for j0 in range(0, 96, 16):
                pb = next_ps()
                for j4 in range(j0, j0 + 16, 4):
                    wt, wb = ws.load(ada_wb[li, j4 // 4])
                    for jj in range(4):
                        j = j4 + jj
                        mm_group(psum[:, pb, (j - j0) * 2:(j - j0) * 2 + 2],
                                 [(wt[:, kk, jj * 128:(jj + 1) * 128], sc[:, kk, :]) for kk in range(KC)],
                                 reads=[wb, Bc], writes=[PSB[pb]])
                k.op(DVE, lambda e, pb=pb, li=li, j0=j0: e.tensor_tensor(
                    out=Mall[:, li, j0:j0 + 16, :],
                    in0=psum[:, pb, 0:32].rearrange("p (j two) -> p j two", two=2),
                    in1=adab[:, li, j0:j0 + 16].unsqueeze(2).to_broadcast([128, 16, 2]),
                    op=ALU.add), reads=[PSB[pb], Bc], writes=[B_M])
        k.barrier()
        k.sb_reset(mark)

    def prep_gs(li, which):
        sh, scl = (0, 1) if which == 0 else (3, 4)
        for col in range(2):
            k.op(DVE, lambda e, col=col: e.scalar_tensor_tensor(
                out=GS[:, 0, col, :], in0=Mall[:, li, scl * 16:(scl + 1) * 16, col], scalar=1.0,
                in1=ng[:, li, which, :], op0=ALU.add, op1=ALU.mult), reads=[B_M, B_const], writes=[B_GS])
            k.op(DVE, lambda e, col=col: e.tensor_copy(out=GS[:, 1, col, :], in_=Mall[:, li, sh * 16:(sh + 1) * 16, col]),
                 reads=[B_M], writes=[B_GS])

    def stage_norm(L, Narena, NB, li, which, final_dst=None, edge_mask=False):
        mark = k.sb_mark()
        final = final_dst is not None
        if not final:
            prep_gs(li, which)
        Ht = Slots("Ht", [128, KC, 256], F32, 3)
        sq = Slots("sq", [128, 4, 256], BF16, 2, chan=False)
        std = Slots("std", [128, 256], F32, 2, chan=False)
        ost = Slots("nout", [128, KC, 256], F32, 1) if final else None
        for ti, (t0, w, col) in enumerate(L.nt):
            ht, hbuf, hch = Ht.next()
            k.dma(SP, hch, ht[:, :, 0:w], L.Hd[:, t0:t0 + w].rearrange("(c p) t -> p c t", p=128), writes=[hbuf])
            pb = next_ps()
            for c4 in range(4):
                st, sbuf_, _ = sq.next()
                k.op(ACT, lambda e, st=st, ht=ht, c4=c4, w=w: e.activation(out=st[:, :, 0:w], in_=ht[:, c4 * 4:(c4 + 1) * 4, 0:w], func=AF.Square),
                     reads=[hbuf], writes=[sbuf_])
                fns = []
                for cc in range(4):
                    first = (c4 == 0 and cc == 0)
                    last = (c4 == 3 and cc == 3)
                    fns.append(lambda e, st=st, cc=cc, first=first, last=last, pb=pb, w=w: e.matmul(
                        psum[:, pb, 0:w], lhsT=ones_bf[:], rhs=st[:, cc, 0:w], start=first, stop=last))
                k.op(PE, fns, reads=[sbuf_, B_const], writes=[PSB[pb]])
            sd, sdb, _ = std.next()
            k.op(ACT, lambda e, sd=sd, pb=pb, w=w: e.activation(out=sd[:, 0:w], in_=psum[:, pb, 0:w], func=AF.Sqrt,
                                                               scale=1.0 / D, bias=eps_t[:]),
                 reads=[PSB[pb], B_const], writes=[sdb])
            k.op(DVE, lambda e, sd=sd, w=w: e.reciprocal(out=sd[:, 0:w], in_=sd[:, 0:w]), reads=[sdb], writes=[sdb])
            if final:
                ot, ob, och = ost.next()
            k.op(DVE, lambda e, ht=ht, sd=sd, w=w: e.tensor_tensor(
                out=ht[:, :, 0:w], in0=ht[:, :, 0:w], in1=sd[:, 0:w].unsqueeze(1).to_broadcast([128, KC, w]), op=ALU.mult),
                 reads=[hbuf, sdb], writes=[hbuf])
            for c in range(KC):
                on_act = True
                if final:
                    if on_act:
                        k.op(ACT, lambda e, ht=ht, c=c, ot=ot, w=w: e.activation(out=ot[:, c, 0:w], in_=ht[:, c, 0:w], func=AF.Copy, scale=fg[:, c:c + 1]),
                             reads=[hbuf, B_const], writes=[ob])
                    else:
                        k.op(DVE, lambda e, ht=ht, c=c, ot=ot, w=w: e.tensor_scalar(out=ot[:, c, 0:w], in0=ht[:, c, 0:w], scalar1=fg[:, c:c + 1],
                                                                                      scalar2=None, op0=ALU.mult),
                             reads=[hbuf, B_const], writes=[ob])
                else:
                    if on_act:
                        k.op(ACT, lambda e, ht=ht, c=c, col=col, t0=t0, w=w: e.activation(
                            out=Narena[:, c, t0:t0 + w], in_=ht[:, c, 0:w], func=AF.Identity,
                            scale=GS[:, 0, col, c:c + 1], bias=GS[:, 1, col, c:c + 1]),
                             reads=[hbuf, B_GS], writes=[NB[t0 // 512]])
                    else:
                        k.op(DVE, lambda e, ht=ht, c=c, col=col, t0=t0, w=w: e.tensor_scalar(
                            out=Narena[:, c, t0:t0 + w], in0=ht[:, c, 0:w], scalar1=GS[:, 0, col, c:c + 1],
                            scalar2=GS[:, 1, col, c:c + 1], op0=ALU.mult, op1=ALU.add),
                             reads=[hbuf, B_GS], writes=[NB[t0 // 512]])
            if final:
                k.dma(SP, och, final_dst[:, t0:t0 + w].rearrange("(c p) t -> p c t", p=128), ot[:, :, 0:w], reads=[ob])
        if edge_mask:
            for cidx, mi in ((0, 0), (L.T - 1, 1)):
                k.op(DVE, lambda e, cidx=cidx, mi=mi: e.tensor_scalar(
                    out=Narena[:, :, cidx], in0=Narena[:, :, cidx], scalar1=emask[:, mi:mi + 1], scalar2=None, op0=ALU.mult),
                     reads=[NB[cidx // 512], B_const], writes=[NB[cidx // 512]])
        k.barrier()
        k.sb_reset(mark)

    def proj1(L, Narena, NB, ws, wsrcs, evac, ps_lo=0, ps_hi=6):
        nblk = len(wsrcs)
        pre = max(1, len(ws.t) - 1)
        loaded = {}
        for bi in range(min(pre, nblk)):
            loaded[bi] = ws.load(wsrcs[bi])
        for bi in range(nblk):
            if bi + pre < nblk:
                loaded[bi + pre] = ws.load(wsrcs[bi + pre])
            wt, wb = loaded.pop(bi)
            for t in range(L.ntiles):
                t0, w, col = L.tiles[t]
                pb = next_ps(ps_lo, ps_hi)
                mm_group(psum[:, pb, 0:w], [(wt[:, kk, :], Narena[:, kk, t0:t0 + w]) for kk in range(KC)],
                         reads=[wb, NB[t]], writes=[PSB[pb]])
                evac(bi, t, pb)

    def hupdate(L, Hc, stc, sti, c, t, pb, li, gate_which):
        t0, w, col = L.tiles[t]
        ht, hbuf, hch = Hc.next()
        k.dma(SP, hch, ht[:, 0:w], L.Hd[c * 128:(c + 1) * 128, t0:t0 + w], reads=[L.hb(c, t)], writes=[hbuf])
        k.op(DVE, lambda e: e.scalar_tensor_tensor(
            out=ht[:, 0:w], in0=psum[:, pb, 0:w], scalar=Mall[:, li, gate_which * 16 + c, col:col + 1],
            in1=ht[:, 0:w], op0=ALU.mult, op1=ALU.add), reads=[PSB[pb], hbuf, B_M], writes=[hbuf])
        k.dma(SP, stc[sti % 4], L.Hd[c * 128:(c + 1) * 128, t0:t0 + w], ht[:, 0:w], reads=[hbuf], writes=[L.hb(c, t)])

    def proj2(L, src, KB, wsrc_c, li, gate_which, resident=False):
        mark = k.sb_mark()
        Ag = [k.sb([128, KB, 512], BF16, "Ag") for _ in range(2)]
        AgB = [Buf("Ag0"), Buf("Ag1")]
        AgC = [get_chan("c_Ag0"), get_chan("c_Ag1")]
        Hc = Slots("Hc", [128, 512], F32, 4)
        stc = [get_chan(f"c_hst{i}") for i in range(4)]
        if resident:
            wres = k.sb([128, KC, KB, 128], BF16, "wres")
            wresB = Buf("wres")
            chw = get_chan("c_wres")
            for c in range(KC):
                k.dma(POOL, chw, wres[:, c], wsrc_c(c), writes=[wresB], max_dma_last_dim=4096)
        else:
            ws = WStream("w2_", [128, KB, 128], 6)
        sti = 0
        for g in L.groups:
            for ti, t in enumerate(g):
                t0, w, col = L.tiles[t]
                k.dma(SP, AgC[ti], Ag[ti][:, :, 0:w], src[:, t0:t0 + w].rearrange("(j p) t -> p j t", p=128),
                      writes=[AgB[ti]])
            for c in range(KC):
                if resident:
                    wt, wb = wres[:, c], wresB
                else:
                    wt, wb = ws.load(wsrc_c(c))
                for ti, t in enumerate(g):
                    t0, w, col = L.tiles[t]
                    pb = next_ps()
                    mm_group(psum[:, pb, 0:w], [(wt[:, j, :], Ag[ti][:, j, 0:w]) for j in range(KB)],
                             reads=[wb, AgB[ti]], writes=[PSB[pb]])
                    hupdate(L, Hc, stc, sti, c, t, pb, li, gate_which)
                    sti += 1
        k.barrier()
        k.sb_reset(mark)

    def stage_ffn(L, Narena, NB, li):
        mark = k.sb_mark()
        ws = WStream("wup", [128, KC, 128], 3)
        ZW = L.zw
        zg = [k.sb([128, ZW], BF16, "zg") for _ in range(2)]
        zv = [k.sb([128, ZW], BF16, "zv") for _ in range(2)]
        zgB = [Buf("zg0"), Buf("zg1")]
        zvB = [Buf("zv0"), Buf("zv1")]
        cw = k.sb([128, 3, 2 * NJ], F32, "cw")
        cb = k.sb([128, 2 * NJ], F32, "cb")
        Bcw = Buf("cw")
        k.dma(SP, ch_misc, cw[:], ffn_cw_fm[:, li, :, :], writes=[Bcw])
        k.dma(SP, ch_misc, cb[:], ffn_cb_fm[:, li, :], writes=[Bcw])
        for s in range(2):
            k.op(POOL, lambda e, s=s: e.memset(zg[s][:], 0.0), writes=[zgB[s]])
            k.op(POOL, lambda e, s=s: e.memset(zv[s][:], 0.0), writes=[zvB[s]])
        yg = Slots("yg", [128, 512], F32, 2, chan=False)
        yv = Slots("yv", [128, 512], F32, 2, chan=False)
        sgs = Slots("sgs", [128, 512], F32, 2, chan=False)
        ao = Slots("ao", [128, 512], BF16, 3)

        def conv_tile(j, t):
            s = j % 2
            t0, w, col = L.tiles[t]
            zo = L.zoff[t]
            res = []
            for half, (zt, zb, ys) in enumerate(((zg[s], zgB[s], yg), (zv[s], zvB[s], yv))):
                blk = j + half * NJ
                yt, yb, _ = ys.next()
                k.op(ACT, lambda e, yt=yt, zt=zt, blk=blk: e.activation(
                    out=yt[:, 0:w], in_=zt[:, zo:zo + w], func=AF.Identity,
                    scale=cw[:, 1, blk:blk + 1], bias=cb[:, blk:blk + 1]), reads=[zb, Bcw], writes=[yb])
                k.op(DVE, lambda e, yt=yt, zt=zt, blk=blk: e.scalar_tensor_tensor(
                    out=yt[:, 0:w], in0=zt[:, zo - 1:zo - 1 + w], scalar=cw[:, 0, blk:blk + 1], in1=yt[:, 0:w],
                    op0=ALU.mult, op1=ALU.add), reads=[zb, Bcw, yb], writes=[yb])
                k.op(DVE, lambda e, yt=yt, zt=zt, blk=blk: e.scalar_tensor_tensor(
                    out=yt[:, 0:w], in0=zt[:, zo + 1:zo + 1 + w], scalar=cw[:, 2, blk:blk + 1], in1=yt[:, 0:w],
                    op0=ALU.mult, op1=ALU.add), reads=[zb, Bcw, yb], writes=[yb])
                res.append((yt, yb))
            (ygt, ygb), (yvt, yvb) = res
            st, sb_, _ = sgs.next()
            k.op(ACT, lambda e: e.activation(out=st[:, 0:w], in_=ygt[:, 0:w], func=AF.Silu),
                 reads=[ygb], writes=[sb_])
            at, ab, ach = ao.next()
            k.op(DVE, lambda e: e.tensor_tensor(out=at[:, 0:w], in0=st[:, 0:w], in1=yvt[:, 0:w], op=ALU.mult),
                 reads=[sb_, yvb], writes=[ab])
            k.dma(SP, ach, S1[j * 128:(j + 1) * 128, t0:t0 + w], at[:, 0:w], reads=[ab])

        for j in range(NJ + 1):
            s = j % 2
            if j < NJ:
                for half, (zt, zb) in enumerate(((zg[s], zgB[s]), (zv[s], zvB[s]))):
                    blk = j + half * NJ
                    wt, wb = ws.load(ffn_up_b[li, blk])
                    for t in range(L.ntiles):
                        t0, w, col = L.tiles[t]
                        pb = next_ps()
                        mm_group(psum[:, pb, 0:w], [(wt[:, kk, :], Narena[:, kk, t0:t0 + w]) for kk in range(KC)],
                                 reads=[wb, NB[t]], writes=[PSB[pb]])
                        zo = L.zoff[t]
                        k.op(ACT, lambda e, zt=zt, pb=pb, zo=zo, w=w: e.activation(out=zt[:, zo:zo + w], in_=psum[:, pb, 0:w], func=AF.Copy),
                             reads=[PSB[pb]], writes=[zb])
                        if half == 0 and j > 0:
                            conv_tile(j - 1, t)
            else:
                for t in range(L.ntiles):
                    conv_tile(j - 1, t)
        k.barrier()
        k.sb_reset(mark)

    def stage_mixA(L, Narena, NB, li, ja, arena_mark):
        nchunk = L.T // 128
        mark = k.sb_mark()
        ws = WStream("wain", [128, KC, 128], 4)
        acc = k.sb([128, L.T], F32, "acc")
        accB = [Buf(f"acc{t}") for t in range(L.ntiles)]
        uo = Slots("uo", [128, 512], BF16, 4)
        sqv = Slots("sqv", [128, 512], F32, 6, chan=False)
        for t in range(L.ntiles):
            t0, w, col = L.tiles[t]
            k.op(POOL, lambda e, t0=t0, w=w: e.memset(acc[:, t0:t0 + w], 0.0), writes=[accB[t]])

        def evac(bi, t, pb):
            t0, w, col = L.tiles[t]
            ot, ob, och = uo.next()
            k.op(ACT, lambda e: e.activation(out=ot[:, 0:w], in_=psum[:, pb, 0:w], func=AF.Gelu_apprx_tanh),
                 reads=[PSB[pb]], writes=[ob])
            if bi < 32:
                k.dma(SP, och, S1[bi * 128:(bi + 1) * 128, t0:t0 + w], ot[:, 0:w], reads=[ob])
            else:
                vb = bi - 32
                k.dma(SP, och, S2[vb * 128:(vb + 1) * 128, t0:t0 + w], ot[:, 0:w], reads=[ob])
                st, sb_, _ = sqv.next()
                k.op(ACT, lambda e: e.activation(out=st[:, 0:w], in_=ot[:, 0:w], func=AF.Square), reads=[ob], writes=[sb_])
                k.op(POOL, lambda e: e.tensor_tensor(out=acc[:, t0:t0 + w], in0=acc[:, t0:t0 + w], in1=st[:, 0:w], op=ALU.add),
                     reads=[sb_, accB[t]], writes=[accB[t]])

        proj1(L, Narena, NB, ws, [a_in_b[ja, bi] for bi in range(64)], evac)
        rs = k.sb([128, 40], F32, "rs")
        rsB = Buf("rs")
        pb = next_ps()
        for n in range(nchunk):
            k.op(PE, lambda e, n=n, pb=pb: e.matmul(psum[:, pb, n:n + 1], lhsT=acc[:, n * 128:(n + 1) * 128], rhs=one_t[:],
                                                   start=True, stop=True),
                 reads=[accB[n // 4], B_const], writes=[PSB[pb]])
        k.op(ACT, lambda e: e.activation(out=rs[:, 0:nchunk], in_=psum[:, pb, 0:nchunk], func=AF.Sqrt, scale=1.0 / 4096, bias=eps_t[:]),
             reads=[PSB[pb], B_const], writes=[rsB])
        k.op(DVE, lambda e: e.reciprocal(out=rs[:, 0:nchunk], in_=rs[:, 0:nchunk]), reads=[rsB], writes=[rsB])
        k.barrier()
        k.sb_reset(arena_mark)
        rs2 = k.sb([128, 40], F32, "rs2")
        assert k.sb_mark() <= mark
        k.op(DVE, lambda e: e.tensor_copy(out=rs2[:, 0:nchunk], in_=rs[:, 0:nchunk]), reads=[rsB], writes=[rsB])
        k.barrier()
        wsT = k.sb([128, 16, 128], BF16, "wsT")
        bsb = k.sb([128, 32, 128], F32, "bsb")
        gvrow = k.sb([128, 4096], BF16, "gvrow")
        Bw = Buf("wsT")
        chA = get_chan("c_wsT")
        k.dma(POOL, chA, wsT[:], a_wsT[ja], writes=[Bw])
        k.dma(SP, ch_misc, bsb[:], a_bs_bc[ja], writes=[Bw])
        k.dma(POOL, chA, gvrow[:], a_gv_bc[ja], writes=[Bw], max_dma_last_dim=4096)
        Ut = Slots("Ut", [128, 32, 256], BF16, 2)
        Vt = Slots("Vt", [128, 32, 256], BF16, 2)
        US = [k.sb([128, 32, 512], BF16, "US") for _ in range(2)]
        USB = [Buf("US0"), Buf("US1")]
        vtm = Slots("vtm", [128, 512], BF16, 3, chan=False)
        stmp = Slots("stmp", [128, 512], F32, 2, chan=False)
        ws2 = WStream("waout", [128, 32, 128], 3)
        Hc = Slots("HcA", [128, 512], F32, 4)
        stc = [get_chan(f"c_hst{i}") for i in range(4)]
        sti = 0
        def sp_front(vt, vb, cnl, n, f4):
            pt = next_ps()
            ptile = psum[:, pt, :].bitcast(BF16)
            fns = []
            for ff in range(4):
                fb = f4 * 4 + ff
                fns.append(lambda e, ff=ff, fb=fb: e.transpose(
                    ptile[:, ff * 128:(ff + 1) * 128], vt[:, fb, cnl * 128:(cnl + 1) * 128], ident_bf[:]))
            k.op(PE, fns, reads=[vb, B_const], writes=[PSB[pt]])
            lt, lb, _ = vtm.next()
            k.op(DVE, lambda e: e.scalar_tensor_tensor(
                out=lt[:], in0=ptile[:, 0:512], scalar=rs2[:, n:n + 1], in1=gvrow[:, f4 * 512:(f4 + 1) * 512],
                op0=ALU.mult, op1=ALU.mult), reads=[PSB[pt], rsB, Bw], writes=[lb])
            return lt, lb

        def sp_back(lt, lb, ut, ub, cnl, ti, cn, f4):
            ps2 = next_ps()
            fns = []
            for ff in range(4):
                fb = f4 * 4 + ff
                fns.append(lambda e, ff=ff, fb=fb: e.matmul(
                    psum[:, ps2, ff * 128:(ff + 1) * 128], lhsT=lt[:, ff * 128:(ff + 1) * 128], rhs=wsT[:, fb // 2, :],
                    start=True, stop=True))
            k.op(PE, fns, reads=[lb, Bw], writes=[PSB[ps2]])
            tt, tb, _ = stmp.next()
            k.op(DVE, lambda e: e.tensor_tensor(
                out=tt[:], in0=psum[:, ps2, 0:512],
                in1=bsb[:, f4 * 4:(f4 + 1) * 4, :].rearrange("p a b -> p (a b)"), op=ALU.add),
                 reads=[PSB[ps2], Bw], writes=[tb])
            k.op(POOL, lambda e: e.tensor_tensor(
                out=US[ti][:, f4 * 4:(f4 + 1) * 4, cn * 128:(cn + 1) * 128],
                in0=tt[:].rearrange("p (a b) -> p a b", b=128),
                in1=ut[:, f4 * 4:(f4 + 1) * 4, cnl * 128:(cnl + 1) * 128], op=ALU.mult),
                 reads=[tb, ub], writes=[USB[ti]])

        for g in L.groups:
            pend = None
            for ti, t in enumerate(g):
                t0, w, col = L.tiles[t]
                for h0 in range(0, w, 256):
                    hw = min(256, w - h0)
                    ut, ub, uch = Ut.next()
                    vt, vb, vch = Vt.next()
                    k.dma(SP, vch, vt[:, :, 0:hw], S2[0:4096, t0 + h0:t0 + h0 + hw].rearrange("(j p) t -> p j t", p=128), writes=[vb])
                    k.dma(SP, uch, ut[:, :, 0:hw], S1[0:4096, t0 + h0:t0 + h0 + hw].rearrange("(j p) t -> p j t", p=128), writes=[ub])
                    for cnl in range(hw // 128):
                        cn = h0 // 128 + cnl
                        n = t0 // 128 + cn
                        for f4 in range(8):
                            lt, lb = sp_front(vt, vb, cnl, n, f4)
                            if pend is not None:
                                sp_back(*pend)
                            pend = (lt, lb, ut, ub, cnl, ti, cn, f4)
            if pend is not None:
                sp_back(*pend)
            for c in range(KC):
                wt, wb = ws2.load(a_out_b[ja, c])
                for ti, t in enumerate(g):
                    t0, w, col = L.tiles[t]
                    pb = next_ps()
                    mm_group(psum[:, pb, 0:w], [(wt[:, j, :], US[ti][:, j, 0:w]) for j in range(32)],
                             reads=[wb, USB[ti]], writes=[PSB[pb]])
                    hupdate(L, Hc, stc, sti, c, t, pb, li, 2)
                    sti += 1
        k.barrier()
        k.sb_reset(mark)

    def stage_mixC_x(Narena, NB, arena_mark):
        mark = k.sb_mark()
        ws = WStream("wcin", [128, KC, 128], 4)
        xo = Slots("xo", [128, 512], F32, 3)

        def evac(bi, t, pb):
            t0, w, col = Lm.tiles[t]
            ot, ob, och = xo.next()
            k.op(DVE, lambda e: e.tensor_copy(out=ot[:, 0:w], in_=psum[:, pb, 0:w]), reads=[PSB[pb]], writes=[ob])
            k.dma(SP, och, X4[bi * 128:(bi + 1) * 128, t0:t0 + w], ot[:, 0:w], reads=[ob])

        proj1(Lm, Narena, NB, ws, [c_in_b[16 + bi] for bi in range(16)], evac)
        k.barrier()
        k.sb_reset(arena_mark)
        ZW = 4360
        LOFF, COFF = 2, 4101
        xz = Slots("xz", [128, ZW], F32, 1)
        xz2c = [get_chan("c_xzb0"), get_chan("c_xzb1")]
        xcs = [k.sb([128, T], F32, "xc") for _ in range(2)]; xcBs = [Buf("xc0"), Buf("xc1")]
        xcbs = [k.sb([128, T], BF16, "xcb") for _ in range(2)]; xcbBs = [Buf("xcb0"), Buf("xcb1")]
        Rbs = [k.sb([128, T], F32, "R") for _ in range(2)]; RBts = [[Buf(f"R{d}_{t}") for t in range(9)] for d in range(2)]
        Ibs = [k.sb([128, T], F32, "I") for _ in range(2)]; IBts = [[Buf(f"I{d}_{t}") for t in range(9)] for d in range(2)]
        Tb = k.sb([128, T], F32, "Tm"); TBt = [Buf(f"Tm{t}") for t in range(9)]
        HF = k.sb([128, T], F32, "HF"); HFB = Buf("HF")
        HR = k.sb([128, T], F32, "HR"); HRB = Buf("HR")
        hst = [get_chan("c_hs0"), get_chan("c_hs1")]
        X4B = [Buf(f"X4_{h}") for h in range(KC)]
        wg = WStream("wg", [128, 2, 2, 128], 2)
        ccw = k.sb([128, 4, KC], F32, "ccw")
        ccb = k.sb([128, KC], F32, "ccb")
        bgt = k.sb([128, 2, 2, KC], F32, "bgt")
        lam = k.sb([128, 2, KC], F32, "lam")
        cneg = k.sb([128, 2, KC], F32, "cneg")
        Bp = Buf("cparams")
        k.dma(SP, ch_misc, ccw[:], c_cw_fm[:, :, :], writes=[Bp])
        k.dma(SP, ch_misc, ccb[:], c_cb_fm[:, :], writes=[Bp])
        k.dma(SP, ch_misc, bgt[:], c_bg_fm[:, :, :, :], writes=[Bp])
        k.dma(SP, ch_misc, lam[:], c_lam_fm[:, :, :], writes=[Bp])
        k.op(ACT, lambda e: e.activation(out=cneg[:], in_=lam[:], func=AF.Exp, scale=-1.0), reads=[Bp], writes=[Bp])
        k.op(ACT, lambda e: e.activation(out=cneg[:], in_=cneg[:], func=AF.Ln, bias=one_t[:], scale=1.0), reads=[Bp, B_const], writes=[Bp])
        k.op(DVE, lambda e: e.tensor_scalar(out=cneg[:], in0=cneg[:], scalar1=-8.0, scalar2=None, op0=ALU.mult), reads=[Bp], writes=[Bp])
        for s in range(1):
            t_, b_, _ = xz.next()
            k.op(POOL, lambda e, t_=t_: e.memset(t_[:], 0.0), writes=[b_])
        nctx = True
        segs = [(0, SEQ, LOFF)] + ([(SEQ, CTX, COFF)] if nctx else [])
        ntok = SEQ + (CTX if nctx else 0)
        def stage_A(h):
            xc, xcB = xcs[h % 2], xcBs[h % 2]
            xcb, xcbB = xcbs[h % 2], xcbBs[h % 2]
            xt, xb, xch = xz.next()
            k.dma(SP, xch, xt[:, LOFF:LOFF + SEQ], X4[h * 128:(h + 1) * 128, 0:SEQ], reads=[X4B[h]], writes=[xb])
            k.dma(SP, xz2c[h % 2], xt[:, COFF:COFF + CTX], X4[h * 128:(h + 1) * 128, SEQ:T], reads=[X4B[h]], writes=[xb])
            wgt, wgb = wg.load(c_wg[:, :, :, h, :])
            for (c0, n, off) in segs:
                k.op(DVE, lambda e, c0=c0, n=n, off=off: e.tensor_scalar(
                    out=xc[:, c0:c0 + n], in0=xt[:, off:off + n], scalar1=ccw[:, 2, h:h + 1], scalar2=ccb[:, h:h + 1],
                    op0=ALU.mult, op1=ALU.add), reads=[xb, Bp], writes=[xcB])
                for j in (0, 1, 3):
                    k.op(DVE, lambda e, c0=c0, n=n, off=off, j=j: e.scalar_tensor_tensor(
                        out=xc[:, c0:c0 + n], in0=xt[:, off + j - 2:off + j - 2 + n], scalar=ccw[:, j, h:h + 1],
                        in1=xc[:, c0:c0 + n], op0=ALU.mult, op1=ALU.add), reads=[xb, Bp, xcB], writes=[xcB])
            k.op(POOL, lambda e: e.tensor_copy(out=xcb[:, 0:ntok], in_=xc[:, 0:ntok]), reads=[xcB], writes=[xcbB])
            return dict(h=h, xc=xc, xcB=xcB, xcb=xcb, xcbB=xcbB, wgt=wgt, wgb=wgb)

        def stage_B(C, d):
            h, xc, xcB, xcb, xcbB, wgt, wgb = C["h"], C["xc"], C["xcB"], C["xcb"], C["xcbB"], C["wgt"], C["wgb"]
            Rb, Ib, RBt, IBt = Rbs[d], Ibs[d], RBts[d], IBts[d]
            for t in range(Lm.ntiles):
                t0, w, col_ = Lm.tiles[t]
                for gidx, (gt, gBs) in enumerate(((Rb, RBt), (Ib, IBt))):
                    pb = next_ps()
                    k.op(PE, lambda e, pb=pb, w=w, t0=t0, gidx=gidx: e.matmul(
                        psum[:, pb, 0:w], lhsT=wgt[:, d, gidx, :], rhs=xcb[:, t0:t0 + w], start=True, stop=True),
                         reads=[wgb, xcbB], writes=[PSB[pb]])
                    k.op(ACT, lambda e, pb=pb, w=w, t0=t0, gidx=gidx, gt=gt: e.activation(
                        out=gt[:, t0:t0 + w], in_=psum[:, pb, 0:w], func=AF.Sigmoid,
                        bias=bgt[:, d, gidx, h:h + 1], scale=1.0), reads=[PSB[pb], Bp], writes=[gBs[t]])
            for t in range(Lm.ntiles):
                t0, w, col_ = Lm.tiles[t]
                k.op(ACT, lambda e, t0=t0, w=w: e.activation(out=Rb[:, t0:t0 + w], in_=Rb[:, t0:t0 + w], func=AF.Exp,
                                                             scale=cneg[:, d, h:h + 1]), reads=[RBt[t], Bp], writes=[RBt[t]])
                k.op(DVE, lambda e, t0=t0, w=w: e.tensor_tensor(out=Tb[:, t0:t0 + w], in0=Rb[:, t0:t0 + w], in1=Rb[:, t0:t0 + w], op=ALU.mult),
                     reads=[RBt[t]], writes=[TBt[t]])
                k.op(ACT, lambda e, t0=t0, w=w: e.activation(out=Tb[:, t0:t0 + w], in_=Tb[:, t0:t0 + w], func=AF.Sqrt, scale=-0.99999994, bias=one_t[:]),
                     reads=[TBt[t], B_const], writes=[TBt[t]])
                k.op(POOL, lambda e, t0=t0, w=w: e.tensor_tensor(out=Ib[:, t0:t0 + w], in0=Ib[:, t0:t0 + w], in1=xc[:, t0:t0 + w], op=ALU.mult),
                     reads=[IBt[t], xcB], writes=[IBt[t]])
                k.op(POOL, lambda e, t0=t0, w=w: e.tensor_tensor(out=Ib[:, t0:t0 + w], in0=Ib[:, t0:t0 + w], in1=Tb[:, t0:t0 + w], op=ALU.mult),
                     reads=[IBt[t], TBt[t]], writes=[IBt[t]])

        def stage_C(C, d):
            Rb, Ib, RBt, IBt = Rbs[d], Ibs[d], RBts[d], IBts[d]
            Hout, HoutB = (HF, HFB) if d == 0 else (HR, HRB)
            if d == 0:
                k.op(DVE, lambda e: e.tensor_tensor_scan(
                    out=Hout[:, SEQ:T], data0=Rb[:, SEQ:T], data1=Ib[:, SEQ:T], initial=0.0, op0=ALU.mult, op1=ALU.add),
                     reads=[RBt[8], IBt[8]], writes=[HoutB])
                init = Hout[:, T - 1:T]
                k.op(DVE, lambda e: e.tensor_tensor_scan(
                    out=Hout[:, 0:SEQ], data0=Rb[:, 0:SEQ], data1=Ib[:, 0:SEQ], initial=init, op0=ALU.mult, op1=ALU.add),
                     reads=RBt[0:8] + IBt[0:8] + [HoutB], writes=[HoutB])
            else:
                k.op(DVE, lambda e: e.tensor_tensor_scan(
                    out=Hout[:, SEQ:T][:, ::-1], data0=Rb[:, SEQ:T][:, ::-1], data1=Ib[:, SEQ:T][:, ::-1], initial=0.0,
                    op0=ALU.mult, op1=ALU.add), reads=[RBt[8], IBt[8]], writes=[HoutB])
                init = Hout[:, SEQ:SEQ + 1]
                k.op(DVE, lambda e: e.tensor_tensor_scan(
                    out=Hout[:, 0:SEQ][:, ::-1], data0=Rb[:, 0:SEQ][:, ::-1], data1=Ib[:, 0:SEQ][:, ::-1], initial=init,
                    op0=ALU.mult, op1=ALU.add), reads=RBt[0:8] + IBt[0:8] + [HoutB], writes=[HoutB])

        def stage_D(C):
            h = C["h"]
            k.op(DVE, lambda e: e.tensor_tensor(out=HR[:, 0:ntok], in0=HF[:, 0:ntok], in1=HR[:, 0:ntok], op=ALU.add),
                 reads=[HFB, HRB], writes=[HRB])
            k.dma(SP, hst[h % 2], X4[h * 128:(h + 1) * 128, 0:ntok], HR[:, 0:ntok], reads=[HRB], writes=[X4B[h]])

        ctxs = {0: stage_A(0)}
        for h in range(KC):
            stage_B(ctxs[h], 0)
            if h + 1 < KC:
                ctxs[h + 1] = stage_A(h + 1)
            stage_B(ctxs[h], 1)
            stage_C(ctxs[h], 0)
            stage_C(ctxs[h], 1)
            stage_D(ctxs[h])
        k.barrier()
        k.sb_reset(mark)

    def stage_mixC_y(L, Narena, NB, li):
        mark = k.sb_mark()
        ws = WStream("wcin", [128, KC, 128], 4)
        yo = Slots("yo", [128, 512], F32, 3, chan=False)
        hs = Slots("hsw", [128, 512], F32, 3)
        oo = Slots("oo", [128, 512], BF16, 3)

        def evac(bi, t, pb):
            t0, w, col = L.tiles[t]
            yt, yb, _ = yo.next()
            k.op(ACT, lambda e: e.activation(out=yt[:, 0:w], in_=psum[:, pb, 0:w], func=AF.Gelu_apprx_tanh),
                 reads=[PSB[pb]], writes=[yb])
            ht, hbuf, hch = hs.next()
            k.dma(SP, hch, ht[:, 0:w], HSw[bi * 128:(bi + 1) * 128, t0:t0 + w], writes=[hbuf])
            ot, ob, och = oo.next()
            k.op(DVE, lambda e: e.tensor_tensor(out=ot[:, 0:w], in0=yt[:, 0:w], in1=ht[:, 0:w], op=ALU.mult),
                 reads=[yb, hbuf], writes=[ob])
            k.dma(SP, och, S4[bi * 128:(bi + 1) * 128, t0:t0 + w], ot[:, 0:w], reads=[ob])

        proj1(L, Narena, NB, ws, [c_in_b[bi] for bi in range(16)], evac)
        k.barrier()
        k.sb_reset(mark)

    def stage_mixB(Narena, NB, li, arena_mark):
        ntiles = 9
        mark = k.sb_mark()
        ws = WStream("wqkv", [128, KC, 128], 4)
        qo = Slots("qo", [128, 512], BF16, 4)
        qscale = float(128 ** -0.5)

        def evac(bi, t, pb):
            t0, w, col_ = Lm.tiles[t]
            ot, ob, och = qo.next()
            which, hh = bi // 16, bi % 16
            dst = (S1, S2, S3)[which]
            if which == 0:
                k.op(ACT, lambda e: e.activation(out=ot[:, 0:w], in_=psum[:, pb, 0:w], func=AF.Copy, scale=qscale),
                     reads=[PSB[pb]], writes=[ob])
            else:
                k.op(DVE, lambda e: e.tensor_copy(out=ot[:, 0:w], in_=psum[:, pb, 0:w]), reads=[PSB[pb]], writes=[ob])
            k.dma(SP, och, dst[hh * 128:(hh + 1) * 128, t0:t0 + w], ot[:, 0:w], reads=[ob])

        proj1(Lm, Narena, NB, ws, [b_qkv_b[bi] for bi in range(48)], evac)
        k.barrier()
        k.sb_reset(arena_mark)
        Qs = Slots("Qh", [128, T], BF16, 2)
        Ks = Slots("Kh", [128, T], BF16, 2)
        Vs = Slots("Vh", [128, T], BF16, 2)
        Bt = Slots("Bt", [128, 6, 1024], BF16, 2)
        Vtm = Slots("Vtm", [128, 34, 128], BF16, 2, chan=False)
        Oh = Slots("Oh", [128, T], BF16, 2)
        Ps = Slots("P", [128, 1024], BF16, 3, chan=False)
        Dgs = Slots("Dg", [128, 128], BF16, 3, chan=False)
        PTs = Slots("PT", [128, 7, 128], BF16, 3, chan=False)
        sm = Slots("sm", [128, 4], F32, 4, chan=False)
        sc_rr = [0]
        o_rr = [0]
        nunits = 34 if ntiles == 9 else 32
        def do_head(h):
            qt, qb, qch = Qs.next()
            kt, kb, kch = Ks.next()
            vt, vb, vch = Vs.next()
            bt, bb, bch = Bt.next()
            k.dma(SP, qch, qt[:], S1[h * 128:(h + 1) * 128, :], writes=[qb])
            k.dma(SP, kch, kt[:], S2[h * 128:(h + 1) * 128, :], writes=[kb])
            k.dma(SP, vch, vt[:], S3[h * 128:(h + 1) * 128, :], writes=[vb])
            k.dma(POOL, bch, bt[:], b_bias[h], writes=[bb], max_dma_last_dim=4096)
            vtm, vtmb, _ = Vtm.next()
            for c4 in range(0, 34, 4):
                nb_ = min(4, 34 - c4)
                pt = 6 + o_rr[0] % 2
                o_rr[0] += 1
                ptile = psum[:, pt, :].bitcast(BF16)
                fns = []
                for ff in range(nb_):
                    cn = c4 + ff
                    fns.append(lambda e, ff=ff, cn=cn, ptile=ptile, vt=vt: e.transpose(
                        ptile[:, ff * 128:(ff + 1) * 128], vt[:, cn * 128:(cn + 1) * 128], ident_bf[:]))
                k.op(PE, fns, reads=[vb, B_const], writes=[PSB[pt]])
                k.op(ACT, lambda e, c4=c4, nb_=nb_, ptile=ptile, vtm=vtm: e.activation(
                    out=vtm[:, c4:c4 + nb_, :], in_=ptile[:, 0:nb_ * 128].rearrange("p (a b) -> p a b", b=128), func=AF.Copy),
                     reads=[PSB[pt]], writes=[vtmb])
            ot, ob, och = Oh.next()
            pend_pt = None
            pend_pv = None

            def do_scores(u):
                if u < 32:
                    a = u
                    ti = {0: 1, 1: 2, 30: 3, 31: 4}.get(a, 0)
                    lo = (2 * a - 4) if 2 <= a <= 29 else (0 if a < 2 else 56)
                    k0 = lo * 64
                    q0 = a * 128
                    nlat = 576
                else:
                    ti = 5
                    q0 = SEQ + (u - 32) * 128
                    k0 = 0
                    nlat = 0
                pa = (sc_rr[0] % 2) * 2
                pb2 = pa + 1
                sc_rr[0] += 1
                qv = qt[:, q0:q0 + 128]
                if nlat:
                    k.op(PE, [lambda e: e.matmul(psum[:, pa, 0:512], lhsT=ident_bf[:], rhs=bt[:, ti, 0:512], start=True, stop=False),
                              lambda e: e.matmul(psum[:, pa, 0:512], lhsT=qv, rhs=kt[:, k0:k0 + 512], start=False, stop=True)],
                         reads=[qb, kb, bb, B_const], writes=[PSB[pa]])
                    k.op(PE, [lambda e: e.matmul(psum[:, pb2, 0:512], lhsT=ident_bf[:], rhs=bt[:, ti, 512:1024], start=True, stop=False),
                              lambda e: e.matmul(psum[:, pb2, 0:64], lhsT=qv, rhs=kt[:, k0 + 512:k0 + 576], start=False, stop=False),
                              lambda e: e.matmul(psum[:, pb2, 64:320], lhsT=qv, rhs=kt[:, SEQ:T], start=False, stop=True)],
                         reads=[qb, kb, bb, B_const], writes=[PSB[pb2]])
                else:
                    k.op(PE, lambda e: e.matmul(psum[:, pa, 0:512], lhsT=ident_bf[:], rhs=bt[:, ti, 0:512], start=True, stop=True),
                         reads=[bb, B_const], writes=[PSB[pa]])
                    k.op(PE, [lambda e: e.matmul(psum[:, pb2, 0:512], lhsT=ident_bf[:], rhs=bt[:, ti, 512:1024], start=True, stop=False),
                              lambda e: e.matmul(psum[:, pb2, 0:256], lhsT=qv, rhs=kt[:, SEQ:T], start=False, stop=True)],
                         reads=[qb, kb, bb, B_const], writes=[PSB[pb2]])
                lv = psum[:, pa:pa + 2, :].rearrange("p a b -> p (a b)")
                st, sb_, _ = sm.next()
                k.op(DVE, lambda e: e.reduce_max(out=st[:, 0:1], in_=lv, axis=AX.X), reads=[PSB[pa], PSB[pb2]], writes=[sb_])
                k.op(DVE, lambda e: e.tensor_scalar(out=st[:, 1:2], in0=st[:, 0:1], scalar1=-1.0, scalar2=None, op0=ALU.mult),
                     reads=[sb_], writes=[sb_])
                pt_, pb_, _ = Ps.next()
                k.op(ACT, lambda e: e.activation(out=pt_[:, 0:1024], in_=lv, func=AF.Exp, bias=st[:, 1:2], scale=1.0, accum_out=st[:, 2:3]),
                     reads=[PSB[pa], PSB[pb2], sb_], writes=[pb_, sb_])
                k.op(DVE, lambda e: e.reciprocal(out=st[:, 3:4], in_=st[:, 2:3]), reads=[sb_], writes=[sb_])
                dg, dgb, _ = Dgs.next()
                k.op(DVE, lambda e: e.tensor_scalar(out=dg[:], in0=ident_bf[:], scalar1=st[:, 3:4], scalar2=None, op0=ALU.mult),
                     reads=[sb_, B_const], writes=[dgb])
                blocks = []
                if nlat:
                    for i in range(4):
                        blocks.append((i * 128, 128, k0 // 128 + i))
                    blocks.append((512, 64, k0 // 128 + 4))
                    blocks.append((576, 128, 32))
                    blocks.append((704, 128, 33))
                else:
                    blocks.append((512, 128, 32))
                    blocks.append((640, 128, 33))
                return dict(P=pt_, Pb=pb_, dg=dg, dgb=dgb, blocks=blocks, q0=q0)

            def do_pt(U):
                ptt, ptb, _ = PTs.next()
                blocks = U["blocks"]
                for bank, lo_, hi_ in ((4, 0, 4), (5, 4, 7)):
                    bl = blocks[lo_:hi_]
                    if not bl:
                        continue
                    fns = []
                    for bi_, (pc, nk_, ch_) in enumerate(bl):
                        fns.append(lambda e, bi_=bi_, pc=pc, nk_=nk_, bank=bank: e.matmul(
                            psum[0:nk_, bank, bi_ * 128:(bi_ + 1) * 128], lhsT=U["P"][:, pc:pc + nk_], rhs=U["dg"][:],
                            start=True, stop=True))
                    k.op(PE, fns, reads=[U["Pb"], U["dgb"]], writes=[PSB[bank]])
                    nb_ = len(bl)
                    k.op(ACT, lambda e, bank=bank, lo_=lo_, nb_=nb_: e.activation(
                        out=ptt[:, lo_:lo_ + nb_, :], in_=psum[:, bank, 0:nb_ * 128].rearrange("p (a b) -> p a b", b=128), func=AF.Copy),
                         reads=[PSB[bank]], writes=[ptb])
                U["PT"] = ptt
                U["PTb"] = ptb

            def do_pv(U):
                po = 6 + o_rr[0] % 2
                o_rr[0] += 1
                blocks = U["blocks"]
                n = len(blocks)
                fns = []
                for bi_, (pc, nk_, ch_) in enumerate(blocks):
                    fns.append(lambda e, bi_=bi_, nk_=nk_, ch_=ch_, po=po: e.matmul(
                        psum[:, po, 0:128], lhsT=vtm[0:nk_, ch_, :], rhs=U["PT"][0:nk_, bi_, :],
                        start=(bi_ == 0), stop=(bi_ == n - 1)))
                k.op(PE, fns, reads=[vtmb, U["PTb"]], writes=[PSB[po]])
                q0 = U["q0"]
                k.op(DVE, lambda e: e.tensor_copy(out=ot[:, q0:q0 + 128], in_=psum[:, po, 0:128]), reads=[PSB[po]], writes=[ob])

            for u in range(nunits + 2):
                Unew = do_scores(u) if u < nunits else None
                if pend_pt is not None:
                    do_pt(pend_pt)
                if pend_pv is not None:
                    do_pv(pend_pv)
                pend_pv = pend_pt
                pend_pt = Unew
            ntok = T
            k.dma(SP, och, S4[h * 128:(h + 1) * 128, 0:ntok], ot[:, 0:ntok], reads=[ob])
        for h in range(KC):
            do_head(h)
        k.barrier()
        k.sb_reset(arena_mark)
        proj2(Lm, S4, KC, lambda c: b_out_b[c], li, 2, resident=True)
        k.sb_reset(mark)

    def new_arena(L):
        k.sb_reset(arena_mark)
        Na = k.sb([128, KC, L.T], BF16, "N")
        return Na, [Buf(f"N{t}") for t in range(L.ntiles)]

    def ffn_block(L, li, edge_mask=False):
        Na, NB = new_arena(L)
        stage_norm(L, Na, NB, li, 1, edge_mask=edge_mask)
        stage_ffn(L, Na, NB, li)
        k.sb_reset(arena_mark)
        proj2(L, S1, NJ, lambda c, li=li: ffn_dn_b[li, c], li, 5)

    def dyn_vals(h):
        if force_q is not None:
            q = force_q
            s3 = min(max(1024 * q - 128, 0), 2816)
            return s3, max(s3 - 1, 0), 1024 * q - s3
        pid = h.partition_id()
        q = pid % 4
        nz = (q + 3) // 4
        ge2 = q // 2
        is3 = q // 3
        s3 = 896 * nz + 1024 * ge2 + 896 * is3
        s3m1 = 895 * nz + 1024 * ge2 + 896 * is3
        own = 128 * nz + 128 * is3
        return s3, s3m1, own

    def dsl(start, size):
        if isinstance(start, int):
            return slice(start, start + size)
        return bass.ds(start, size)

    stage_ada()
    arena_mark = k.sb_mark()
    done = False
    for li in range(min(n_layers, 3)):
        Na, NB = new_arena(Lm)
        stage_norm(Lm, Na, NB, li, 0)
        if li == 0:
            stage_mixA(Lm, Na, NB, li, 0, arena_mark)
        elif li == 1:
            stage_mixB(Na, NB, li, arena_mark)
        else:
            stage_mixC_x(Na, NB, arena_mark)
            for ci, (dst, srcT) in enumerate(((Hw, Hd_full), (HSw, X4_full))):
                k.dma(SP, ch_cp[ci], dst[:, 0:1282], lambda h, srcT=srcT: srcT[:, dsl(dyn_vals(h)[0], 1282)])
            k.barrier()
            Na, NB = new_arena(LW2)
            stage_norm(LW2, Na, NB, li, 0)
            stage_mixC_y(LW2, Na, NB, li)
            k.sb_reset(arena_mark)
            proj2(LW2, S4, KC, lambda c: c_out_b[c], li, 2, resident=True)
        if stop_after == (li, "mix"):
            done = True
            break
        ffn_block(Lm if li < 2 else LW2, li, edge_mask=(li == 2))
        if stop_after == (li, "ffn"):
            done = True
            break
    if not done and n_layers == DEPTH:
        li = 3
        Na, NB = new_arena(LW3)
        stage_norm(LW3, Na, NB, li, 0)
        stage_mixA(LW3, Na, NB, li, 1, arena_mark)
        if stop_after != (3, "mix"):
            ffn_block(LW3, li)
            if stop_after != (3, "ffn"):
                k.sb_reset(arena_mark)
                stage_norm(LW3, None, None, 0, 0, final_dst=OutW)
                k.dma(SP, ch_cp[2], outT[:, :], lambda h: OutW[:, dsl(dyn_vals(h)[2], 1024)])
    if stop_after is not None:
        li_, st_ = stop_after
        if li_ < 2 or (li_ == 2 and False):
            dbg = nc.dram_tensor("dbgH", [D, T], F32, kind="ExternalOutput").ap()
            k.dma(SP, ch_cp[3], dbg[:, :], Hd[:, :])
        else:
            dbg = nc.dram_tensor("dbgH", [D, 1282], F32, kind="ExternalOutput").ap()
            k.dma(SP, ch_cp[3], dbg[:, :], Hw[:, :])
    k.barrier()

    def replay(eng, h):
        for kind_, waits, payload, sem in eng.prog:
            for s, v in waits:
                h.wait_ge(s.h, v)
            if kind_ == "wait":
                continue
            if kind_ == "op":
                ins = None
                for fn in payload:
                    ins = fn(h)
                ins.then_inc(sem.h, 1)
            elif kind_ == "dma":
                out, in_, kw = payload
                if callable(out):
                    out = out(h)
                if callable(in_):
                    in_ = in_(h)
                try:
                    ins = h.dma_start(out=out, in_=in_, **kw)
                except Exception:
                    print("DMA FAILED", out, in_, kw)
                    raise
                ins.then_inc(sem.h, 16)

    with nc.Block() as block:
        @block.tensor
        def _(e):
            replay(PE, e)

        @block.scalar
        def _(e):
            replay(ACT, e)

        @block.vector
        def _(e):
            replay(DVE, e)

        @block.gpsimd
        def _(e):
            replay(POOL, e)

        @block.sync
        def _(e):
            replay(SP, e)

    for name, (cm, h) in sem_handles.items():
        pass
    return nc


def _blk(w, kc=KC):
    Kd, Nd = w.shape
    nb = Nd // 128
    return np.ascontiguousarray(w.reshape(kc, 128, nb, 128).transpose(2, 1, 0, 3))


def _blk2(w, kb):
    return np.ascontiguousarray(w.reshape(kb, 128, KC, 128).transpose(2, 1, 0, 3))


def _fm(v):
    lead = v.shape[:-1]
    C = v.shape[-1] // 128
    a = v.reshape(*lead, C, 128)
    return np.ascontiguousarray(np.moveaxis(a, -1, 0))


def _na_bias(rpb):
    NEG = np.float32(-1e30)
    H = rpb.shape[0]
    out = np.full((H, 128, 6, 1024), NEG, np.float32)
    col = np.arange(64)
    c0 = np.clip(col - 8, 0, 48)
    col_ok = (col[None, :] >= c0[:, None]) & (col[None, :] < c0[:, None] + 16)
    dc = np.clip(col[None, :] - col[:, None], -15, 15) + 15
    specs = [(8, 12, 9), (0, 0, 8), (1, 0, 8), (30, 56, 8), (31, 56, 8)]
    for ti, (a, lo, nrows) in enumerate(specs):
        tab = np.full((H, 128, 1024), NEG, np.float32)
        tab[:, :, 576:832] = 0.0
        for qr in range(2):
            r = 2 * a + qr
            r0 = min(max(r - 4, 0), 56)
            for i in range(8):
                kr = r0 + i
                u = kr - lo
                assert 0 <= u < nrows
                dr = kr - r
                vals = rpb[:, dr + 7, :][:, dc]
                vals = np.where(col_ok[None], vals, NEG)
                tab[:, qr * 64:(qr + 1) * 64, u * 64:(u + 1) * 64] = vals
        out[:, :, ti, :] = tab
    out[:, :, 5, 512:768] = 0.0
    return out


def _prep_inputs(inp):
    f = lambda a: np.asarray(a, np.float32)
    shared = {}
    aw = f(inp["ada_w"])
    shared["ada_wb"] = np.ascontiguousarray(aw.reshape(DEPTH, KC, 128, 24, 512).transpose(0, 3, 2, 1, 4))
    shared["ada_b_fm"] = _fm(f(inp["ada_b"]))
    shared["norm_g_fm"] = _fm(f(inp["norm_g"]))
    shared["final_g_fm"] = _fm(f(inp["final_g"]))
    shared["ffn_up_b"] = np.stack([_blk(f(inp["ffn_w_up"])[i]) for i in range(DEPTH)])
    shared["ffn_cw_fm"] = _fm(f(inp["ffn_conv_w"]))
    shared["ffn_cb_fm"] = _fm(f(inp["ffn_conv_b"]))
    shared["ffn_dn_b"] = np.stack([_blk2(f(inp["ffn_w_down"])[i], NJ) for i in range(DEPTH)])
    shared["a_in_b"] = np.stack([_blk(f(inp["a_w_in"])[i]) for i in range(2)])
    shared["a_gv_fm"] = _fm(f(inp["a_g_v"]))
    shared["a_wsT"] = np.ascontiguousarray(f(inp["a_w_s"]).transpose(0, 3, 1, 2))
    shared["a_bs_bc"] = np.ascontiguousarray(np.broadcast_to(np.repeat(f(inp["a_b_s"]), 2, axis=1)[:, None], (2, 128, 32, 128)))
    shared["a_gv_bc"] = np.ascontiguousarray(np.broadcast_to(f(inp["a_g_v"])[:, None], (2, 128, 4096)))
    shared["a_out_b"] = np.stack([_blk2(f(inp["a_w_out"])[i], 32) for i in range(2)])
    shared["b_qkv_b"] = _blk(f(inp["b_w_qkv"])[0])
    shared["b_bias"] = _na_bias(f(inp["b_rpb"])[0])
    shared["b_out_b"] = _blk2(f(inp["b_w_out"])[0], KC)
    shared["c_in_b"] = _blk(f(inp["c_w_in"])[0])
    shared["c_cw_fm"] = _fm(f(inp["c_conv_w"])[0])
    shared["c_cb_fm"] = _fm(f(inp["c_conv_b"])[0])
    shared["c_wg"] = np.ascontiguousarray(f(inp["c_w_gate"])[0].transpose(3, 0, 1, 2, 4))
    shared["c_bg_fm"] = np.ascontiguousarray(f(inp["c_b_gate"])[0].transpose(3, 0, 1, 2))
    shared["c_lam_fm"] = _fm(f(inp["c_lam"])[0])
    shared["c_out_b"] = _blk2(f(inp["c_w_out"])[0], KC)
    shared["ident_in"] = np.eye(128, dtype=np.float32)
    maps = []
    x = f(inp["x"]); ctx = f(inp["ctx"]); c = f(inp["c"]); cc = f(inp["c_ctx"])
    per_b = []
    for b in range(2):
        per_b.append((np.ascontiguousarray(np.concatenate([x[b], ctx[b]], axis=0).T),
                      np.ascontiguousarray(np.stack([_fm(c[b]), _fm(cc)], axis=-1))))
    for r in range(8):
        b, q = r // 4, r % 4
        m = dict(shared)
        m["xT"], m["c_fm"] = per_b[b]
        em = np.ones((128, 2), np.float32)
        if q == 0:
            em[:, 0] = 0.0
        if q == 3:
            em[:, 1] = 0.0
        m["emask_in"] = em
        maps.append(m)
    return maps


_CACHE = {}


def kernel(**inputs):
    maps = _prep_inputs(inputs)
    if "nc" not in _CACHE:
        _CACHE["nc"] = build_program()
    nc = _CACHE["nc"]
    res = run_bass_kernel_spmd(nc, maps, core_ids=list(range(8)))
    out = np.empty((2, SEQ, D), np.float32)
    for r in range(8):
        b, q = r // 4, r % 4
        out[b, q * 1024:(q + 1) * 1024, :] = res.results[r]["outT"].T
    return out
```
